# Optimizing a Trainium2 kernel written in Bass

```python
import math
import jax, jax.numpy as jnp
from jax import lax
import numpy as np

D_MODEL = 2048
BATCH = 1
SEQ = 8192
DEPTH = 2
DEC_BATCH = 128
DEC_SEQ = 1
PAST_LEN = 8192
PAGE_SIZE = 128

EPS = 1e-6
NEG = -1e30
F32 = jnp.float32
D_POOL = 512
POOL_WINDOWS = (2, 4, 8, 16)
POOL_GROUPS = 4
POOL_GROUP_DIM = D_POOL // POOL_GROUPS
POOL_BUF = max(POOL_WINDOWS) - 1
HEAD_DIM = 64
N_Q_HEADS = 24
N_KV_HEADS = 4
Q_PER_KV = N_Q_HEADS // N_KV_HEADS
WINDOW = 128
D_ATTN = N_Q_HEADS * HEAD_DIM
D_KV = N_KV_HEADS * HEAD_DIM
D_IN0 = D_POOL + D_ATTN + 2 * D_KV
D_MIX0 = D_POOL + D_ATTN
GDN_HEADS = 12
GDN_DK = 128
GDN_DV = 128
D_GDN_K = GDN_HEADS * GDN_DK
D_GDN_V = GDN_HEADS * GDN_DV
D_GDN_CONV = 2 * D_GDN_K + D_GDN_V
GDN_CONV = 4
GDN_CHUNK = 64
D_SCONV = 512
SCONV_W = 3
D_IN1 = 2 * D_GDN_K + 2 * D_GDN_V + 2 * GDN_HEADS + 3 * D_SCONV
D_MIX1 = D_GDN_V + D_SCONV
D_FF = 5632
FFN_CONV = 3

kernel_name = 'hybrid_pool_swa_gdn_shortconv_step'


def rmsnorm(x, w):
    xf = x.astype(F32)
    y = xf * lax.rsqrt(jnp.mean(xf * xf, axis=-1, keepdims=True) + EPS)
    return (y * w.astype(F32)).astype(x.dtype)


def l2norm(x):
    return x * lax.rsqrt(jnp.sum(x * x, axis=-1, keepdims=True) + EPS)


def gated_rmsnorm(o, w, gate):
    y = o * lax.rsqrt(jnp.mean(o * o, axis=-1, keepdims=True) + EPS) * w.astype(F32)
    return y * jax.nn.silu(gate.astype(F32))


def causal_dwconv(x_ext, w):
    width = w.shape[0]
    t = x_ext.shape[1] - (width - 1)
    out = w[0] * x_ext[:, :t]
    for j in range(1, width):
        out = out + w[j] * x_ext[:, j:j + t]
    return out


def pool_mix(u_ext, pos, pool_w, pool_scale):
    b, l, _ = u_ext.shape
    t = l - POOL_BUF
    uf = u_ext.astype(F32).reshape(b, l, POOL_GROUPS, POOL_GROUP_DIM)
    cs = jnp.concatenate([jnp.zeros_like(uf[:, :1]), jnp.cumsum(uf, axis=1)], axis=1)
    end = cs[:, POOL_BUF + 1:]
    pooled = []
    for g, w in enumerate(POOL_WINDOWS):
        start = cs[:, POOL_BUF + 1 - w:POOL_BUF + 1 - w + t, g]
        cnt = jnp.minimum(pos + 1, w).astype(F32)[None, :, None]
        pooled.append((end[:, :, g] - start) / cnt)
    diff = jnp.stack(pooled, axis=2) - uf[:, POOL_BUF:]
    y = jnp.einsum('btgc,gcd->btgd', diff, pool_w.astype(F32))
    return (y.reshape(b, t, D_POOL) * pool_scale.astype(F32)).astype(u_ext.dtype)


def sink_softmax(s, sinks):
    sk = sinks.astype(F32).reshape(N_KV_HEADS, Q_PER_KV, 1)
    m = jnp.maximum(jnp.max(s, axis=-1), sk)
    p = jnp.exp(s - m[..., None])
    den = jnp.sum(p, axis=-1) + jnp.exp(sk - m)
    return p / den[..., None]


def swa_prompt(q, k, v, sinks):
    b, t = q.shape[:2]
    nb = t // WINDOW
    qb = q.reshape(b, nb, WINDOW, N_KV_HEADS, Q_PER_KV, HEAD_DIM)
    kb = k.reshape(b, nb, WINDOW, N_KV_HEADS, HEAD_DIM)
    vb = v.reshape(b, nb, WINDOW, N_KV_HEADS, HEAD_DIM)

    def with_prev(a):
        prev = jnp.concatenate([jnp.zeros_like(a[:, :1]), a[:, :-1]], axis=1)
        return jnp.concatenate([prev, a], axis=2)

    k2, v2 = with_prev(kb), with_prev(vb)
    i = jnp.arange(WINDOW)[:, None]
    j = jnp.arange(2 * WINDOW)[None, :]
    rel = i + WINDOW - j
    blk = jnp.arange(nb)[:, None, None]
    valid = (rel >= 0) & (rel <= WINDOW) & ((blk > 0) | (j >= WINDOW))
    s = jnp.einsum('bnqhgd,bnkhd->bnhgqk', qb, k2, preferred_element_type=F32) * HEAD_DIM ** -0.5
    s = jnp.where(valid[None, :, None, None], s, NEG)
    p = sink_softmax(s, sinks)
    o = jnp.einsum('bnhgqk,bnkhd->bnqhgd', p.astype(v.dtype), v2)
    return o.reshape(b, t, D_ATTN)


def swa_sample(q, k_ext, v_ext, sinks):
    b, t = q.shape[:2]
    l = k_ext.shape[1]
    wb = l - t
    qg = q.reshape(b, t, N_KV_HEADS, Q_PER_KV, HEAD_DIM)
    rel = jnp.arange(t)[:, None] + wb - jnp.arange(l)[None, :]
    valid = (rel >= 0) & (rel <= WINDOW)
    s = jnp.einsum('bqhgd,bkhd->bhgqk', qg, k_ext, preferred_element_type=F32) * HEAD_DIM ** -0.5
    s = jnp.where(valid, s, NEG)
    p = sink_softmax(s, sinks)
    o = jnp.einsum('bhgqk,bkhd->bqhgd', p.astype(v_ext.dtype), v_ext)
    return o.reshape(b, t, D_ATTN)


def gdn_chunked(q, k, v, g, beta, s0):
    b, t, h, _ = q.shape
    dv = v.shape[-1]
    c = GDN_CHUNK
    n = t // c

    def chunks(a):
        return a.reshape((b, n, c) + a.shape[2:]).swapaxes(2, 3)

    q, k, v, g, beta = chunks(q), chunks(k), chunks(v), chunks(g), chunks(beta)
    gc = jnp.cumsum(g, axis=-1)
    tri = jnp.tril(jnp.ones((c, c), dtype=bool))
    strict = jnp.tril(jnp.ones((c, c), dtype=bool), -1)
    decay = jnp.exp(jnp.where(tri, gc[..., :, None] - gc[..., None, :], NEG))
    kb = k * beta[..., None]
    lmat = jnp.where(strict, jnp.einsum('bnhid,bnhjd->bnhij', kb, k) * decay, 0.0)
    amat = lmat + jnp.eye(c, dtype=lmat.dtype)
    rhs = jnp.concatenate([v * beta[..., None], kb * jnp.exp(gc)[..., None]], axis=-1)
    sol = lax.linalg.triangular_solve(amat, rhs, left_side=True, lower=True, unit_diagonal=True)
    u, w = sol[..., :dv], sol[..., dv:]
    intra = jnp.where(tri, jnp.einsum('bnhid,bnhjd->bnhij', q, k) * decay, 0.0)
    q_dec = q * jnp.exp(gc)[..., None]
    k_dec = k * jnp.exp(gc[..., -1:] - gc)[..., None]
    g_tot = jnp.exp(gc[..., -1])

    def step(s, xs):
        u_n, w_n, intra_n, qd_n, kd_n, gt_n = xs
        v_new = u_n - jnp.einsum('bhcd,bhde->bhce', w_n, s)
        o_n = jnp.einsum('bhcd,bhde->bhce', qd_n, s) + jnp.einsum('bhij,bhje->bhie', intra_n, v_new)
        s = s * gt_n[..., None, None] + jnp.einsum('bhcd,bhce->bhde', kd_n, v_new)
        return s, o_n

    xs = tuple(jnp.moveaxis(a, 1, 0) for a in (u, w, intra, q_dec, k_dec, g_tot))
    s_fin, o = lax.scan(step, s0, xs)
    o = o.transpose(1, 0, 3, 2, 4).reshape(b, t, h, dv)
    return o, s_fin


def gdn_recurrent(q, k, v, g, beta, s0):
    xs = (q.swapaxes(0, 1), k.swapaxes(0, 1), v.swapaxes(0, 1), g.swapaxes(0, 1), beta.swapaxes(0, 1))

    def step(s, xt):
        qt, kt, vt, gt, bt = xt
        s = s * jnp.exp(gt)[..., None, None]
        kv = jnp.einsum('bhd,bhde->bhe', kt, s)
        s = s + jnp.einsum('bhd,bhe->bhde', kt, (vt - kv) * bt[..., None])
        return s, jnp.einsum('bhd,bhde->bhe', qt, s)

    s_fin, o = lax.scan(step, s0, xs)
    return o.swapaxes(0, 1), s_fin


def conv_ffn(x, buf, norm_w, w_up, w_conv, w_down):
    up = rmsnorm(x, norm_w) @ w_up
    ext = jnp.concatenate([buf.astype(up.dtype), up], axis=1)
    c = causal_dwconv(ext, w_conv)
    hid = jax.nn.silu(c[..., :D_FF]) * c[..., D_FF:]
    return x + hid @ w_down, ext[:, -(FFN_CONV - 1):]


def layer_ab(x, pool_buf, k_buf, v_buf, ffn_buf, pos, is_prompt, wb, params):
    (norm_mix, w_in, pool_w, pool_scale, sinks, w_out, norm_ffn, w_up, w_conv, w_down) = params
    b, t, _ = x.shape
    z = rmsnorm(x, norm_mix) @ w_in
    o_q = D_POOL
    o_k = o_q + D_ATTN
    o_v = o_k + D_KV
    u = z[..., :o_q]
    q = z[..., o_q:o_k].reshape(b, t, N_Q_HEADS, HEAD_DIM)
    k = z[..., o_k:o_v].reshape(b, t, N_KV_HEADS, HEAD_DIM)
    v = z[..., o_v:].reshape(b, t, N_KV_HEADS, HEAD_DIM)
    u_ext = jnp.concatenate([pool_buf.astype(u.dtype), u], axis=1)
    y_pool = pool_mix(u_ext, pos, pool_w, pool_scale)
    if is_prompt:
        k_ext, v_ext = k, v
        y_att = swa_prompt(q, k, v, sinks)
    else:
        k_ext = jnp.concatenate([k_buf.astype(k.dtype), k], axis=1)
        v_ext = jnp.concatenate([v_buf.astype(v.dtype), v], axis=1)
        y_att = swa_sample(q, k_ext, v_ext, sinks)
    x = x + jnp.concatenate([y_pool, y_att], axis=-1) @ w_out
    x, new_ffn = conv_ffn(x, ffn_buf, norm_ffn, w_up, w_conv, w_down)
    return x, (u_ext[:, -POOL_BUF:], k_ext[:, -wb:], v_ext[:, -wb:], new_ffn)


def layer_cd(x, conv_buf, s0, sconv_buf, ffn_buf, is_prompt, params):
    (norm_mix, w_in, gdn_conv, a_log, dt_bias, gdn_norm, sconv_w, w_out,
     norm_ffn, w_up, w_conv, w_down) = params
    b, t, _ = x.shape
    z = rmsnorm(x, norm_mix) @ w_in
    o0 = D_GDN_CONV
    o1 = o0 + D_GDN_V
    o2 = o1 + 2 * GDN_HEADS
    qkv = z[..., :o0]
    zg = z[..., o0:o1].reshape(b, t, GDN_HEADS, GDN_DV)
    b_raw = z[..., o1:o1 + GDN_HEADS].astype(F32)
    a_raw = z[..., o1 + GDN_HEADS:o2].astype(F32)
    sb = z[..., o2:o2 + D_SCONV]
    sc = z[..., o2 + D_SCONV:o2 + 2 * D_SCONV]
    sh = z[..., o2 + 2 * D_SCONV:]
    qkv_ext = jnp.concatenate([conv_buf.astype(qkv.dtype), qkv], axis=1)
    qkv_c = jax.nn.silu(causal_dwconv(qkv_ext, gdn_conv).astype(F32))
    q = l2norm(qkv_c[..., :D_GDN_K].reshape(b, t, GDN_HEADS, GDN_DK)) * GDN_DK ** -0.5
    k = l2norm(qkv_c[..., D_GDN_K:2 * D_GDN_K].reshape(b, t, GDN_HEADS, GDN_DK))
    v = qkv_c[..., 2 * D_GDN_K:].reshape(b, t, GDN_HEADS, GDN_DV)
    beta = jax.nn.sigmoid(b_raw)
    g = -jnp.exp(a_log.astype(F32)) * jax.nn.softplus(a_raw + dt_bias.astype(F32))
    if is_prompt:
        o, s_fin = gdn_chunked(q, k, v, g, beta, s0.astype(F32))
    else:
        o, s_fin = gdn_recurrent(q, k, v, g, beta, s0.astype(F32))
    y_gdn = gated_rmsnorm(o, gdn_norm, zg).reshape(b, t, D_GDN_V).astype(x.dtype)
    u_ext = jnp.concatenate([sconv_buf.astype(sc.dtype), sc * sh], axis=1)
    y_sc = sb * causal_dwconv(u_ext, sconv_w)
    x = x + jnp.concatenate([y_gdn, y_sc], axis=-1) @ w_out
    x, new_ffn = conv_ffn(x, ffn_buf, norm_ffn, w_up, w_conv, w_down)
    return x, (qkv_ext[:, -(GDN_CONV - 1):], s_fin.astype(x.dtype), u_ext[:, -(SCONV_W - 1):], new_ffn)


def trunk(x, bufs, pos, is_prompt, wb, layer_params, final_norm):
    new_state = []
    for layer in range(DEPTH):
        if layer % 2 == 0:
            x, st = layer_ab(x, *bufs[layer], pos, is_prompt, wb, layer_params[layer])
        else:
            x, st = layer_cd(x, *bufs[layer], is_prompt, layer_params[layer])
        new_state.append(st)
    return rmsnorm(x, final_norm), new_state


def setup_inputs(seed: int = 0) -> dict:
    key = jax.random.key(seed)
    keys = list(jax.random.split(key, 48))

    def nxt():
        return keys.pop()

    def normal(shape, scale=1.0):
        return scale * jax.random.normal(nxt(), shape, F32)

    def dense(fan_in, shape):
        return normal(shape, fan_in ** -0.5)

    def gain(n):
        return 1.0 + normal((n,), 0.02)

    wb = min(WINDOW, PAST_LEN)
    dt = jnp.exp(jax.random.uniform(nxt(), (GDN_HEADS,), F32, math.log(1e-3), math.log(1e-1)))
    dt_bias = jnp.log(jnp.expm1(dt))
    a_log = jnp.log(jax.random.uniform(nxt(), (GDN_HEADS,), F32, 1.0, 16.0))
    return {
        'x_prompt': normal((BATCH, SEQ, D_MODEL)),
        'x_sample': normal((DEC_BATCH, DEC_SEQ, D_MODEL)),
        'state_l0_pool': normal((DEC_BATCH, POOL_BUF, D_POOL)),
        'cache_l0_k': normal((DEC_BATCH, wb, N_KV_HEADS, HEAD_DIM)),
        'cache_l0_v': normal((DEC_BATCH, wb, N_KV_HEADS, HEAD_DIM)),
        'state_l0_ffn_conv': normal((DEC_BATCH, FFN_CONV - 1, 2 * D_FF)),
        'state_l1_gdn_conv': normal((DEC_BATCH, GDN_CONV - 1, D_GDN_CONV)),
        'state_l1_gdn_S': normal((DEC_BATCH, GDN_HEADS, GDN_DK, GDN_DV), 0.1),
        'state_l1_sconv': normal((DEC_BATCH, SCONV_W - 1, D_SCONV)),
        'state_l1_ffn_conv': normal((DEC_BATCH, FFN_CONV - 1, 2 * D_FF)),
        'l0_norm_mix': gain(D_MODEL),
        'l0_w_in': dense(D_MODEL, (D_MODEL, D_IN0)),
        'l0_pool_w': dense(POOL_GROUP_DIM, (POOL_GROUPS, POOL_GROUP_DIM, POOL_GROUP_DIM)),
        'l0_pool_scale': 1.0 + normal((D_POOL,), 0.1),
        'l0_sinks': normal((N_Q_HEADS,), 0.5),
        'l0_w_out': dense(D_MIX0, (D_MIX0, D_MODEL)),
        'l0_norm_ffn': gain(D_MODEL),
        'l0_ffn_w_up': dense(D_MODEL, (D_MODEL, 2 * D_FF)),
        'l0_ffn_conv': dense(FFN_CONV, (FFN_CONV, 2 * D_FF)),
        'l0_ffn_w_down': dense(D_FF, (D_FF, D_MODEL)),
        'l1_norm_mix': gain(D_MODEL),
        'l1_w_in': dense(D_MODEL, (D_MODEL, D_IN1)),
        'l1_gdn_conv': dense(GDN_CONV, (GDN_CONV, D_GDN_CONV)),
        'l1_gdn_A_log': a_log,
        'l1_gdn_dt_bias': dt_bias,
        'l1_gdn_norm': gain(GDN_DV),
        'l1_sconv_w': dense(SCONV_W, (SCONV_W, D_SCONV)),
        'l1_w_out': dense(D_MIX1, (D_MIX1, D_MODEL)),
        'l1_norm_ffn': gain(D_MODEL),
        'l1_ffn_w_up': dense(D_MODEL, (D_MODEL, 2 * D_FF)),
        'l1_ffn_conv': dense(FFN_CONV, (FFN_CONV, 2 * D_FF)),
        'l1_ffn_w_down': dense(D_FF, (D_FF, D_MODEL)),
        'final_norm': gain(D_MODEL),
    }


def reference(x_prompt, x_sample, state_l0_pool, cache_l0_k, cache_l0_v, state_l0_ffn_conv,
              state_l1_gdn_conv, state_l1_gdn_S, state_l1_sconv, state_l1_ffn_conv,
              l0_norm_mix, l0_w_in, l0_pool_w, l0_pool_scale, l0_sinks, l0_w_out,
              l0_norm_ffn, l0_ffn_w_up, l0_ffn_conv, l0_ffn_w_down,
              l1_norm_mix, l1_w_in, l1_gdn_conv, l1_gdn_A_log, l1_gdn_dt_bias, l1_gdn_norm,
              l1_sconv_w, l1_w_out, l1_norm_ffn, l1_ffn_w_up, l1_ffn_conv, l1_ffn_w_down,
              final_norm):
    layer_params = (
        (l0_norm_mix, l0_w_in, l0_pool_w, l0_pool_scale, l0_sinks, l0_w_out,
         l0_norm_ffn, l0_ffn_w_up, l0_ffn_conv, l0_ffn_w_down),
        (l1_norm_mix, l1_w_in, l1_gdn_conv, l1_gdn_A_log, l1_gdn_dt_bias, l1_gdn_norm,
         l1_sconv_w, l1_w_out, l1_norm_ffn, l1_ffn_w_up, l1_ffn_conv, l1_ffn_w_down),
    )
    wb = cache_l0_k.shape[1]
    bp, tp = x_prompt.shape[:2]
    dt = x_prompt.dtype
    prompt_bufs = (
        (jnp.zeros((bp, POOL_BUF, D_POOL), dt), None, None,
         jnp.zeros((bp, FFN_CONV - 1, 2 * D_FF), dt)),
        (jnp.zeros((bp, GDN_CONV - 1, D_GDN_CONV), dt),
         jnp.zeros((bp, GDN_HEADS, GDN_DK, GDN_DV), F32),
         jnp.zeros((bp, SCONV_W - 1, D_SCONV), dt),
         jnp.zeros((bp, FFN_CONV - 1, 2 * D_FF), dt)),
    )
    sample_bufs = (
        (state_l0_pool, cache_l0_k, cache_l0_v, state_l0_ffn_conv),
        (state_l1_gdn_conv, state_l1_gdn_S, state_l1_sconv, state_l1_ffn_conv),
    )
    y_prompt, new_p = trunk(x_prompt, prompt_bufs, jnp.arange(tp), True, wb, layer_params, final_norm)
    y_sample, new_s = trunk(x_sample, sample_bufs, PAST_LEN + jnp.arange(x_sample.shape[1]), False,
                            wb, layer_params, final_norm)
    (p_pool, p_k, p_v, p_ffn0), (p_gconv, p_S, p_sconv, p_ffn1) = new_p
    (s_pool, s_k, s_v, s_ffn0), (s_gconv, s_S, s_sconv, s_ffn1) = new_s
    return (y_prompt, y_sample, p_pool, s_pool, p_k, s_k, p_v, s_v, p_ffn0, s_ffn0,
            p_gconv, s_gconv, p_S, s_S, p_sconv, s_sconv, p_ffn1, s_ffn1)
```

```python
import contextlib
import numpy as np
import concourse.bass as bass
import concourse.mybir as mybir
from concourse.bass_utils import run_bass_kernel_spmd

F32 = mybir.dt.float32
BF16 = mybir.dt.bfloat16
AF = mybir.ActivationFunctionType
ALU = mybir.AluOpType
AX = mybir.AxisListType

NCORES = 8
D = 2048
SEQ = 8192
TP = SEQ // NCORES
NS = 16
NT = TP + NS
DFF = 5632
EPS = 1e-6
TILES = [(0, 512), (512, 512), (1024, 16)]

D_POOL = 512
D_ATTN = 1536
D_KV = 256
D_IN0 = 2560
D_IN1 = 7704
NDMASEM = 24


class Buf:
    __slots__ = ("name", "w", "r", "excl")

    def __init__(self, name="", excl=False):
        self.name = name
        self.w = None
        self.r = {}
        self.excl = excl


class Prog:
    ENGS = ("pe", "act", "dve", "pool", "sp")

    def __init__(self, nc, es):
        self.nc = nc
        self.q = {e: [] for e in self.ENGS}
        self.cnt = {}
        self.sem = {}
        for e in self.ENGS:
            self.sem[e] = es.enter_context(nc.semaphore("sem_" + e))
            self.cnt[e] = 0
        for i in range(NDMASEM):
            k = "dma%d" % i
            self.sem[k] = es.enter_context(nc.semaphore("sem_" + k))
            self.cnt[k] = 0
        self.waited = {e: {} for e in self.ENGS}
        self.dma_rr = 0
        self.dma_rr2 = 0
        self.ncc = 0
        self.es = es

    def _deps(self, reads, writes, extra):
        deps = list(extra)
        for b in reads:
            if b.w is not None:
                deps.append(b.w)
            if b.excl:
                deps.extend(b.r.items())
        for b in writes:
            if b.w is not None:
                deps.append(b.w)
            deps.extend(b.r.items())
        return deps

    def _waits(self, eng, deps):
        waits = []
        wd = self.waited[eng]
        for (k, c) in deps:
            if k == "pe" and eng == "pe":
                continue
            if wd.get(k, 0) >= c:
                continue
            wd[k] = c
            waits.append((k, c))
        return waits

    def _mark(self, tok, reads, writes):
        k, c = tok
        for b in reads:
            if b.r.get(k, 0) < c:
                b.r[k] = c
        for b in writes:
            b.w = tok
            b.r = {}

    def op(self, eng, fn, reads=(), writes=(), extra=()):
        waits = self._waits(eng, self._deps(reads, writes, extra))
        self.cnt[eng] += 1
        tok = (eng, self.cnt[eng])
        self.q[eng].append((waits, fn, (eng, 1)))
        self._mark(tok, reads, writes)
        return tok

    def group(self, eng, fns, reads=(), writes=(), extra=()):
        waits = self._waits(eng, self._deps(reads, writes, extra))
        self.cnt[eng] += 1
        tok = (eng, self.cnt[eng])
        n = len(fns)
        for i, fn in enumerate(fns):
            self.q[eng].append((waits if i == 0 else [], fn, (eng, 1) if i == n - 1 else None))
        self._mark(tok, reads, writes)
        return tok

    def dma(self, eng, fn, reads=(), writes=(), extra=()):
        if eng == "sp":
            k = "dma%d" % self.dma_rr
            self.dma_rr = (self.dma_rr + 1) % 16
        else:
            k = "dma%d" % (16 + self.dma_rr2)
            self.dma_rr2 = (self.dma_rr2 + 1) % (NDMASEM - 16)
        deps = self._deps(reads, writes, extra)
        if self.cnt[k] > 0:
            deps.append((k, self.cnt[k]))
        waits = self._waits(eng, deps)
        self.cnt[k] += 16
        tok = (k, self.cnt[k])
        self.q[eng].append((waits, fn, (k, 16)))
        self._mark(tok, reads, writes)
        return tok

    def collective(self, fn, reads=(), writes=()):
        k = "cc%d" % self.ncc
        self.ncc += 1
        self.sem[k] = self.es.enter_context(self.nc.semaphore("sem_" + k))
        self.cnt[k] = 0
        waits = self._waits("pool", self._deps(reads, writes, ()))
        self.cnt[k] += 1
        tok = (k, 1)
        self.q["pool"].append((waits, fn, (k, None)))
        self._mark(tok, reads, writes)
        return tok

    def barrier(self):
        toks = [(k, c) for k, c in self.cnt.items() if c > 0]
        for e in self.ENGS:
            waits = self._waits(e, toks)
            if waits:
                self.q[e].append((waits, None, None))

    def replay(self):
        nc = self.nc
        handles = {"pe": "tensor", "act": "scalar", "dve": "vector", "pool": "gpsimd", "sp": "sync"}
        self.barrier()
        with nc.Block() as block:
            for e in self.ENGS:
                q = self.q[e]
                sem = self.sem

                def body(eng, q=q):
                    for (waits, fn, sig) in q:
                        for (k, c) in waits:
                            eng.wait_ge(sem[k], c)
                        if fn is None:
                            continue
                        inst = fn(eng)
                        if sig is not None:
                            if sig[1] is None:
                                inst.then_inc(sem[sig[0]])
                            else:
                                inst.then_inc(sem[sig[0]], sig[1])

                getattr(block, handles[e])(body)


IN_SHAPES = {
    "xp": [TP, D], "xs": [NS, D], "st_pool": [NS, 15, 512], "ck": [NS, 128, 256], "cv": [NS, 128, 256],
    "st_ffn0": [NS, 2, 2 * DFF], "st_gconv": [NS, 3, 4608], "st_S": [NS, 12, 128, 128],
    "st_sconv": [NS, 2, 512], "st_ffn1": [NS, 2, 2 * DFF],
    "l0_norm_mix": [D], "l0_w_in": [D, D_IN0], "l0_pool_w": [4, 128, 128], "l0_pool_scale": [512],
    "l0_sinks": [24], "l0_w_out": [D, D], "l0_norm_ffn": [D], "l0_ffn_w_up": [D, 2 * DFF],
    "l0_ffn_conv": [3, 2 * DFF], "l0_ffn_w_down": [DFF, D],
    "l1_norm_mix": [D], "l1_w_in": [D, D_IN1], "l1_gdn_conv": [4, 4608], "l1_gdn_A_log": [12],
    "l1_gdn_dt_bias": [12], "l1_gdn_norm": [128], "l1_sconv_w": [3, 512], "l1_w_out": [D, D],
    "l1_norm_ffn": [D], "l1_ffn_w_up": [D, 2 * DFF], "l1_ffn_conv": [3, 2 * DFF],
    "l1_ffn_w_down": [DFF, D], "final_norm": [D],
    "c_ident": [128, 128], "c_onehot": [128, 8], "c_masks": [128, 256], "c_flag": [128, 1],
    "c_poolcorr": [128, 4, 16], "c_bdones": [128, 128], "c_gmasks": [128, 256], "c_selfhot": [128, 8],
}
OUT_SHAPES = {
    "y_p": [TP, D], "y_s": [NS, D], "pool_p": [15, 512], "pool_s": [NS, 15, 512],
    "k_p": [128, 256], "k_s": [NS, 128, 256], "v_p": [128, 256], "v_s": [NS, 128, 256],
    "ffn0_p": [2, 2 * DFF], "ffn0_s": [NS, 2, 2 * DFF], "gconv_p": [3, 4608], "gconv_s": [NS, 3, 4608],
    "S_p": [12, 128, 128], "S_s": [NS, 12, 128, 128], "sconv_p": [2, 512], "sconv_s": [NS, 2, 512],
    "ffn1_p": [2, 2 * DFF], "ffn1_s": [NS, 2, 2 * DFF],
}
OUT_ORDER = ["y_p", "y_s", "pool_p", "pool_s", "k_p", "k_s", "v_p", "v_s", "ffn0_p", "ffn0_s",
             "gconv_p", "gconv_s", "S_p", "S_s", "sconv_p", "sconv_s", "ffn1_p", "ffn1_s"]


class Lazy(dict):
    def __init__(self, mk):
        super().__init__()
        self.mk = mk

    def __missing__(self, key):
        v = self.mk(key)
        self[key] = v
        return v


def build(stage=99):
    try:
        return _build(stage)
    except _StopBuild as ex:
        return ex.nc


class _StopBuild(Exception):
    pass


def _build(stage=99):
    nc = bass.Bass("TRN2", target_bir_lowering=False)
    es = contextlib.ExitStack()
    P = Prog(nc, es)

    I = Lazy(lambda n: nc.dram_tensor(n, list(IN_SHAPES[n]), F32, kind="ExternalInput").ap())
    O = Lazy(lambda n: nc.dram_tensor(n, list(OUT_SHAPES[n]), F32, kind="ExternalOutput").ap())
    OB = Lazy(lambda n: Buf("o_" + n))
    nc._I, nc._O = I, O

    def sb(name, shape, dt=F32, st=None):
        return (st or es).enter_context(nc.sbuf_tensor(name, list(shape), dt))

    def ps(name, shape, dt=F32):
        return es.enter_context(nc.psum_tensor(name, list(shape), dt))

    def scratch(name, shape, dt=F32):
        return nc.dram_tensor(name, list(shape), dt).ap()

    import os
    rr = {"ev": 0, "acc": 0, "pst": 0, "wp": 0}
    KSTOP = os.environ.get("KSTOP", "")

    class Stop(Exception):
        pass

    def maybe_stop(tag, src_ap, bufs, shape, dt):
        if KSTOP != tag:
            return
        dbg = nc.dram_tensor("dbg_" + tag, list(shape), dt, kind="ExternalOutput").ap()
        P.dma("sp", lambda e: e.dma_start(out=dbg, in_=src_ap), reads=bufs)
        P.replay()
        es.close()
        ex = _StopBuild()
        ex.nc = nc
        raise ex
    AWORDS = 19 * 1024
    arena_t = sb("arena", [128, AWORDS], F32)
    ar = {"off": 0}

    def aalloc(shape, dt=F32):
        n = 1
        for d_ in shape[1:]:
            n *= d_
        words = n if dt == F32 else (n + 1) // 2
        off = ar["off"]
        assert off + words <= AWORDS, ("arena overflow", off, words)
        ar["off"] = off + words
        a = arena_t[0:shape[0], off:off + words]
        if dt != F32:
            a = a.bitcast(dt)[:, 0:n]
        if len(shape) == 3:
            a = a.rearrange("p (a b) -> p a b", a=shape[1])
        elif len(shape) == 4:
            a = a.rearrange("p (a b c) -> p a b c", a=shape[1], b=shape[2])
        return a

    def evac_eng():
        rr["ev"] += 1
        return "act" if rr["ev"] % 2 else "dve"

    def copy_fn(eng, out, in_, scale=None):
        if eng == "act":
            if scale is None:
                return lambda e: e.copy(out, in_)
            return lambda e: e.mul(out, in_, scale)
        if scale is None:
            return lambda e: e.tensor_copy(out, in_)
        return lambda e: e.tensor_scalar_mul(out, in_, scale)

    xT = sb("xT", [128, 16, NT], F32)
    xT_b = [Buf("xT%d" % c) for c in range(16)]
    xn = sb("xn", [128, 16, NT], BF16)
    xn_b = [Buf("xn%d" % c) for c in range(16)]
    ident = sb("ident", [128, 128], F32)
    ident_b = Buf("ident")
    ones_bf = sb("ones_bf", [128, 128], BF16)
    bdones = sb("bdones", [128, 128], BF16)
    onehot = sb("onehot", [128, 8], F32)
    flag = sb("flag", [128, 1], F32)
    const_b = Buf("consts")
    cst = sb("cst", [128, 256], F32)
    cst_b = Buf("cst")
    P.dma("sp", lambda e: e.dma_start(out=ident[:], in_=I["c_ident"][:, :]), writes=[ident_b])
    P.dma("sp", lambda e: e.dma_start(out=onehot[:], in_=I["c_onehot"][:, :]), writes=[const_b])
    P.dma("sp", lambda e: e.dma_start(out=flag[:], in_=I["c_flag"][:, :]), writes=[const_b])
    P.dma("sp", lambda e: e.dma_start(out=cst[:, 0:128], in_=I["c_bdones"][:, :]), writes=[cst_b])
    P.op("dve", lambda e: e.tensor_copy(bdones[:], cst[:, 0:128]), reads=[cst_b], writes=[const_b])
    P.op("dve", lambda e: e.memset(ones_bf[:], 1.0), writes=[const_b])
    identb = sb("identb", [128, 128], BF16)
    P.op("dve", lambda e: e.tensor_copy(identb[:], ident[:]), reads=[ident_b], writes=[const_b])
    epsc = sb("epsc", [128, 1], F32)
    for _ in range(int(os.environ.get("KSALT", "0"))):
        P.op("dve", lambda e: e.memset(ones_bf[:], 1.0), writes=[const_b])
    P.op("dve", lambda e: e.memset(epsc[:], EPS), writes=[const_b])

    NACC = 5
    pacc = [ps("pacc%d" % i, [128, 512], F32) for i in range(NACC)]
    pacc_b = [Buf("pacc%d" % i, True) for i in range(NACC)]
    pst = [ps("pst%d" % i, [128, 512], F32) for i in range(3)]
    pst_b = [Buf("pst%d" % i, True) for i in range(3)]

    def next_acc():
        i = rr["acc"] % NACC
        rr["acc"] += 1
        return pacc[i], pacc_b[i]

    def next_pst():
        i = rr["pst"] % 3
        rr["pst"] += 1
        return pst[i], pst_b[i]

    strow = sb("strow", [128, 128], F32)
    strow_b = Buf("strow")

    def load_cols(src, R, name):
        dst = sb(name, [128, R], F32)
        dstb = Buf(name)
        for r0 in range(0, R, 128):
            rn = min(128, R - r0)
            P.dma("sp", lambda e, r0=r0, rn=rn: e.dma_start(out=strow[0:rn, :], in_=src[r0:r0 + rn, :]),
                  writes=[strow_b])
            pt, ptb = next_pst()
            P.group("pe", [lambda e, pt=pt, rn=rn: e.transpose(pt[:, 0:rn], strow[0:rn, :], ident[0:rn, 0:rn])],
                    reads=[strow_b, ident_b], writes=[ptb])
            P.op("dve", lambda e, pt=pt, r0=r0, rn=rn: e.tensor_copy(dst[:, r0:r0 + rn], pt[:, 0:rn]),
                 reads=[ptb], writes=[dstb])
        return dst, dstb

    xin = aalloc([128, 2, D], F32)
    xin_b = [Buf("xin0"), Buf("xin1")]
    nblk = TP // 128
    for blk in range(nblk + 1):
        s = blk % 2
        if blk < nblk:
            rows = 128
            src = I["xp"][blk * 128:(blk + 1) * 128, :]
        else:
            rows = NS
            src = I["xs"][:, :]
        P.dma("sp", lambda e, s=s, rows=rows, src=src: e.dma_start(out=xin[0:rows, s, :], in_=src),
              writes=[xin_b[s]])
        for c4 in range(4):
            pt, ptb = next_pst()
            fns = []
            for j in range(4):
                c = c4 * 4 + j
                fns.append(lambda e, pt=pt, j=j, c=c, s=s, rows=rows: e.transpose(
                    pt[:, j * 128:j * 128 + rows], xin[0:rows, s, c * 128:(c + 1) * 128],
                    ident[0:rows, 0:rows]))
            P.group("pe", fns, reads=[xin_b[s], ident_b], writes=[ptb])
            t0 = blk * 128
            eng = evac_eng()
            P.op(eng, copy_fn(eng, xT[:, c4 * 4:c4 * 4 + 4, t0:t0 + rows],
                              pt[:].rearrange("p (j t) -> p j t", j=4)[:, :, 0:rows]),
                 reads=[ptb], writes=[xT_b[c4 * 4 + j] for j in range(4)])
    if stage == 0:
        dbg = nc.dram_tensor("dbg", [128, 16, NT], F32, kind="ExternalOutput").ap()
        P.dma("sp", lambda e: e.dma_start(out=dbg[:, :, :], in_=xT[:, :, :]), reads=xT_b)
        P.replay()
        es.close()
        return nc
    P.barrier()
    ar["off"] = 0

    sq = sb("sq", [128, 2, NT], BF16)
    sq_b = [Buf("sq0"), Buf("sq1")]
    rstd = sb("rstd", [128, NT], F32)
    rstd_b = Buf("rstd")

    def rmsnorm(wname):
        wcol, wcol_b = load_cols(I[wname].rearrange("(c p) -> c p", p=128), 16, "wc_" + wname)
        accs = [next_acc() for _ in TILES]
        for c in range(16):
            s = c % 2
            P.op("act", lambda e, s=s, c=c: e.activation(out=sq[:, s, :], in_=xT[:, c, :], func=AF.Square),
                 reads=[xT_b[c]], writes=[sq_b[s]])
            for ti, (t0, tn) in enumerate(TILES):
                a, ab = accs[ti]
                P.group("pe", [lambda e, a=a, s=s, t0=t0, tn=tn, c=c: e.matmul(
                    a[:, 0:tn], ones_bf[:], sq[:, s, t0:t0 + tn], start=(c == 0), stop=(c == 15))],
                    reads=[sq_b[s], const_b], writes=[ab])
        for ti, (t0, tn) in enumerate(TILES):
            a, ab = accs[ti]
            P.op("act", lambda e, a=a, t0=t0, tn=tn: e.activation(
                out=rstd[:, t0:t0 + tn], in_=a[:, 0:tn], func=AF.Sqrt, bias=epsc[:, 0:1], scale=1.0 / D),
                reads=[ab, const_b], writes=[rstd_b])
        P.op("dve", lambda e: e.reciprocal(rstd[:], rstd[:]), reads=[rstd_b], writes=[rstd_b])
        for c in range(16):
            P.op("dve", lambda e, c=c: e.scalar_tensor_tensor(
                xn[:, c, :], xT[:, c, :], wcol[:, c:c + 1], rstd[:], ALU.mult, ALU.mult),
                reads=[xT_b[c], wcol_b, rstd_b], writes=[xn_b[c]])
        return wcol, wcol_b

    WPE = 4096
    wp = [sb("wp%d" % i, [128, WPE], BF16) for i in range(2)]
    wp_b = [Buf("wp0"), Buf("wp1")]

    def load_panel(W, KC, c0, w):
        i = rr["wp"] % 2
        rr["wp"] += 1
        pv = wp[i][:, 0:KC * w].rearrange("p (k n) -> p k n", k=KC)
        src = W[:, c0:c0 + w].rearrange("(kc p) n -> p kc n", p=128)
        P.dma("pool", lambda e: e.dma_start(out=pv, in_=src), writes=[wp_b[i]])
        return pv, wp_b[i]

    def gemm_fm(pv, pvb, KC, parts, rhs, rhs_b, evac, tiles=TILES):
        for ti, (t0, tn) in enumerate(tiles):
            a, ab = next_acc()
            fns = []
            for (col, msz, op0) in parts:
                for kc in range(KC):
                    fns.append(lambda e, a=a, col=col, msz=msz, op0=op0, kc=kc, t0=t0, tn=tn: e.matmul(
                        a[op0:op0 + msz, 0:tn], pv[:, kc, col:col + msz], rhs(kc, t0, tn),
                        start=(kc == 0), stop=(kc == KC - 1)))
            P.group("pe", fns, reads=[pvb] + list(rhs_b), writes=[ab])
            evac(a, ab, ti, t0, tn)

    def gemm_tm(pv, pvb, KC, ncols, lhs, lhs_b, evac, blocks):
        for (bi, t0, tn) in blocks:
            a, ab = next_acc()
            fns = []
            for kc in range(KC):
                fns.append(lambda e, a=a, kc=kc, t0=t0, tn=tn: e.matmul(
                    a[0:tn, 0:ncols], lhs(kc, t0, tn), pv[:, kc, 0:ncols],
                    start=(kc == 0), stop=(kc == KC - 1)))
            P.group("pe", fns, reads=[pvb] + list(lhs_b), writes=[ab])
            evac(a, ab, bi, t0, tn)

    xn_rhs = lambda kc, t0, tn: xn[:, kc, t0:t0 + tn]

    TK = 128 + TP + NS
    qT = aalloc([128, 12, NT], BF16)
    qT_b = [Buf("qT%d" % c) for c in range(12)]
    kT2 = aalloc([128, 4, TK], BF16)
    kT2_b = Buf("kT2")
    vT2S = aalloc([128, 4, NS], F32)
    vT2S_b = Buf("vT2S")
    vtok = aalloc([128, 10, 256], BF16)
    vtok_b = [Buf("vtok%d" % i) for i in range(10)]
    mark_attn = ar["off"]
    uext = aalloc([128, 4, 15 + TP], F32)
    uext_b = [Buf("uext%d" % g) for g in range(4)]
    uS = aalloc([128, 4, NS], F32)
    uS_b = Buf("uS")
    tokst = aalloc([128, 2, 256], F32)
    tokst_b = [Buf("tokst0"), Buf("tokst1")]

    rmsnorm("l0_norm_mix")
    maybe_stop("norm", xn[:, :, :], xn_b, [128, 16, NT], BF16)

    W = I["l0_w_in"]
    def evac_u(g):
        def f(a, ab, ti, t0, tn):
            eng = evac_eng()
            if ti < 2:
                P.op(eng, copy_fn(eng, uext[:, g, 15 + t0:15 + t0 + tn], a[:, 0:tn]), reads=[ab],
                     writes=[uext_b[g]])
            else:
                P.op(eng, copy_fn(eng, uS[:, g, :], a[:, 0:tn]), reads=[ab], writes=[uS_b])
        return f

    for pi in range(2):
        pv, pvb = load_panel(W, 16, pi * 256, 256)
        if KSTOP == "panel":
            maybe_stop("panel", pv, [pvb], [128, 16, 256], BF16)
        for m in range(2):
            gemm_fm(pv, pvb, 16, [(m * 128, 128, 0)], xn_rhs, xn_b, evac_u(pi * 2 + m))
        if KSTOP == "gemm":
            maybe_stop("gemm", uext[:, 0:2, :], uext_b, [128, 2, 15 + TP], F32)

        def evac_utok(a, ab, bi, t0, tn, pi=pi):
            s = bi % 2
            eng = evac_eng()
            P.op(eng, copy_fn(eng, tokst[0:tn, s, 0:256], a[0:tn, 0:256]), reads=[ab], writes=[tokst_b[s]])
            if bi == 0:
                P.dma("sp", lambda e, s=s: e.dma_start(out=O["pool_p"][:, pi * 256:(pi + 1) * 256],
                                                       in_=tokst[113:128, s, 0:256]),
                      reads=[tokst_b[s]], writes=[OB["pool_p"]])
            else:
                P.dma("sp", lambda e, s=s: e.dma_start(out=O["pool_s"][:, 14, pi * 256:(pi + 1) * 256],
                                                       in_=tokst[0:NS, s, 0:256]),
                      reads=[tokst_b[s]], writes=[OB["pool_s"]])
        gemm_tm(pv, pvb, 16, 256, xn_rhs, xn_b, evac_utok, [(0, TP - 128, 128), (1, TP, NS)])
        if KSTOP == "tm":
            maybe_stop("tm", tokst[:, :, :], tokst_b, [128, 2, 256], F32)
    P.dma("sp", lambda e: e.dma_start(out=O["pool_s"][:, 0:14, :], in_=I["st_pool"][:, 1:15, :]),
          writes=[OB["pool_s"]])

    if KSTOP == "d2d":
        maybe_stop("d2d", tokst[:, :, :], tokst_b, [128, 2, 256], F32)
    for pi in range(6):
        pv, pvb = load_panel(W, 16, 512 + pi * 256, 256)
        for m in range(2):
            c = pi * 2 + m

            def evac_q(a, ab, ti, t0, tn, c=c):
                eng = evac_eng()
                P.op(eng, copy_fn(eng, qT[:, c, t0:t0 + tn], a[:, 0:tn], 0.125), reads=[ab], writes=[qT_b[c]])
            gemm_fm(pv, pvb, 16, [(m * 128, 128, 0)], xn_rhs, xn_b, evac_q)

    pv, pvb = load_panel(W, 16, 2048, 256)
    for g in range(4):
        def evac_k(a, ab, ti, t0, tn, g=g):
            eng = evac_eng()
            P.op(eng, copy_fn(eng, kT2[:, g, 128 + t0:128 + t0 + tn], a[:, 0:tn]), reads=[ab], writes=[kT2_b])
        gemm_fm(pv, pvb, 16, [(g * 64, 64, 0), (g * 64, 64, 64)], xn_rhs, xn_b, evac_k)

    if KSTOP == "k":
        maybe_stop("k", kT2[:, :, 128:128 + TP], [kT2_b], [128, 4, TP], BF16)

    def evac_ktok(a, ab, bi, t0, tn):
        s = bi % 2
        eng = evac_eng()
        P.op(eng, copy_fn(eng, tokst[0:tn, s, 0:256], a[0:tn, 0:256]), reads=[ab], writes=[tokst_b[s]])
        if bi == 0:
            P.dma("sp", lambda e, s=s: e.dma_start(out=O["k_p"][:, :], in_=tokst[:, s, 0:256]),
                  reads=[tokst_b[s]], writes=[OB["k_p"]])
        else:
            P.dma("sp", lambda e, s=s: e.dma_start(out=O["k_s"][:, 127, :], in_=tokst[0:NS, s, 0:256]),
                  reads=[tokst_b[s]], writes=[OB["k_s"]])
    gemm_tm(pv, pvb, 16, 256, xn_rhs, xn_b, evac_ktok, [(0, TP - 128, 128), (1, TP, NS)])

    if KSTOP == "ktok":
        maybe_stop("ktok", tokst[:, :, :], tokst_b, [128, 2, 256], F32)
    pv, pvb = load_panel(W, 16, 2304, 256)
    for g in range(4):
        def evac_vs(a, ab, ti, t0, tn, g=g):
            eng = evac_eng()
            P.op(eng, copy_fn(eng, vT2S[:, g, :], a[:, 0:tn]), reads=[ab], writes=[vT2S_b])
        gemm_fm(pv, pvb, 16, [(g * 64, 64, 0), (g * 64, 64, 64)], xn_rhs, xn_b, evac_vs, tiles=[TILES[2]])

    if KSTOP == "vs":
        maybe_stop("vs", vT2S[:, :, :], [vT2S_b], [128, 4, NS], F32)

    def evac_vtok(a, ab, bi, t0, tn):
        eng = evac_eng()
        P.op(eng, copy_fn(eng, vtok[0:tn, bi, :], a[0:tn, 0:256]), reads=[ab], writes=[vtok_b[bi]])
        if bi >= 8:
            s = bi % 2
            P.op(eng, copy_fn(eng, tokst[0:tn, s, 0:256], a[0:tn, 0:256]), reads=[ab], writes=[tokst_b[s]])
            if bi == 8:
                P.dma("sp", lambda e, s=s: e.dma_start(out=O["v_p"][:, :], in_=tokst[:, s, 0:256]),
                      reads=[tokst_b[s]], writes=[OB["v_p"]])
            else:
                P.dma("sp", lambda e, s=s: e.dma_start(out=O["v_s"][:, 127, :], in_=tokst[0:NS, s, 0:256]),
                      reads=[tokst_b[s]], writes=[OB["v_s"]])
    _vb = [(b + 1, b * 128, 128) for b in range(8)] + [(9, TP, NS)]
    if KSTOP == "vtokA":
        _vb = _vb[:7]
    if KSTOP == "vtokB":
        _vb = _vb[:8]
    gemm_tm(pv, pvb, 16, 256, xn_rhs, xn_b, evac_vtok, _vb)
    if KSTOP in ("vtokA", "vtokB"):
        maybe_stop(KSTOP, vtok[:, 1:8, :], vtok_b[1:8], [128, 7, 256], BF16)
    if KSTOP == "vtok":
        maybe_stop("vtok", vtok[:, :, :], vtok_b[1:], [128, 10, 256], BF16)
    P.dma("sp", lambda e: e.dma_start(out=O["k_s"][:, 0:127, :], in_=I["ck"][:, 1:128, :]), writes=[OB["k_s"]])
    P.dma("sp", lambda e: e.dma_start(out=O["v_s"][:, 0:127, :], in_=I["cv"][:, 1:128, :]), writes=[OB["v_s"]])

    if stage == 1:
        P.replay()
        es.close()
        return nc

    mark_halo = ar["off"]
    HW_ = 60 + 256 + 128
    hs = aalloc([128, HW_], F32)
    hs_b = Buf("hs")
    hg = aalloc([128, 2, HW_], F32)
    hg_b = [Buf("hg0"), Buf("hg1")]
    bin_t = nc.dram_tensor("halo_in", [128, HW_], F32)
    bout_t = nc.dram_tensor("halo_out", [NCORES * 128, HW_], F32)
    bin_b, bout_b = Buf("bin"), Buf("bout")

    def views(t):
        return (t[:, 0:60].rearrange("p (g t) -> p g t", g=4),
                t[:, 60:316].bitcast(BF16).rearrange("p (g t) -> p g t", g=4),
                t[:, 316:444].bitcast(BF16))
    hu, hk, hv = views(hs)
    P.op("dve", lambda e: e.tensor_copy(hu, uext[:, :, TP:TP + 15]), reads=uext_b, writes=[hs_b])
    P.op("dve", lambda e: e.tensor_copy(hk, kT2[:, :, TP:TP + 128]), reads=[kT2_b], writes=[hs_b])
    P.op("dve", lambda e: e.tensor_copy(hv, vtok[:, 8, :]), reads=[vtok_b[8]], writes=[hs_b])
    P.dma("sp", lambda e: e.dma_start(out=bin_t.ap(), in_=hs), reads=[hs_b], writes=[bin_b])
    P.collective(lambda e: e.collective_compute(
        "AllGather", ALU.bypass, replica_groups=[list(range(NCORES))],
        ins=[bin_t.ap().opt()], outs=[bout_t.ap().opt()]), reads=[bin_b], writes=[bout_b])
    for r in range(NCORES):
        s_ = r % 2
        P.dma("sp", lambda e, r=r, s_=s_: e.dma_start(out=hg[:, s_, :], in_=bout_t.ap()[r * 128:(r + 1) * 128, :]),
              reads=[bout_b], writes=[hg_b[s_]])
        gu, gk, gv = views(hg[:, s_, :])
        dsts = [(uext[:, :, 0:15], gu, uext_b), (kT2[:, :, 0:128], gk, [kT2_b]), (vtok[:, 0, :], gv, [vtok_b[0]])]
        for (dst, src_, db) in dsts:
            if r == 0:
                P.op("dve", lambda e, dst=dst, src_=src_, r=r: e.tensor_scalar_mul(dst, src_, onehot[:, r:r + 1]),
                     reads=[hg_b[s_], const_b], writes=db)
            else:
                P.op("dve", lambda e, dst=dst, src_=src_, r=r: e.scalar_tensor_tensor(
                    dst, src_, onehot[:, r:r + 1], dst, ALU.mult, ALU.add),
                    reads=[hg_b[s_], const_b], writes=db)

    if stage == 2:
        dbg3 = nc.dram_tensor("dbg_u", [128, 4, 15 + TP], F32, kind="ExternalOutput").ap()
        P.dma("sp", lambda e: e.dma_start(out=dbg3[:, :, :], in_=uext), reads=uext_b)
        P.replay()
        es.close()
        return nc

    P.barrier()
    ar["off"] = mark_halo
    L = 15 + TP
    tmpA = rstd[:, 0:L]
    tmpB = sq[:, :, :].rearrange("p a b -> p (a b)").bitcast(F32)[:, 0:L]
    tmp_b = [rstd_b, Buf("tmpB")]
    poolw = aalloc([128, 4, 128], BF16)
    poolw_b = Buf("poolw")
    P.dma("pool", lambda e: e.dma_start(out=poolw, in_=I["l0_pool_w"].rearrange("g c d -> c g d")),
          writes=[poolw_b])
    pscale, pscale_b = load_cols(I["l0_pool_scale"].rearrange("(c p) -> c p", p=128), 4, "pscale")
    corr = aalloc([128, 4, 16], F32)
    corr_b = Buf("corr")
    P.dma("sp", lambda e: e.dma_start(out=corr, in_=I["c_poolcorr"][:, :, :]), writes=[corr_b])
    diff = aalloc([128, 1, NT], BF16)
    diff_b = [Buf("diff0"), Buf("diff0b")]
    diff_b[1] = diff_b[0]
    stp = aalloc([NS, 15, 128], F32)
    stp_b = Buf("stp")
    Hs = aalloc([NS, 512], F32)
    Hs_b = Buf("Hs")
    HsT = aalloc([128, 4, NS], F32)
    HsT_b = Buf("HsT")
    for g in range(4):
        w = 2 ** (g + 1)
        P.dma("sp", lambda e, g=g: e.dma_start(out=stp, in_=I["st_pool"][:, :, g * 128:(g + 1) * 128]),
              writes=[stp_b])
        P.op("dve", lambda e, g=g, w=w: e.tensor_reduce(
            Hs[:, g * 128:(g + 1) * 128], stp[:, 15 - (w - 1):15, :].rearrange("p r c -> p c r"),
            AX.X, ALU.add), reads=[stp_b], writes=[Hs_b])
    for g in range(4):
        pt, ptb = next_pst()
        P.group("pe", [lambda e, pt=pt, g=g: e.transpose(pt[:, 0:NS], Hs[0:NS, g * 128:(g + 1) * 128],
                                                         ident[0:NS, 0:NS])],
                reads=[Hs_b, ident_b], writes=[ptb])
        P.op("dve", lambda e, pt=pt, g=g: e.tensor_copy(HsT[:, g, :], pt[:, 0:NS]), reads=[ptb], writes=[HsT_b])
    for g in range(4):
        w = 2 ** (g + 1)
        d_ = 0
        cur = uext[:, g, :]
        curb = uext_b[g]
        off = 0
        for step in range(g + 1):
            sh = 2 ** step
            new_ = tmpA if step % 2 == 0 else tmpB
            newb = tmp_b[step % 2]
            P.op("dve", lambda e, new_=new_, cur=cur, off=off, sh=sh: e.tensor_tensor(
                new_[:, off + sh:L], cur[:, off + sh:L], cur[:, off:L - sh], ALU.add),
                reads=[curb], writes=[newb])
            cur, curb = new_, newb
            off += sh
        P.op("dve", lambda e, cur=cur, g=g: e.tensor_tensor(cur[:, 15:31], cur[:, 15:31], corr[:, g, :], ALU.mult),
             reads=[curb, corr_b], writes=[curb])
        P.op("dve", lambda e, cur=cur, g=g, w=w, d_=d_: e.scalar_tensor_tensor(
            diff[:, d_, 0:TP], cur[:, 15:L], 1.0 / w, uext[:, g, 15:L], ALU.mult, ALU.subtract),
            reads=[curb, uext_b[g]], writes=[diff_b[d_]])
        P.op("dve", lambda e, g=g: e.tensor_tensor(HsT[:, g, :], HsT[:, g, :], uS[:, g, :], ALU.add),
             reads=[HsT_b, uS_b], writes=[HsT_b])
        P.op("dve", lambda e, g=g, w=w, d_=d_: e.scalar_tensor_tensor(
            diff[:, d_, TP:NT], HsT[:, g, :], 1.0 / w, uS[:, g, :], ALU.mult, ALU.subtract),
            reads=[HsT_b, uS_b], writes=[diff_b[d_]])
        for ti, (t0, tn) in enumerate(TILES):
            a_, ab = next_acc()
            P.group("pe", [lambda e, a_=a_, g=g, d_=d_, t0=t0, tn=tn: e.matmul(
                a_[:, 0:tn], poolw[:, g, :], diff[:, d_, t0:t0 + tn], start=True, stop=True)],
                reads=[poolw_b, diff_b[d_]], writes=[ab])
            P.op("act", lambda e, a_=a_, g=g, t0=t0, tn=tn: e.mul(xn[:, g, t0:t0 + tn], a_[:, 0:tn], pscale[:, g:g + 1]),
                 reads=[ab, pscale_b], writes=[xn_b[g]])

    P.barrier()
    ar["off"] = mark_attn
    mk = aalloc([128, 3, 128], BF16)
    mk_b = Buf("mk")
    P.dma("sp", lambda e: e.dma_start(out=cst[:, :], in_=I["c_masks"][:, :]), writes=[cst_b])
    P.op("dve", lambda e: e.tensor_copy(mk[:, 0:2, :], cst[:, :].rearrange("p (a b) -> p a b", a=2)),
         reads=[cst_b], writes=[mk_b])
    P.op("dve", lambda e: e.tensor_scalar_mul(mk[:, 2, :], cst[:, 128:256], flag[:, 0:1]),
         reads=[cst_b, const_b], writes=[mk_b])
    es24 = aalloc([128, 24], F32)
    esP = aalloc([128, 12], F32)
    esP_b = Buf("esP")
    P.dma("sp", lambda e: e.dma_start(out=es24, in_=I["l0_sinks"].partition_broadcast(128)), writes=[esP_b])
    P.op("act", lambda e: e.activation(out=es24, in_=es24, func=AF.Exp), reads=[esP_b], writes=[esP_b])
    es3 = es24.rearrange("p (c two) -> p c two", two=2)
    P.op("dve", lambda e: e.tensor_copy(esP[0:64, :], es3[0:64, :, 0]), reads=[esP_b], writes=[esP_b])
    P.op("dve", lambda e: e.tensor_copy(esP[64:128, :], es3[64:128, :, 1]), reads=[esP_b], writes=[esP_b])

    pT = aalloc([128, 8, 384], BF16)
    pT_b = [Buf("pT%d" % i) for i in range(8)]
    den = aalloc([128, 2, 384], F32)
    den_b = [Buf("den0"), Buf("den1")]
    it = 0
    for qb in range(8):
        for g in range(4):
            po, pob = next_pst()
            pd, pdb = next_pst()
            for half in range(2):
                p0 = 64 * half
                pts = []
                for kb in range(2):
                    kc0 = qb * 128 + kb * 128
                    a_, ab = next_acc()
                    P.group("pe", [lambda e, a_=a_, p0=p0, g=g, kc0=kc0, qb=qb: e.matmul(
                        a_[:, 0:384].rearrange("p (a b) -> p a b", a=3),
                        kT2[p0:p0 + 64, g, kc0:kc0 + 128],
                        qT[p0:p0 + 64, 3 * g:3 * g + 3, qb * 128:(qb + 1) * 128], start=True, stop=True)],
                        reads=[kT2_b] + qT_b[3 * g:3 * g + 3], writes=[ab])
                    pi_ = (it % 2) * 4 + half * 2 + kb
                    pv_ = pT[:, pi_, :]
                    P.op("act", lambda e, a_=a_, pv_=pv_: e.activation(out=pv_, in_=a_[:, 0:384], func=AF.Exp),
                         reads=[ab], writes=[pT_b[pi_]])
                    mi = 0 if kb == 1 else (2 if qb == 0 else 1)
                    pv3 = pv_.rearrange("p (a b) -> p a b", a=3)
                    P.op("dve", lambda e, pv3=pv3, mi=mi: e.tensor_tensor(
                        pv3, pv3, mk[:, mi, :].unsqueeze(1).broadcast_to([128, 3, 128]), ALU.mult),
                        reads=[mk_b], writes=[pT_b[pi_]])
                    pts.append((pv_, pT_b[pi_]))
                fns = []
                for kb in range(2):
                    fns.append(lambda e, po=po, p0=p0, kb=kb, qb=qb, g=g, pv_=pts[kb][0]: e.matmul(
                        po[p0:p0 + 64, 0:384], vtok[:, qb + kb, g * 64:(g + 1) * 64], pv_,
                        start=(kb == 0), stop=(kb == 1)))
                P.group("pe", fns, reads=[pts[0][1], pts[1][1], vtok_b[qb], vtok_b[qb + 1]], writes=[pob])
                fns = []
                for kb in range(2):
                    fns.append(lambda e, pd=pd, p0=p0, kb=kb, pv_=pts[kb][0]: e.matmul(
                        pd[p0:p0 + 64, 0:384], ones_bf[:, 0:64], pv_, start=(kb == 0), stop=(kb == 1)))
                P.group("pe", fns, reads=[pts[0][1], pts[1][1], const_b], writes=[pdb])
            d_ = it % 2
            dv = den[:, d_, :].rearrange("p (a b) -> p a b", a=3)
            P.op("dve", lambda e, dv=dv, pd=pd, g=g: e.tensor_tensor(
                dv, pd[:, 0:384].rearrange("p (a b) -> p a b", a=3),
                esP[:, 3 * g:3 * g + 3].unsqueeze(2).broadcast_to([128, 3, 128]), ALU.add),
                reads=[pdb, esP_b], writes=[den_b[d_]])
            P.op("dve", lambda e, dv=dv: e.reciprocal(dv, dv), reads=[den_b[d_]], writes=[den_b[d_]])
            P.op("dve", lambda e, dv=dv, po=po, g=g, qb=qb: e.tensor_tensor(
                xn[:, 4 + 3 * g:4 + 3 * g + 3, qb * 128:(qb + 1) * 128],
                po[:, 0:384].rearrange("p (a b) -> p a b", a=3), dv, ALU.mult),
                reads=[pob, den_b[d_]], writes=xn_b[4 + 3 * g:4 + 3 * g + 3])
            it += 1

    kdup = aalloc([128, 2, 4, 128], F32)
    kdup_b = [Buf("kdup0"), Buf("kdup1")]
    KcT = aalloc([128, 2, 4, 128], BF16)
    KcT_b = [Buf("KcT0"), Buf("KcT1")]
    vcS = aalloc([128, NS, 256], BF16)
    vcS_b = Buf("vcS")
    P.dma("pool", lambda e: e.dma_start(out=vcS, in_=I["cv"].rearrange("b k f -> k b f")), writes=[vcS_b])
    psS, psS_b = next_acc()
    for b_ in range(NS):
        s_ = b_ % 2
        kd4 = kdup[:, s_, :, :].rearrange("p g (two d) -> p g two d", two=2)
        for two in range(2):
            P.dma("sp", lambda e, b_=b_, two=two, kd4=kd4: e.dma_start(
                out=kd4[:, :, two, :], in_=I["ck"][b_, :, :].rearrange("k (g d) -> k g d", g=4)),
                writes=[kdup_b[s_]])
        pt, ptb = next_pst()
        P.group("pe", [lambda e, pt=pt, g=g, s_=s_: e.transpose(pt[:, g * 128:(g + 1) * 128], kdup[:, s_, g, :],
                                                                ident[:, :]) for g in range(4)],
                reads=[kdup_b[s_], ident_b], writes=[ptb])
        P.op("act", lambda e, pt=pt, s_=s_: e.copy(KcT[:, s_, :, :], pt[:, :].rearrange("p (g k) -> p g k", g=4)),
             reads=[ptb], writes=[KcT_b[s_]])
        fns = []
        for half in range(2):
            p0 = 64 * half
            for g in range(4):
                c0 = ((half * 4 + g) * NS + b_) * 3
                fns.append(lambda e, p0=p0, g=g, c0=c0, s_=s_, b_=b_: e.matmul(
                    psS[:, c0:c0 + 3], KcT[p0:p0 + 64, s_, g, :], qT[p0:p0 + 64, 3 * g:3 * g + 3, TP + b_],
                    start=True, stop=True))
        P.group("pe", fns, reads=[KcT_b[s_]] + qT_b, writes=[psS_b])
    pS = aalloc([128, 384], BF16)
    pS_b = Buf("pS")
    P.op("act", lambda e: e.activation(out=pS, in_=psS[:, 0:384], func=AF.Exp), reads=[psS_b], writes=[pS_b])
    po2, po2b = next_pst()
    pd2, pd2b = next_pst()
    fo, fd = [], []
    for b_ in range(NS):
        for half in range(2):
            p0 = 64 * half
            for g in range(4):
                c0 = ((half * 4 + g) * NS + b_) * 3
                o0 = (g * NS + b_) * 3
                fo.append(lambda e, p0=p0, g=g, c0=c0, o0=o0, b_=b_: e.matmul(
                    po2[p0:p0 + 64, o0:o0 + 3], vcS[:, b_, g * 64:(g + 1) * 64], pS[:, c0:c0 + 3],
                    start=True, stop=True))
                fd.append(lambda e, p0=p0, c0=c0, o0=o0: e.matmul(
                    pd2[p0:p0 + 64, o0:o0 + 3], ones_bf[:, 0:64], pS[:, c0:c0 + 3], start=True, stop=True))
    P.group("pe", fo, reads=[pS_b, vcS_b], writes=[po2b])
    P.group("pe", fd, reads=[pS_b, const_b], writes=[pd2b])
    prod = aalloc([128, 12, NS], BF16)
    prod_b = Buf("prod")
    for g in range(4):
        P.op("dve", lambda e, g=g: e.tensor_tensor(
            prod[:, 3 * g:3 * g + 3, :], qT[:, 3 * g:3 * g + 3, TP:NT],
            kT2[:, g, 128 + TP:TK].unsqueeze(1).broadcast_to([128, 3, NS]), ALU.mult),
            reads=qT_b[3 * g:3 * g + 3] + [kT2_b], writes=[prod_b])
    psN, psN_b = next_acc()
    P.group("pe", [lambda e: e.matmul(psN[:, 0:12 * NS], bdones[:, :], prod.rearrange("p a b -> p (a b)"),
                                      start=True, stop=True)], reads=[prod_b, const_b], writes=[psN_b])
    pnew = aalloc([128, 12, NS], F32)
    onew = aalloc([128, 12, NS], F32)
    pn_b = Buf("pnew")
    P.op("act", lambda e: e.activation(out=pnew.rearrange("p a b -> p (a b)"), in_=psN[:, 0:12 * NS], func=AF.Exp),
         reads=[psN_b], writes=[pn_b])
    for g in range(4):
        P.op("dve", lambda e, g=g: e.tensor_tensor(
            onew[:, 3 * g:3 * g + 3, :], pnew[:, 3 * g:3 * g + 3, :],
            vT2S[:, g, :].unsqueeze(1).broadcast_to([128, 3, NS]), ALU.mult),
            reads=[pn_b, vT2S_b], writes=[pn_b])
        pov = po2[:, g * 48:(g + 1) * 48].rearrange("p (b j) -> p j b", j=3)
        pdv = pd2[:, g * 48:(g + 1) * 48].rearrange("p (b j) -> p j b", j=3)
        P.op("dve", lambda e, g=g, pov=pov: e.tensor_tensor(
            onew[:, 3 * g:3 * g + 3, :], pov, onew[:, 3 * g:3 * g + 3, :], ALU.add),
            reads=[po2b, pn_b], writes=[pn_b])
        P.op("dve", lambda e, g=g, pdv=pdv: e.tensor_tensor(
            pnew[:, 3 * g:3 * g + 3, :], pdv, pnew[:, 3 * g:3 * g + 3, :], ALU.add),
            reads=[pd2b, pn_b], writes=[pn_b])
        P.op("dve", lambda e, g=g: e.tensor_tensor(
            pnew[:, 3 * g:3 * g + 3, :], pnew[:, 3 * g:3 * g + 3, :],
            esP[:, 3 * g:3 * g + 3].unsqueeze(2).broadcast_to([128, 3, NS]), ALU.add),
            reads=[esP_b, pn_b], writes=[pn_b])
        P.op("dve", lambda e, g=g: e.reciprocal(pnew[:, 3 * g:3 * g + 3, :], pnew[:, 3 * g:3 * g + 3, :]),
             reads=[pn_b], writes=[pn_b])
        P.op("dve", lambda e, g=g: e.tensor_tensor(
            xn[:, 4 + 3 * g:4 + 3 * g + 3, TP:NT], onew[:, 3 * g:3 * g + 3, :], pnew[:, 3 * g:3 * g + 3, :], ALU.mult),
            reads=[pn_b], writes=xn_b[4 + 3 * g:4 + 3 * g + 3])

    if stage == 3:
        dbg = nc.dram_tensor("dbg_mix", [128, 16, NT], BF16, kind="ExternalOutput").ap()
        P.dma("sp", lambda e: e.dma_start(out=dbg[:, :, :], in_=xn[:, :, :]), reads=xn_b)
        P.replay()
        es.close()
        return nc

    def load_panel_rows(Wd, row0, KC, c0, w):
        i = rr["wp"] % 2
        rr["wp"] += 1
        pv = wp[i][:, 0:KC * w].rearrange("p (k n) -> p k n", k=KC)
        src = Wd[row0:row0 + KC * 128, c0:c0 + w].rearrange("(kc p) n -> p kc n", p=128)
        P.dma("pool", lambda e: e.dma_start(out=pv, in_=src), writes=[wp_b[i]])
        return pv, wp_b[i]

    def proj_residual_rows(Wd, row0, KC, rhs, rhs_b):
        for pi in range(8):
            pv, pvb = load_panel_rows(Wd, row0, KC, pi * 256, 256)
            for m in range(2):
                c = pi * 2 + m

                def evac_res(a, ab, ti, t0, tn, c=c):
                    P.op("dve", lambda e, a=a, t0=t0, tn=tn, c=c: e.tensor_tensor(
                        xT[:, c, t0:t0 + tn], a[:, 0:tn], xT[:, c, t0:t0 + tn], ALU.add),
                        reads=[ab], writes=[xT_b[c]])
                gemm_fm(pv, pvb, KC, [(m * 128, 128, 0)], rhs, rhs_b, evac_res)

    def proj_residual(Wd, rhs=None, rhs_b=None):
        rhs = rhs or xn_rhs
        rhs_b = rhs_b or xn_b
        for pi in range(8):
            pv, pvb = load_panel(Wd, 16, pi * 256, 256)
            for m in range(2):
                c = pi * 2 + m

                def evac_res(a, ab, ti, t0, tn, c=c):
                    P.op("dve", lambda e, a=a, t0=t0, tn=tn, c=c: e.tensor_tensor(
                        xT[:, c, t0:t0 + tn], a[:, 0:tn], xT[:, c, t0:t0 + tn], ALU.add),
                        reads=[ab], writes=[xT_b[c]])
                gemm_fm(pv, pvb, 16, [(m * 128, 128, 0)], rhs, rhs_b, evac_res)
    proj_residual(I["l0_w_out"])
    P.barrier()
    ar["off"] = 0

    if stage == 4:
        dbg = nc.dram_tensor("dbg_x", [128, 16, NT], F32, kind="ExternalOutput").ap()
        P.dma("sp", lambda e: e.dma_start(out=dbg[:, :, :], in_=xT[:, :, :]), reads=xT_b)
        P.replay()
        es.close()
        return nc

    xh = sb("xh", [128, 16, 3], F32)
    xnh = sb("xnh", [128, 16, 3], BF16)
    xh_b, xnh_b = Buf("xh"), Buf("xnh")
    rsth = sb("rsth", [128, 3], F32)
    sqh = sb("sqh", [128, 16, 3], BF16)
    xex = {"n": 0}

    def exchange_x():
        n = xex["n"]
        xex["n"] += 1
        src_t = aalloc([128, 48], F32)
        gat = aalloc([128, 2, 48], F32)
        gat_b = [Buf("gat0"), Buf("gat1")]
        src_b = Buf("xsrc")
        bi = nc.dram_tensor("xh_in%d" % n, [128, 48], F32)
        bo = nc.dram_tensor("xh_out%d" % n, [NCORES * 128, 48], F32)
        bib, bob = Buf("bi"), Buf("bo")
        P.op("dve", lambda e: e.tensor_copy(src_t.rearrange("p (c t) -> p c t", c=16), xT[:, :, TP - 3:TP]),
             reads=xT_b, writes=[src_b])
        P.dma("sp", lambda e: e.dma_start(out=bi.ap(), in_=src_t), reads=[src_b], writes=[bib])
        P.collective(lambda e: e.collective_compute(
            "AllGather", ALU.bypass, replica_groups=[list(range(NCORES))],
            ins=[bi.ap().opt()], outs=[bo.ap().opt()]), reads=[bib], writes=[bob])
        dst = xh[:, :, :].rearrange("p c t -> p (c t)")
        for r in range(NCORES):
            s_ = r % 2
            P.dma("sp", lambda e, r=r, s_=s_: e.dma_start(out=gat[:, s_, :], in_=bo.ap()[r * 128:(r + 1) * 128, :]),
                  reads=[bob], writes=[gat_b[s_]])
            if r == 0:
                P.op("dve", lambda e, s_=s_, r=r: e.tensor_scalar_mul(dst, gat[:, s_, :], onehot[:, r:r + 1]),
                     reads=[gat_b[s_], const_b], writes=[xh_b])
            else:
                P.op("dve", lambda e, s_=s_, r=r: e.scalar_tensor_tensor(
                    dst, gat[:, s_, :], onehot[:, r:r + 1], dst, ALU.mult, ALU.add),
                    reads=[gat_b[s_], const_b], writes=[xh_b])

    def rmsnorm_halo(wname, wcol, wcol_b):
        P.op("act", lambda e: e.activation(out=sqh[:, :, :], in_=xh[:, :, :], func=AF.Square),
             reads=[xh_b], writes=[xnh_b])
        a_, ab = next_acc()
        P.group("pe", [lambda e, a_=a_, c=c: e.matmul(a_[:, 0:3], ones_bf[:], sqh[:, c, :], start=(c == 0),
                                                    stop=(c == 15)) for c in range(16)],
                reads=[xnh_b, const_b], writes=[ab])
        P.op("act", lambda e, a_=a_: e.activation(out=rsth[:, :], in_=a_[:, 0:3], func=AF.Sqrt, bias=epsc[:, 0:1],
                                                  scale=1.0 / D), reads=[ab, const_b], writes=[xnh_b])
        P.op("dve", lambda e: e.reciprocal(rsth[:, :], rsth[:, :]), reads=[xnh_b], writes=[xnh_b])
        for c in range(16):
            P.op("dve", lambda e, c=c: e.scalar_tensor_tensor(
                xnh[:, c, :], xh[:, c, :], wcol[:, c:c + 1], rsth[:, :], ALU.mult, ALU.mult),
                reads=[xh_b, wcol_b, xnh_b], writes=[xnh_b])

    def ffn(Lx):
        pre = "l%d_" % Lx
        Wup, Wdn = I[pre + "ffn_w_up"], I[pre + "ffn_w_down"]
        st_in = I["st_ffn%d" % Lx]
        o_p, o_s = "ffn%d_p" % Lx, "ffn%d_s" % Lx
        P.barrier()
        ar["off"] = 0
        exchange_x()
        rmsnorm(pre + "norm_ffn")
        wcol, wcol_b = load_cols(I[pre + "norm_ffn"].rearrange("(c p) -> c p", p=128), 16, "wch%d" % Lx)
        rmsnorm_halo(pre + "norm_ffn", wcol, wcol_b)
        cw, cw_b = load_cols(I[pre + "ffn_conv"].rearrange("j (c p) -> (j c) p", p=128), 264, "cw%d" % Lx)
        P.dma("sp", lambda e: e.dma_start(out=O[o_s][:, 0, :], in_=st_in[:, 1, :]), writes=[OB[o_s]])
        hid = aalloc([128, 12, NT], BF16)
        hid_b = [Buf("hid%d" % i) for i in range(12)]
        U = [aalloc([128, 2 + TP], F32) for _ in range(2)]
        U_b = [Buf("Ua"), Buf("Ub")]
        US = [aalloc([128, 18], F32) for _ in range(2)]
        US_b = [Buf("USa"), Buf("USb")]
        CV = [aalloc([128, NT], F32) for _ in range(2)]
        CV_b = [Buf("ca"), Buf("cb")]
        hst = aalloc([NS, 2, 2, 256], F32)
        hst_b = [Buf("hsta"), Buf("hstb")]
        hT = aalloc([128, 2, 4 * NS], F32)
        hT_b = [Buf("hTa"), Buf("hTb")]
        ost = aalloc([18, 2, 256], F32)
        ost_b = [Buf("osta"), Buf("ostb")]
        FT = TILES + [(-1, 2)]

        def rhs_f(kc, t0, tn):
            if t0 < 0:
                return xnh[:, kc, 1:3]
            return xn[:, kc, t0:t0 + tn]

        gs = [12, 12, 12, 8]
        j_base = 0
        for gsz in gs:
            for jp in range(0, gsz, 2):
                j0 = j_base + jp
                pvs = []
                for h_ in range(2):
                    colbase = h_ * DFF + j0 * 128
                    pvs.append(load_panel(Wup, 16, colbase, 256))
                    P.dma("sp", lambda e, colbase=colbase, h_=h_: e.dma_start(
                        out=hst[:, h_, :, :], in_=st_in[:, :, colbase:colbase + 256]), writes=[hst_b[h_]])
                    pt, ptb = next_pst()
                    P.group("pe", [lambda e, pt=pt, r=r, m=m, h_=h_: e.transpose(
                        pt[:, (r * 2 + m) * NS:(r * 2 + m + 1) * NS], hst[0:NS, h_, r, m * 128:(m + 1) * 128],
                        ident[0:NS, 0:NS]) for r in range(2) for m in range(2)],
                        reads=[hst_b[h_], ident_b], writes=[ptb])
                    P.op("act", lambda e, pt=pt, h_=h_: e.copy(hT[:, h_, :], pt[:, 0:4 * NS]),
                         reads=[ptb], writes=[hT_b[h_]])
                for m in range(2):
                    j = j0 + m
                    jj = jp + m
                    for h_ in range(2):
                        pv, pvb = pvs[h_]
                        ceng = "dve"
                        Uh, Ub, USh, USb, CVh, CVb = U[h_], U_b[h_], US[h_], US_b[h_], CV[h_], CV_b[h_]

                        def evac_up(a, ab, ti, t0, tn, Uh=Uh, Ub=Ub, USh=USh, USb=USb):
                            eng = evac_eng()
                            if ti < 2:
                                P.op(eng, copy_fn(eng, Uh[:, 2 + t0:2 + t0 + tn], a[:, 0:tn]), reads=[ab], writes=[Ub])
                            elif ti == 2:
                                P.op(eng, copy_fn(eng, USh[:, 0:NS], a[:, 0:NS]), reads=[ab], writes=[USb])
                            else:
                                P.op(eng, copy_fn(eng, Uh[:, 0:2], a[:, 0:2]), reads=[ab], writes=[Ub])
                        gemm_fm(pv, pvb, 16, [(m * 128, 128, 0)], rhs_f, xn_b + [xnh_b], evac_up, tiles=FT)
                        wi = [cw[:, t * 88 + h_ * 44 + j:t * 88 + h_ * 44 + j + 1] for t in range(3)]
                        P.op(ceng, lambda e, CVh=CVh, Uh=Uh, w=wi[0]: e.tensor_scalar_mul(CVh[:, 0:TP], Uh[:, 0:TP], w),
                             reads=[Ub, cw_b], writes=[CVb])
                        P.op(ceng, lambda e, CVh=CVh, Uh=Uh, w=wi[1]: e.scalar_tensor_tensor(
                            CVh[:, 0:TP], Uh[:, 1:TP + 1], w, CVh[:, 0:TP], ALU.mult, ALU.add),
                            reads=[Ub, cw_b], writes=[CVb])
                        P.op(ceng, lambda e, CVh=CVh, Uh=Uh, w=wi[2]: e.scalar_tensor_tensor(
                            CVh[:, 0:TP], Uh[:, 2:TP + 2], w, CVh[:, 0:TP], ALU.mult, ALU.add),
                            reads=[Ub, cw_b], writes=[CVb])
                        h0 = hT[:, h_, (0 * 2 + m) * NS:(0 * 2 + m + 1) * NS]
                        h1 = hT[:, h_, (1 * 2 + m) * NS:(1 * 2 + m + 1) * NS]
                        P.op(ceng, lambda e, CVh=CVh, h0=h0, w=wi[0]: e.tensor_scalar_mul(CVh[:, TP:NT], h0, w),
                             reads=[hT_b[h_], cw_b], writes=[CVb])
                        P.op(ceng, lambda e, CVh=CVh, h1=h1, w=wi[1]: e.scalar_tensor_tensor(
                            CVh[:, TP:NT], h1, w, CVh[:, TP:NT], ALU.mult, ALU.add),
                            reads=[hT_b[h_], cw_b], writes=[CVb])
                        P.op(ceng, lambda e, CVh=CVh, USh=USh, w=wi[2]: e.scalar_tensor_tensor(
                            CVh[:, TP:NT], USh[:, 0:NS], w, CVh[:, TP:NT], ALU.mult, ALU.add),
                            reads=[USb, cw_b], writes=[CVb])
                        P.op(ceng, lambda e, USh=USh, Uh=Uh: e.tensor_copy(USh[:, NS:18], Uh[:, TP:TP + 2]),
                             reads=[Ub], writes=[USb])
                        pt2, pt2b = next_pst()
                        P.group("pe", [lambda e, pt2=pt2, USh=USh: e.transpose(pt2[0:18, 0:128], USh[:, :], ident[:, :])],
                                reads=[USb, ident_b], writes=[pt2b])
                        P.op("act", lambda e, pt2=pt2, m=m, h_=h_: e.copy(ost[0:18, h_, m * 128:(m + 1) * 128],
                                                                        pt2[0:18, 0:128]),
                             reads=[pt2b], writes=[ost_b[h_]])
                        if m == 1:
                            colbase = h_ * DFF + j0 * 128
                            P.dma("sp", lambda e, colbase=colbase, h_=h_: e.dma_start(
                                out=O[o_s][:, 1, colbase:colbase + 256], in_=ost[0:NS, h_, :]),
                                reads=[ost_b[h_]], writes=[OB[o_s]])
                            P.dma("sp", lambda e, colbase=colbase, h_=h_: e.dma_start(
                                out=O[o_p][:, colbase:colbase + 256], in_=ost[NS:18, h_, :]),
                                reads=[ost_b[h_]], writes=[OB[o_p]])
                    P.op("act", lambda e: e.activation(out=CV[0][:, :], in_=CV[0][:, :], func=AF.Silu),
                         reads=[CV_b[0]], writes=[CV_b[0]])
                    P.op("dve", lambda e, jj=jj: e.tensor_tensor(hid[:, jj, :], CV[0][:, :], CV[1][:, :], ALU.mult),
                         reads=[CV_b[0], CV_b[1]], writes=[hid_b[jj]])
            for pi in range(8):
                pv, pvb = load_panel_rows(Wdn, j_base * 128, gsz, pi * 256, 256)
                for m in range(2):
                    c = pi * 2 + m

                    def evac_res(a, ab, ti, t0, tn, c=c):
                        P.op("dve", lambda e, a=a, t0=t0, tn=tn, c=c: e.tensor_tensor(
                            xT[:, c, t0:t0 + tn], a[:, 0:tn], xT[:, c, t0:t0 + tn], ALU.add),
                            reads=[ab], writes=[xT_b[c]])
                    gemm_fm(pv, pvb, gsz, [(m * 128, 128, 0)],
                            lambda kc, t0, tn: hid[:, kc, t0:t0 + tn], hid_b[0:gsz], evac_res)
            j_base += gsz

    ffn(0)
    if KSTOP == "x1":
        maybe_stop("x1", xT[:, :, :], xT_b, [128, 16, NT], F32)

    L1 = {}

    def layer1_inproj():
        P.barrier()
        ar["off"] = 0
        exchange_x()
        rmsnorm("l1_norm_mix")
        wcol, wcol_b = load_cols(I["l1_norm_mix"].rearrange("(c p) -> c p", p=128), 16, "wch_l1")
        rmsnorm_halo("l1_norm_mix", wcol, wcol_b)
        W1 = I["l1_w_in"]
        mark1 = ar["off"]
        gw, gw_b = load_cols(I["l1_gdn_conv"].rearrange("j (c p) -> (j c) p", p=128), 144, "gw")
        sw, sw_b = load_cols(I["l1_sconv_w"].rearrange("j (c p) -> (j c) p", p=128), 12, "sw")
        P.dma("sp", lambda e: e.dma_start(out=O["gconv_s"][:, 0:2, :], in_=I["st_gconv"][:, 1:3, :]),
              writes=[OB["gconv_s"]])
        P.dma("sp", lambda e: e.dma_start(out=O["sconv_s"][:, 0, :], in_=I["st_sconv"][:, 1, :]),
              writes=[OB["sconv_s"]])
        U = aalloc([128, 3 + TP], F32)
        U_b = Buf("U1")
        US = aalloc([128, 2, 19], F32)
        US_b = [Buf("US1a"), Buf("US1b")]
        ost = aalloc([19, 256], F32)
        ost_b = Buf("ost1")
        hst = aalloc([NS, 3, 256], F32)
        hst_b = Buf("hst1")
        hT = aalloc([128, 6 * NS], F32)
        hT_b = Buf("hT1")
        CVx = aalloc([128, NT], F32)
        CV_b = Buf("cv1")
        FT = TILES + [(-1, 3)]

        def rhs_f(kc, t0, tn):
            if t0 < 0:
                return xnh[:, kc, 0:3]
            return xn[:, kc, t0:t0 + tn]

        for pi in range(18):
            colbase = pi * 256
            pv, pvb = load_panel(W1, 16, colbase, 256)
            P.dma("sp", lambda e, colbase=colbase: e.dma_start(out=hst[:, :, :], in_=I["st_gconv"][:, :, colbase:colbase + 256]),
                  writes=[hst_b])
            pt, ptb = next_pst()
            P.group("pe", [lambda e, pt=pt, r=r, m=m: e.transpose(
                pt[:, (r * 2 + m) * NS:(r * 2 + m + 1) * NS], hst[0:NS, r, m * 128:(m + 1) * 128],
                ident[0:NS, 0:NS]) for r in range(3) for m in range(2)], reads=[hst_b, ident_b], writes=[ptb])
            P.op("act", lambda e, pt=pt: e.copy(hT[:, :], pt[:, 0:6 * NS]), reads=[ptb], writes=[hT_b])
            for m in range(2):
                ci = pi * 2 + m
                s_ = m

                def evac_up(a, ab, ti, t0, tn, s_=s_):
                    eng = evac_eng()
                    if ti < 2:
                        P.op(eng, copy_fn(eng, U[:, 3 + t0:3 + t0 + tn], a[:, 0:tn]), reads=[ab], writes=[U_b])
                    elif ti == 2:
                        P.op(eng, copy_fn(eng, US[:, s_, 0:NS], a[:, 0:NS]), reads=[ab], writes=[US_b[s_]])
                    else:
                        P.op(eng, copy_fn(eng, U[:, 0:3], a[:, 0:3]), reads=[ab], writes=[U_b])
                gemm_fm(pv, pvb, 16, [(m * 128, 128, 0)], rhs_f, xn_b + [xnh_b], evac_up, tiles=FT)
                P.op("pool", lambda e, s_=s_: e.tensor_copy(US[:, s_, NS:19], U[:, TP:TP + 3]),
                     reads=[U_b], writes=[US_b[s_]])
                pt2, pt2b = next_pst()
                P.group("pe", [lambda e, pt2=pt2, s_=s_: e.transpose(pt2[0:19, 0:128], US[:, s_, :], ident[:, :])],
                        reads=[US_b[s_], ident_b], writes=[pt2b])
                P.op("act", lambda e, pt2=pt2, m=m: e.copy(ost[0:19, m * 128:(m + 1) * 128], pt2[0:19, 0:128]),
                     reads=[pt2b], writes=[ost_b])
            P.dma("sp", lambda e, colbase=colbase: e.dma_start(
                out=O["gconv_s"][:, 2, colbase:colbase + 256], in_=ost[0:NS, :]), reads=[ost_b], writes=[OB["gconv_s"]])
            P.dma("sp", lambda e, colbase=colbase: e.dma_start(
                out=O["gconv_p"][:, colbase:colbase + 256], in_=ost[NS:19, :]), reads=[ost_b], writes=[OB["gconv_p"]])

        P.barrier()
        ar["off"] = mark1
        CVx = aalloc([128, NT], F32)
        CV_b = Buf("cv1b")
        ysc = aalloc([128, 4, NT], BF16)
        ysc_b = [Buf("ysc%d" % c_) for c_ in range(4)]
        L1["mark_ysc"] = ar["off"]
        o2 = 4608 + 1536 + 24
        sB = aalloc([128, 4, NT], BF16)
        sB_b = Buf("sB")
        scu = aalloc([128, 4, 2 + TP], F32)
        scu_b = Buf("scu")
        scS = aalloc([128, 4, NS], F32)
        scS_b = Buf("scS")
        for part in range(3):
            for half in range(2):
                colbase = o2 + part * 512 + half * 256
                pv, pvb = load_panel(W1, 16, colbase, 256)
                for m in range(2):
                    cc = half * 2 + m

                    def evac_sc(a, ab, ti, t0, tn, part=part, cc=cc):
                        if part == 0:
                            if ti < 3:
                                eng = evac_eng()
                                P.op(eng, copy_fn(eng, sB[:, cc, t0:t0 + tn], a[:, 0:tn]), reads=[ab], writes=[sB_b])
                            return
                        if ti < 2:
                            dst = scu[:, cc, 2 + t0:2 + t0 + tn]
                            src = a[:, 0:tn]
                            wb = scu_b
                        elif ti == 2:
                            dst = scS[:, cc, :]
                            src = a[:, 0:NS]
                            wb = scS_b
                        else:
                            dst = scu[:, cc, 0:2]
                            src = a[:, 1:3]
                            wb = scu_b
                        if part == 1:
                            eng = evac_eng()
                            P.op(eng, copy_fn(eng, dst, src), reads=[ab], writes=[wb])
                        else:
                            P.op("dve", lambda e, dst=dst, src=src: e.tensor_tensor(dst, src, dst, ALU.mult),
                                 reads=[ab], writes=[wb])
                    gemm_fm(pv, pvb, 16, [(m * 128, 128, 0)], rhs_f, xn_b + [xnh_b], evac_sc, tiles=FT)
        hs2 = aalloc([NS, 2, 512], F32)
        hs2_b = Buf("hs2")
        P.dma("sp", lambda e: e.dma_start(out=hs2[:, :, :], in_=I["st_sconv"][:, :, :]), writes=[hs2_b])
        hT2 = aalloc([128, 2, 4, NS], F32)
        hT2_b = Buf("hT2")
        US2 = aalloc([128, 4, 18], F32)
        US2_b = Buf("US2")
        ost2 = aalloc([18, 512], F32)
        ost2_b = Buf("ost2")
        pt, ptb = next_pst()
        P.group("pe", [lambda e, pt=pt, r=r, c=c: e.transpose(
            pt[:, (r * 4 + c) * NS:(r * 4 + c + 1) * NS], hs2[0:NS, r, c * 128:(c + 1) * 128], ident[0:NS, 0:NS])
            for r in range(2) for c in range(4)], reads=[hs2_b, ident_b], writes=[ptb])
        P.op("act", lambda e, pt=pt: e.copy(hT2.rearrange("p r c s -> p (r c s)"), pt[:, 0:8 * NS]),
             reads=[ptb], writes=[hT2_b])
        P.op("dve", lambda e: e.tensor_copy(US2[:, :, 0:NS], scS[:, :, :]), reads=[scS_b], writes=[US2_b])
        P.op("dve", lambda e: e.tensor_copy(US2[:, :, NS:18], scu[:, :, TP:TP + 2]), reads=[scu_b], writes=[US2_b])
        pt2, pt2b = next_pst()
        P.group("pe", [lambda e, pt2=pt2, c=c: e.transpose(pt2[0:18, c * 128:(c + 1) * 128], US2[:, c, :], ident[:, :])
                       for c in range(4)], reads=[US2_b, ident_b], writes=[pt2b])
        P.op("act", lambda e, pt2=pt2: e.copy(ost2[0:18, :], pt2[0:18, 0:512]), reads=[pt2b], writes=[ost2_b])
        P.dma("sp", lambda e: e.dma_start(out=O["sconv_s"][:, 1, :], in_=ost2[0:NS, :]), reads=[ost2_b],
              writes=[OB["sconv_s"]])
        P.dma("sp", lambda e: e.dma_start(out=O["sconv_p"][:, :], in_=ost2[NS:18, :]), reads=[ost2_b],
              writes=[OB["sconv_p"]])
        for c in range(4):
            wi = [sw[:, t * 4 + c:t * 4 + c + 1] for t in range(3)]
            P.op("dve", lambda e, c=c, w=wi[0]: e.tensor_scalar_mul(CVx[:, 0:TP], scu[:, c, 0:TP], w),
                 reads=[scu_b, sw_b], writes=[CV_b])
            P.op("dve", lambda e, c=c, w=wi[1]: e.scalar_tensor_tensor(
                CVx[:, 0:TP], scu[:, c, 1:TP + 1], w, CVx[:, 0:TP], ALU.mult, ALU.add),
                reads=[scu_b, sw_b], writes=[CV_b])
            P.op("dve", lambda e, c=c, w=wi[2]: e.scalar_tensor_tensor(
                CVx[:, 0:TP], scu[:, c, 2:TP + 2], w, CVx[:, 0:TP], ALU.mult, ALU.add),
                reads=[scu_b, sw_b], writes=[CV_b])
            P.op("dve", lambda e, c=c, w=wi[0]: e.tensor_scalar_mul(CVx[:, TP:NT], hT2[:, 0, c, :], w),
                 reads=[hT2_b, sw_b], writes=[CV_b])
            P.op("dve", lambda e, c=c, w=wi[1]: e.scalar_tensor_tensor(
                CVx[:, TP:NT], hT2[:, 1, c, :], w, CVx[:, TP:NT], ALU.mult, ALU.add),
                reads=[hT2_b, sw_b], writes=[CV_b])
            P.op("dve", lambda e, c=c, w=wi[2]: e.scalar_tensor_tensor(
                CVx[:, TP:NT], scS[:, c, :], w, CVx[:, TP:NT], ALU.mult, ALU.add),
                reads=[scS_b, sw_b], writes=[CV_b])
            P.op("dve", lambda e, c=c: e.tensor_tensor(ysc[:, c, :], CVx[:, :], sB[:, c, :], ALU.mult),
                 reads=[CV_b, sB_b], writes=[ysc_b[c]])
        L1["gw"], L1["gw_b"], L1["mark1"], L1["FT"] = gw, gw_b, mark1, FT
        L1["ysc"], L1["ysc_b"] = ysc, ysc_b
        L1["mark2"] = ar["off"]

    layer1_inproj()

    def layer1_gdn():
        gw, gw_b, FT = L1["gw"], L1["gw_b"], L1["FT"]
        ysc, ysc_b = L1["ysc"], L1["ysc_b"]
        W1 = I["l1_w_in"]
        P.barrier()
        ar["off"] = L1["mark_ysc"]
        yh = aalloc([128, NT], BF16)
        yh_b = Buf("yh")

        def rhs3(kc, t0, tn):
            if t0 < 0:
                return xnh[:, kc, 0:3]
            return xn[:, kc, t0:t0 + tn]

        onesf = aalloc([128, 128], F32)
        gm = aalloc([128, 2, 128], F32)
        selfhot = aalloc([128, 8], F32)
        gnw, gnw_b = load_cols(I["l1_gdn_norm"].rearrange("(c p) -> c p", p=128), 1, "gnw")
        cg_b = Buf("cg")
        P.op("dve", lambda e: e.memset(onesf, 1.0), writes=[cg_b])
        P.dma("sp", lambda e: e.dma_start(out=gm.rearrange("p a b -> p (a b)"), in_=I["c_gmasks"][:, :]), writes=[cg_b])
        P.dma("sp", lambda e: e.dma_start(out=selfhot, in_=I["c_selfhot"][:, :]), writes=[cg_b])
        hp = aalloc([1, 3, 12], F32)
        hp_b = Buf("hp")
        P.dma("sp", lambda e: e.dma_start(out=hp[:, 0, :], in_=I["l1_gdn_dt_bias"].rearrange("(o n) -> o n", o=1)),
              writes=[hp_b])
        P.dma("sp", lambda e: e.dma_start(out=hp[:, 1, :], in_=I["l1_gdn_A_log"].rearrange("(o n) -> o n", o=1)),
              writes=[hp_b])
        P.op("act", lambda e: e.activation(out=hp[:, 1, :], in_=hp[:, 1, :], func=AF.Exp), reads=[hp_b], writes=[hp_b])
        P.op("dve", lambda e: e.tensor_scalar_mul(hp[:, 1, :], hp[:, 1, :], -1.0), reads=[hp_b], writes=[hp_b])
        bap = aalloc([128, 16, 24], BF16)
        bap_b = Buf("bap")
        P.dma("pool", lambda e: e.dma_start(out=bap, in_=W1[:, 6144:6168].rearrange("(kc p) n -> p kc n", p=128)),
              writes=[bap_b])

        U = aalloc([128, 3 + TP], F32)
        U_b = Buf("gU")
        USn = aalloc([128, NS], F32)
        USn_b = Buf("gUSn")
        hst3 = aalloc([NS, 3, 128], F32)
        hst3_b = Buf("hst3")
        hT3 = aalloc([128, 3, NS], F32)
        hT3_b = Buf("hT3")
        CV = aalloc([128, NT], F32)
        CV_b = Buf("gCV")
        qkv = aalloc([128, 3, NT], BF16)
        qkv_b = [Buf("gq"), Buf("gk"), Buf("gv")]
        smp = aalloc([128, 3, NS], F32)
        smp_b = Buf("smp")
        zgs = aalloc([128, NT], BF16)
        zgs_b = Buf("zgs")
        rows = aalloc([1, 4, NT], F32)
        rmisc = aalloc([1, 2, NS], F32)
        rows_b = Buf("rows")
        colsT = aalloc([128, 8, 4], F32)
        colsT_b = Buf("colsT")
        Gb = aalloc([128, 16], F32)
        Gb_b = Buf("Gb")
        sscal = aalloc([128, 2, NS], F32)
        sscal_b = Buf("sscal")
        gkT = aalloc([128, 128], BF16)
        qdT = aalloc([128, 128], BF16)
        kd = aalloc([128, 128], BF16)
        bvx = aalloc([128, 256], F32)
        Rsb = aalloc([128, 128], F32)
        decT = aalloc([128, 128], F32)
        decL = aalloc([128, 128], F32)
        intraT = aalloc([128, 128], BF16)
        Lm = aalloc([128, 2, 128], F32)
        Nm = aalloc([128, 2, 128], F32)
        XT = aalloc([128, 128], F32)
        TT = aalloc([128, 128], BF16)
        prep_b = Buf("prep")
        br = aalloc([128, 256], BF16)
        br_b = Buf("br")
        vn = aalloc([128, 256], BF16)
        vn_b = Buf("vn")
        S = aalloc([128, 256], F32)
        Sb = aalloc([128, 256], BF16)
        S_b, Sb_b = Buf("S"), Buf("Sb")
        o0T = aalloc([128, NT], F32)
        o0T_b = Buf("o0T")
        OphT = aalloc([128, TP], BF16)
        OphT_b = Buf("OphT")
        ABt = aalloc([128, 256], F32)
        ABt_b = Buf("ABt")
        ABr = aalloc([128, 2, 256], F32)
        ABr_b = [Buf("ABr0"), Buf("ABr1")]
        Srun = aalloc([128, 128], F32)
        Sin = aalloc([128, 128], F32)
        Sinb = aalloc([128, 128], BF16)
        scan_b = Buf("scan")
        S0 = aalloc([128, 2, 128], F32)
        S0_b = [Buf("S0a"), Buf("S0b")]
        S1 = aalloc([128, 2, 128], F32)
        S1_b = [Buf("S1a"), Buf("S1b")]
        dg = aalloc([128, 128], F32)
        dl = aalloc([128, 2], F32)
        smisc_b = Buf("smisc")
        P.op("dve", lambda e: e.memset(bvx, 0.0), writes=[prep_b])
        sqv = sq[:, :, :].rearrange("p a b -> p (a b)")[:, 0:NT]
        sqv_b = sq_b[0]

        def row_bcast(dst_cols, src_row, ncols, pt):
            return lambda e: e.matmul(pt[:, dst_cols:dst_cols + ncols], onesf[0:1, 0:128], src_row, start=True, stop=True)

        for h in range(12):
            for wi_, ci in enumerate((h, 12 + h, 24 + h)):
                colbase = ci * 128
                pv, pvb = load_panel(W1, 16, colbase, 128)
                P.dma("sp", lambda e, colbase=colbase: e.dma_start(out=hst3, in_=I["st_gconv"][:, :, colbase:colbase + 128]),
                      writes=[hst3_b])
                pt, ptb = next_pst()
                P.group("pe", [lambda e, pt=pt, r=r: e.transpose(pt[:, r * NS:(r + 1) * NS], hst3[0:NS, r, :],
                                                                 ident[0:NS, 0:NS]) for r in range(3)],
                        reads=[hst3_b, ident_b], writes=[ptb])
                P.op("act", lambda e, pt=pt: e.copy(hT3.rearrange("p r s -> p (r s)"), pt[:, 0:3 * NS]),
                     reads=[ptb], writes=[hT3_b])

                def evac_up(a, ab, ti, t0, tn):
                    eng = evac_eng()
                    if ti < 2:
                        P.op(eng, copy_fn(eng, U[:, 3 + t0:3 + t0 + tn], a[:, 0:tn]), reads=[ab], writes=[U_b])
                    elif ti == 2:
                        P.op(eng, copy_fn(eng, USn[:, :], a[:, 0:NS]), reads=[ab], writes=[USn_b])
                    else:
                        P.op(eng, copy_fn(eng, U[:, 0:3], a[:, 0:3]), reads=[ab], writes=[U_b])
                gemm_fm(pv, pvb, 16, [(0, 128, 0)], rhs3, xn_b + [xnh_b], evac_up, tiles=FT)
                wt = [gw[:, t * 36 + ci:t * 36 + ci + 1] for t in range(4)]
                P.op("dve", lambda e, w=wt[0]: e.tensor_scalar_mul(CV[:, 0:TP], U[:, 0:TP], w),
                     reads=[U_b, gw_b], writes=[CV_b])
                for t in range(1, 4):
                    P.op("dve", lambda e, w=wt[t], t=t: e.scalar_tensor_tensor(
                        CV[:, 0:TP], U[:, t:t + TP], w, CV[:, 0:TP], ALU.mult, ALU.add),
                        reads=[U_b, gw_b], writes=[CV_b])
                P.op("dve", lambda e, w=wt[0]: e.tensor_scalar_mul(CV[:, TP:NT], hT3[:, 0, :], w),
                     reads=[hT3_b, gw_b], writes=[CV_b])
                for t in range(1, 3):
                    P.op("dve", lambda e, w=wt[t], t=t: e.scalar_tensor_tensor(
                        CV[:, TP:NT], hT3[:, t, :], w, CV[:, TP:NT], ALU.mult, ALU.add),
                        reads=[hT3_b, gw_b], writes=[CV_b])
                P.op("dve", lambda e, w=wt[3]: e.scalar_tensor_tensor(
                    CV[:, TP:NT], USn[:, :], w, CV[:, TP:NT], ALU.mult, ALU.add),
                    reads=[USn_b, gw_b], writes=[CV_b])
                P.op("act", lambda e: e.activation(out=CV[:, :], in_=CV[:, :], func=AF.Silu), reads=[CV_b], writes=[CV_b])
                if wi_ < 2:
                    P.op("act", lambda e: e.activation(out=sqv, in_=CV[:, :], func=AF.Square), reads=[CV_b], writes=[sqv_b])
                    accs = [next_acc() for _ in TILES]
                    for ti, (t0, tn) in enumerate(TILES):
                        a_, ab = accs[ti]
                        P.group("pe", [lambda e, a_=a_, t0=t0, tn=tn: e.matmul(a_[:, 0:tn], ones_bf[:], sqv[:, t0:t0 + tn],
                                                                              start=True, stop=True)],
                                reads=[sqv_b, const_b], writes=[ab])
                        P.op("act", lambda e, a_=a_, t0=t0, tn=tn: e.activation(
                            out=rstd[:, t0:t0 + tn], in_=a_[:, 0:tn], func=AF.Sqrt, bias=epsc[:, 0:1], scale=1.0),
                            reads=[ab, const_b], writes=[rstd_b])
                    P.op("dve", lambda e: e.reciprocal(rstd[:], rstd[:]), reads=[rstd_b], writes=[rstd_b])
                    sc_ = (128.0 ** -0.5) if wi_ == 0 else 1.0
                    P.op("dve", lambda e, sc_=sc_: e.scalar_tensor_tensor(
                        CV[:, :], CV[:, :], sc_, rstd[:], ALU.mult, ALU.mult), reads=[CV_b, rstd_b], writes=[CV_b])
                P.op("act", lambda e, wi_=wi_: e.copy(qkv[:, wi_, :], CV[:, :]), reads=[CV_b], writes=[qkv_b[wi_]])
                P.op("dve", lambda e, wi_=wi_: e.tensor_copy(smp[:, wi_, :], CV[:, TP:NT]), reads=[CV_b], writes=[smp_b])
            pv, pvb = load_panel(W1, 16, 4608 + h * 128, 128)

            def evac_zg(a, ab, ti, t0, tn):
                P.op("act", lambda e, a=a, t0=t0, tn=tn: e.activation(out=zgs[:, t0:t0 + tn], in_=a[:, 0:tn], func=AF.Silu),
                     reads=[ab], writes=[zgs_b])
            gemm_fm(pv, pvb, 16, [(0, 128, 0)], xn_rhs, xn_b, evac_zg)

            for qi, col in ((0, h), (1, 12 + h)):
                for ti, (t0, tn) in enumerate(TILES):
                    a_, ab = next_acc()
                    P.group("pe", [lambda e, a_=a_, kc=kc, col=col, t0=t0, tn=tn: e.matmul(
                        a_[0:1, 0:tn], bap[:, kc, col:col + 1], xn[:, kc, t0:t0 + tn], start=(kc == 0), stop=(kc == 15))
                        for kc in range(16)], reads=[bap_b] + xn_b, writes=[ab])
                    P.op("dve", lambda e, a_=a_, qi=qi, t0=t0, tn=tn: e.tensor_copy(rows[:, qi, t0:t0 + tn], a_[0:1, 0:tn]),
                         reads=[ab], writes=[rows_b])
            P.op("act", lambda e: e.activation(out=rows[:, 0, :], in_=rows[:, 0, :], func=AF.Sigmoid),
                 reads=[rows_b], writes=[rows_b])
            P.op("act", lambda e, h=h: e.activation(out=rows[:, 1, :], in_=rows[:, 1, :], func=AF.Exp, bias=hp[:, 0, h:h + 1]),
                 reads=[rows_b, hp_b], writes=[rows_b])
            P.op("act", lambda e: e.activation(out=rows[:, 1, :], in_=rows[:, 1, :], func=AF.Ln, bias=onesf[0:1, 0:1]),
                 reads=[rows_b, cg_b], writes=[rows_b])
            P.op("dve", lambda e, h=h: e.tensor_scalar_mul(rows[:, 1, :], rows[:, 1, :], hp[:, 1, h:h + 1]),
                 reads=[rows_b, hp_b], writes=[rows_b])
            P.op("act", lambda e: e.activation(out=rmisc[:, 0, :], in_=rows[:, 1, TP:NT], func=AF.Exp),
                 reads=[rows_b], writes=[rows_b])
            src_i = 1
            for st_i, sh in enumerate((1, 2, 4, 8, 16, 32)):
                dst_i = 2 if src_i == 1 else 1
                sv = rows[:, src_i, 0:TP].rearrange("o (c t) -> o c t", t=64)
                dv_ = rows[:, dst_i, 0:TP].rearrange("o (c t) -> o c t", t=64)
                P.op("dve", lambda e, sv=sv, dv_=dv_, sh=sh: e.tensor_copy(dv_[:, :, 0:sh], sv[:, :, 0:sh]),
                     reads=[rows_b], writes=[rows_b])
                P.op("dve", lambda e, sv=sv, dv_=dv_, sh=sh: e.tensor_tensor(dv_[:, :, sh:64], sv[:, :, sh:64],
                                                                           sv[:, :, 0:64 - sh], ALU.add),
                     reads=[rows_b], writes=[rows_b])
                src_i = dst_i
            P.op("dve", lambda e: e.tensor_copy(rows[:, 2, 0:TP], rows[:, 1, 0:TP]), reads=[rows_b], writes=[rows_b])
            gcv = rows[:, 2, 0:TP].rearrange("o (c t) -> o c t", t=64)
            P.op("dve", lambda e: e.tensor_tensor(rows[:, 3, 0:TP].rearrange("o (c t) -> o c t", t=64), gcv,
                                                  gcv[:, :, 63:64].broadcast_to([1, 16, 64]), ALU.subtract),
                 reads=[rows_b], writes=[rows_b])
            P.op("act", lambda e: e.activation(out=rmisc[:, 1, :], in_=gcv[:, :, 63], func=AF.Exp),
                 reads=[rows_b], writes=[rows_b])
            pt, ptb = next_pst()
            P.group("pe", [row_bcast(0, rmisc[:, 1, :], 16, pt), row_bcast(16, rmisc[:, 0, :], NS, pt),
                           row_bcast(32, rows[:, 0, TP:NT], NS, pt)], reads=[rows_b, cg_b], writes=[ptb])
            P.op("dve", lambda e, pt=pt: e.tensor_copy(Gb[:, :], pt[:, 0:16]), reads=[ptb], writes=[Gb_b])
            P.op("dve", lambda e, pt=pt: e.tensor_copy(sscal.rearrange("p a b -> p (a b)"), pt[:, 16:48]),
                 reads=[ptb], writes=[sscal_b])
            pt, ptb = next_pst()
            fns = []
            for blk in range(8):
                for qi, ri in enumerate((0, 3, 2)):
                    fns.append(lambda e, pt=pt, blk=blk, qi=qi, ri=ri: e.matmul(
                        pt[:, blk * 4 + qi:blk * 4 + qi + 1], rows[:, ri, blk * 128:(blk + 1) * 128], onesf[0:1, 0:1],
                        start=True, stop=True))
            P.group("pe", fns, reads=[rows_b, cg_b], writes=[ptb])
            P.op("dve", lambda e, pt=pt: e.tensor_copy(colsT[:, :, 0:3], pt[:, 0:32].rearrange("p (b q) -> p b q", q=4)[:, :, 0:3]),
                 reads=[ptb], writes=[colsT_b])
            P.op("dve", lambda e: e.tensor_scalar_mul(colsT[:, :, 3], colsT[:, :, 0], -1.0), reads=[colsT_b], writes=[colsT_b])
            P.op("act", lambda e: e.activation(out=colsT[:, :, 1], in_=colsT[:, :, 1], func=AF.Exp, scale=-1.0),
                 reads=[colsT_b], writes=[colsT_b])

            P.op("dve", lambda e: e.memset(S[:, 0:128], 0.0), writes=[S_b])
            P.op("dve", lambda e: e.tensor_copy(S[:, 128:256], ident[:, :]), reads=[ident_b], writes=[S_b])
            P.op("act", lambda e: e.copy(Sb[:, :], S[:, :]), reads=[S_b], writes=[Sb_b])

            for blk in range(8):
                tk = slice(blk * 128, (blk + 1) * 128)
                bcol, kcol, gcol, nbcol = (colsT[:, blk, i:i + 1] for i in range(4))
                pt, ptb = next_pst()
                P.group("pe", [row_bcast(128, rows[:, 2, tk], 128, pt)], reads=[rows_b, cg_b], writes=[ptb])
                P.op("act", lambda e, pt=pt: e.copy(Rsb[:, :], pt[:, 128:256]), reads=[ptb], writes=[prep_b])
                P.op("act", lambda e: e.activation(out=decT[:, :], in_=Rsb[:, :], func=AF.Exp), reads=[prep_b], writes=[prep_b])
                P.op("dve", lambda e, tk=tk: e.tensor_tensor(gkT[:, :], qkv[:, 1, tk], decT[:, :], ALU.mult),
                     reads=[prep_b, qkv_b[1]], writes=[prep_b])
                P.op("dve", lambda e, tk=tk: e.tensor_tensor(qdT[:, :], qkv[:, 0, tk], decT[:, :], ALU.mult),
                     reads=[prep_b, qkv_b[0]], writes=[prep_b])
                ptk, ptkb = next_pst()
                ptkv = ptk[:, :].bitcast(BF16)
                P.group("pe", [lambda e, ptkv=ptkv, tk=tk: e.transpose(ptkv[:, 0:128], qkv[:, 1, tk], identb[:, :]),
                               lambda e, ptkv=ptkv, tk=tk: e.transpose(ptkv[:, 128:256], qkv[:, 2, tk], identb[:, :])],
                        reads=[qkv_b[1], qkv_b[2], const_b], writes=[ptkb])
                P.op("dve", lambda e, ptkv=ptkv, kcol=kcol: e.tensor_scalar_mul(kd[:, :], ptkv[:, 0:128], kcol),
                     reads=[ptkb, colsT_b], writes=[prep_b])
                P.op("dve", lambda e, ptkv=ptkv, bcol=bcol: e.tensor_scalar_mul(bvx[:, 0:128], ptkv[:, 128:256], bcol),
                     reads=[ptkb, colsT_b], writes=[prep_b])
                P.op("dve", lambda e, gcol=gcol: e.tensor_scalar(decT[:, :], Rsb[:, :], gcol, 0.0, ALU.subtract, ALU.min),
                     reads=[prep_b, colsT_b], writes=[prep_b])
                P.op("act", lambda e: e.activation(out=decT[:, :], in_=decT[:, :], func=AF.Exp), reads=[prep_b], writes=[prep_b])
                P.op("dve", lambda e: e.tensor_tensor(decT[:, :], decT[:, :], gm[:, 0, :], ALU.mult),
                     reads=[prep_b, cg_b], writes=[prep_b])
                P.op("dve", lambda e, gcol=gcol: e.tensor_scalar(decL[:, :], Rsb[:, :], gcol, 0.0, ALU.subtract, ALU.max),
                     reads=[prep_b, colsT_b], writes=[prep_b])
                P.op("act", lambda e: e.activation(out=decL[:, :], in_=decL[:, :], func=AF.Exp, scale=-1.0),
                     reads=[prep_b], writes=[prep_b])
                P.op("dve", lambda e: e.tensor_tensor(decL[:, :], decL[:, :], gm[:, 1, :], ALU.mult),
                     reads=[prep_b, cg_b], writes=[prep_b])
                a_, ab = next_acc()
                P.group("pe", [lambda e, a_=a_, tk=tk: e.matmul(a_[:, 0:128], qkv[:, 1, tk], qkv[:, 0, tk], start=True, stop=True),
                               lambda e, a_=a_, tk=tk: e.matmul(a_[:, 128:256], qkv[:, 1, tk], qkv[:, 1, tk], start=True, stop=True)],
                        reads=[qkv_b[0], qkv_b[1]], writes=[ab])
                P.op("dve", lambda e, a_=a_: e.tensor_tensor(intraT[:, :], a_[:, 0:128], decT[:, :], ALU.mult),
                     reads=[ab, prep_b], writes=[prep_b])
                P.op("dve", lambda e, a_=a_: e.tensor_tensor(Lm[:, 0, :], a_[:, 128:256], decL[:, :], ALU.mult),
                     reads=[ab, prep_b], writes=[prep_b])
                P.op("dve", lambda e, bcol=bcol: e.tensor_scalar_mul(Lm[:, 0, :], Lm[:, 0, :], bcol),
                     reads=[prep_b, colsT_b], writes=[prep_b])
                pt, ptb = next_pst()
                P.group("pe", [lambda e, pt=pt: e.transpose(pt[:, 0:128], Lm[:, 0, :], ident[:, :])],
                        reads=[prep_b, ident_b], writes=[ptb])
                P.op("act", lambda e, pt=pt: e.copy(Nm[:, 0, :], pt[:, 0:128]), reads=[ptb], writes=[prep_b])
                P.op("dve", lambda e, pt=pt: e.tensor_tensor(XT[:, :], ident[:, :], pt[:, 0:128], ALU.subtract),
                     reads=[ptb, ident_b], writes=[prep_b])
                for lvl in range(5):
                    si, di = lvl % 2, (lvl + 1) % 2
                    a_, ab = next_acc()
                    P.group("pe", [lambda e, a_=a_, si=si: e.matmul(a_[:, 0:128], Nm[:, si, :], Lm[:, si, :], start=True, stop=True),
                                   lambda e, a_=a_, si=si: e.matmul(a_[:, 128:256], Lm[:, si, :], Nm[:, si, :], start=True, stop=True)],
                            reads=[prep_b], writes=[ab])
                    P.op("act", lambda e, a_=a_, di=di: e.copy(Lm[:, di, :], a_[:, 0:128]), reads=[ab], writes=[prep_b])
                    P.op("act", lambda e, a_=a_, di=di: e.copy(Nm[:, di, :], a_[:, 128:256]), reads=[ab], writes=[prep_b])
                    a2, a2b = next_acc()
                    P.group("pe", [lambda e, a2=a2, di=di: e.matmul(a2[:, 0:128], Lm[:, di, :], XT[:, :], start=True, stop=True)],
                            reads=[prep_b], writes=[a2b])
                    P.op("dve", lambda e, a2=a2: e.tensor_tensor(XT[:, :], XT[:, :], a2[:, 0:128], ALU.add),
                         reads=[a2b, prep_b], writes=[prep_b])
                P.op("act", lambda e: e.copy(TT[:, :], XT[:, :]), reads=[prep_b], writes=[prep_b])

                for cc in range(2):
                    p0 = 64 * cc
                    ch = blk * 2 + cc
                    ct = slice(blk * 128 + p0, blk * 128 + p0 + 64)
                    a1, a1b = next_acc()
                    P.group("pe", [lambda e, a1=a1, p0=p0: e.matmul(a1[p0:p0 + 64, 0:256], gkT[:, p0:p0 + 64], Sb[:, :],
                                                                    start=True, stop=True)],
                            reads=[prep_b, Sb_b], writes=[a1b])
                    P.op("dve", lambda e, a1=a1, p0=p0, blk=blk: e.scalar_tensor_tensor(
                        br[p0:p0 + 64, :], a1[p0:p0 + 64, 0:256], colsT[p0:p0 + 64, blk, 3:4], bvx[p0:p0 + 64, :],
                        ALU.mult, ALU.add), reads=[a1b, colsT_b, prep_b], writes=[br_b])
                    a2, a2b = next_acc()
                    P.group("pe", [lambda e, a2=a2, p0=p0: e.matmul(a2[p0:p0 + 64, 0:256], TT[p0:p0 + 64, p0:p0 + 64],
                                                                    br[p0:p0 + 64, :], start=True, stop=True)],
                            reads=[prep_b, br_b], writes=[a2b])
                    P.op("act", lambda e, a2=a2, p0=p0: e.copy(vn[p0:p0 + 64, :], a2[p0:p0 + 64, 0:256]),
                         reads=[a2b], writes=[vn_b])
                    a3, a3b = next_acc()
                    fns = []
                    for half in range(2):
                        cs = slice(half * 128, (half + 1) * 128)
                        fns.append(lambda e, a3=a3, cs=cs, half=half, p0=p0: e.matmul(
                            a3[:, half * 64:half * 64 + 64], Sb[:, cs], qdT[:, p0:p0 + 64], start=True, stop=False))
                        fns.append(lambda e, a3=a3, cs=cs, half=half, p0=p0: e.matmul(
                            a3[:, half * 64:half * 64 + 64], vn[p0:p0 + 64, cs], intraT[p0:p0 + 64, p0:p0 + 64],
                            start=False, stop=True))
                    P.group("pe", fns, reads=[Sb_b, prep_b, vn_b], writes=[a3b])
                    P.op("dve", lambda e, a3=a3, ct=ct: e.tensor_copy(o0T[:, ct], a3[:, 0:64]), reads=[a3b], writes=[o0T_b])
                    P.op("dve", lambda e, a3=a3, ct=ct: e.tensor_copy(OphT[:, ct], a3[:, 64:128]), reads=[a3b], writes=[OphT_b])
                    a4, a4b = next_acc()
                    P.group("pe", [lambda e, a4=a4, p0=p0: e.matmul(a4[:, 0:256], kd[p0:p0 + 64, :], vn[p0:p0 + 64, :],
                                                                    start=True, stop=True)],
                            reads=[prep_b, vn_b], writes=[a4b])
                    P.op("dve", lambda e, a4=a4, ch=ch: e.scalar_tensor_tensor(
                        S[:, :], S[:, :], Gb[:, ch:ch + 1], a4[:, 0:256], ALU.mult, ALU.add),
                        reads=[a4b, Gb_b, S_b], writes=[S_b])
                    P.op("act", lambda e: e.copy(Sb[:, :], S[:, :]), reads=[S_b], writes=[Sb_b])

            pt, ptb = next_pst()
            P.group("pe", [lambda e, pt=pt: e.transpose(pt[:, 0:128], S[:, 128:256], ident[:, :])],
                    reads=[S_b, ident_b], writes=[ptb])
            P.op("act", lambda e, pt=pt: e.copy(ABt[:, 0:128], pt[:, 0:128]), reads=[ptb], writes=[ABt_b])
            P.op("dve", lambda e: e.tensor_copy(ABt[:, 128:256], S[:, 0:128]), reads=[S_b], writes=[ABt_b])
            bi = nc.dram_tensor("gdn_in%d" % h, [128, 256], F32)
            bo = nc.dram_tensor("gdn_out%d" % h, [NCORES * 128, 256], F32)
            bib, bob = Buf("gbi"), Buf("gbo")
            P.dma("sp", lambda e, bi=bi: e.dma_start(out=bi.ap(), in_=ABt), reads=[ABt_b], writes=[bib])
            P.collective(lambda e, bi=bi, bo=bo: e.collective_compute(
                "AllGather", ALU.bypass, replica_groups=[list(range(NCORES))],
                ins=[bi.ap().opt()], outs=[bo.ap().opt()]), reads=[bib], writes=[bob])
            P.op("dve", lambda e: e.memset(Srun, 0.0), writes=[scan_b])
            P.op("dve", lambda e: e.memset(Sin, 0.0), writes=[scan_b])
            for r in range(NCORES):
                s_ = r % 2
                P.dma("sp", lambda e, r=r, s_=s_, bo=bo: e.dma_start(out=ABr[:, s_, :], in_=bo.ap()[r * 128:(r + 1) * 128, :]),
                      reads=[bob], writes=[ABr_b[s_]])
                P.op("dve", lambda e, r=r: e.scalar_tensor_tensor(Sin, Srun, selfhot[:, r:r + 1], Sin, ALU.mult, ALU.add),
                     reads=[scan_b, cg_b], writes=[scan_b])
                a_, ab = next_acc()
                P.group("pe", [lambda e, a_=a_, s_=s_: e.matmul(a_[:, 0:128], ABr[:, s_, 0:128], Srun, start=True, stop=True)],
                        reads=[ABr_b[s_], scan_b], writes=[ab])
                P.op("dve", lambda e, a_=a_, s_=s_: e.tensor_tensor(Srun, a_[:, 0:128], ABr[:, s_, 128:256], ALU.add),
                     reads=[ab, ABr_b[s_]], writes=[scan_b])
            P.dma("sp", lambda e, h=h: e.dma_start(out=O["S_p"][h, :, :], in_=Srun), reads=[scan_b], writes=[OB["S_p"]])
            P.op("act", lambda e: e.copy(Sinb, Sin), reads=[scan_b], writes=[scan_b])
            for (t0, tn) in TILES[0:2]:
                a_, ab = next_acc()
                P.group("pe", [lambda e, a_=a_, t0=t0, tn=tn: e.matmul(a_[:, 0:tn], Sinb, OphT[:, t0:t0 + tn], start=True, stop=True)],
                        reads=[scan_b, OphT_b], writes=[ab])
                P.op("dve", lambda e, a_=a_, t0=t0, tn=tn: e.tensor_tensor(o0T[:, t0:t0 + tn], o0T[:, t0:t0 + tn], a_[:, 0:tn], ALU.add),
                     reads=[ab, o0T_b], writes=[o0T_b])

            for b_ in range(NS):
                s_ = b_ % 2
                egc = sscal[:, 0, b_:b_ + 1]
                btc = sscal[:, 1, b_:b_ + 1]
                P.dma("sp", lambda e, b_=b_, s_=s_, h=h: e.dma_start(out=S0[:, s_, :], in_=I["st_S"][b_, h, :, :]),
                      writes=[S0_b[s_]])
                a_, ab = next_acc()
                P.group("pe", [lambda e, a_=a_, s_=s_, b_=b_: e.matmul(a_[:, 0:1], S0[:, s_, :], smp[:, 1, b_:b_ + 1],
                                                                      start=True, stop=True)],
                        reads=[S0_b[s_], smp_b], writes=[ab])
                P.op("dve", lambda e, a_=a_, egc=egc: e.tensor_scalar_mul(dl[:, 0:1], a_[:, 0:1], egc),
                     reads=[ab, sscal_b], writes=[smisc_b])
                P.op("dve", lambda e, b_=b_, btc=btc: e.scalar_tensor_tensor(
                    dl[:, 1:2], smp[:, 2, b_:b_ + 1], 1.0, dl[:, 0:1], ALU.mult, ALU.subtract),
                    reads=[smp_b, smisc_b], writes=[smisc_b])
                P.op("dve", lambda e, btc=btc: e.tensor_scalar_mul(dl[:, 1:2], dl[:, 1:2], btc),
                     reads=[smisc_b, sscal_b], writes=[smisc_b])
                P.op("dve", lambda e: e.tensor_scalar_mul(dg[:, :], ident[:, :], dl[:, 1:2]),
                     reads=[smisc_b, ident_b], writes=[smisc_b])
                a2, a2b = next_acc()
                P.group("pe", [lambda e, a2=a2: e.matmul(a2[:, 0:128], onesf[:, :], dg[:, :], start=True, stop=True)],
                        reads=[smisc_b, cg_b], writes=[a2b])
                P.op("act", lambda e, s_=s_, egc=egc: e.mul(S1[:, s_, :], S0[:, s_, :], egc),
                     reads=[S0_b[s_], sscal_b], writes=[S1_b[s_]])
                P.op("dve", lambda e, a2=a2, s_=s_, b_=b_: e.scalar_tensor_tensor(
                    S1[:, s_, :], a2[:, 0:128], smp[:, 1, b_:b_ + 1], S1[:, s_, :], ALU.mult, ALU.add),
                    reads=[a2b, smp_b, S1_b[s_]], writes=[S1_b[s_]])
                P.dma("sp", lambda e, b_=b_, s_=s_, h=h: e.dma_start(out=O["S_s"][b_, h, :, :], in_=S1[:, s_, :]),
                      reads=[S1_b[s_]], writes=[OB["S_s"]])
                a3, a3b = next_acc()
                P.group("pe", [lambda e, a3=a3, s_=s_, b_=b_: e.matmul(a3[:, 0:1], S1[:, s_, :], smp[:, 0, b_:b_ + 1],
                                                                      start=True, stop=True)],
                        reads=[S1_b[s_], smp_b], writes=[a3b])
                P.op("dve", lambda e, a3=a3, b_=b_: e.tensor_copy(o0T[:, TP + b_:TP + b_ + 1], a3[:, 0:1]),
                     reads=[a3b], writes=[o0T_b])

            P.op("act", lambda e: e.activation(out=sqv, in_=o0T[:, :], func=AF.Square), reads=[o0T_b], writes=[sqv_b])
            for ti, (t0, tn) in enumerate(TILES):
                a_, ab = next_acc()
                P.group("pe", [lambda e, a_=a_, t0=t0, tn=tn: e.matmul(a_[:, 0:tn], ones_bf[:], sqv[:, t0:t0 + tn],
                                                                      start=True, stop=True)],
                        reads=[sqv_b, const_b], writes=[ab])
                P.op("act", lambda e, a_=a_, t0=t0, tn=tn: e.activation(
                    out=rstd[:, t0:t0 + tn], in_=a_[:, 0:tn], func=AF.Sqrt, bias=epsc[:, 0:1], scale=1.0 / 128),
                    reads=[ab, const_b], writes=[rstd_b])
            P.op("dve", lambda e: e.reciprocal(rstd[:], rstd[:]), reads=[rstd_b], writes=[rstd_b])
            P.op("dve", lambda e: e.scalar_tensor_tensor(o0T[:, :], o0T[:, :], gnw[:, 0:1], rstd[:], ALU.mult, ALU.mult),
                 reads=[o0T_b, gnw_b, rstd_b], writes=[o0T_b])
            P.op("dve", lambda e: e.tensor_tensor(yh[:, :], o0T[:, :], zgs[:, :], ALU.mult),
                 reads=[o0T_b, zgs_b], writes=[yh_b])
            proj_residual_rows(I["l1_w_out"], h * 128, 1, lambda kc, t0, tn: yh[:, t0:t0 + tn], [yh_b])

        proj_residual_rows(I["l1_w_out"], 1536, 4, lambda kc, t0, tn: ysc[:, kc, t0:t0 + tn], ysc_b)

    layer1_gdn()
    if KSTOP == "x2":
        maybe_stop("x2", xT[:, :, :], xT_b, [128, 16, NT], F32)

    ffn(1)

    def final_out():
        P.barrier()
        ar["off"] = 0
        wcol, wcol_b = rmsnorm("final_norm")
        ytmp = aalloc([128, 2, 4, 128], F32)
        ytmp_b = [Buf("ytmp0"), Buf("ytmp1")]
        yst = aalloc([128, 2, D], F32)
        yst_b = [Buf("yst0"), Buf("yst1")]
        k_ = 0
        for blk in range(9):
            t0 = blk * 128
            tn = 128 if blk < 8 else NS
            so = blk % 2
            for c4 in range(4):
                s_ = k_ % 2
                k_ += 1
                for j in range(4):
                    c = c4 * 4 + j
                    P.op("dve", lambda e, c=c, j=j, s_=s_, t0=t0, tn=tn: e.scalar_tensor_tensor(
                        ytmp[:, s_, j, 0:tn], xT[:, c, t0:t0 + tn], wcol[:, c:c + 1], rstd[:, t0:t0 + tn],
                        ALU.mult, ALU.mult), reads=[xT_b[c], wcol_b, rstd_b], writes=[ytmp_b[s_]])
                pt, ptb = next_pst()
                P.group("pe", [lambda e, pt=pt, j=j, s_=s_, tn=tn: e.transpose(
                    pt[0:tn, j * 128:(j + 1) * 128], ytmp[:, s_, j, 0:tn], ident[:, :]) for j in range(4)],
                    reads=[ytmp_b[s_], ident_b], writes=[ptb])
                P.op("act", lambda e, pt=pt, so=so, c4=c4, tn=tn: e.copy(yst[0:tn, so, c4 * 512:(c4 + 1) * 512], pt[0:tn, 0:512]),
                     reads=[ptb], writes=[yst_b[so]])
            if blk < 8:
                P.dma("sp", lambda e, so=so, t0=t0: e.dma_start(out=O["y_p"][t0:t0 + 128, :], in_=yst[:, so, :]),
                      reads=[yst_b[so]], writes=[OB["y_p"]])
            else:
                P.dma("sp", lambda e, so=so: e.dma_start(out=O["y_s"][:, :], in_=yst[0:NS, so, :]),
                      reads=[yst_b[so]], writes=[OB["y_s"]])

    final_out()
    for nm in OUT_ORDER:
        O[nm]

    P.replay()
    es.close()
    return nc


def _core_inputs(inputs, c):
    m = {}
    m["xp"] = np.ascontiguousarray(inputs["x_prompt"][0, c * TP:(c + 1) * TP, :])
    sl = slice(c * NS, (c + 1) * NS)
    m["xs"] = np.ascontiguousarray(inputs["x_sample"][sl, 0, :])
    m["st_pool"] = np.ascontiguousarray(inputs["state_l0_pool"][sl])
    m["ck"] = np.ascontiguousarray(inputs["cache_l0_k"][sl]).reshape(NS, 128, 256)
    m["cv"] = np.ascontiguousarray(inputs["cache_l0_v"][sl]).reshape(NS, 128, 256)
    m["st_ffn0"] = np.ascontiguousarray(inputs["state_l0_ffn_conv"][sl])
    m["st_gconv"] = np.ascontiguousarray(inputs["state_l1_gdn_conv"][sl])
    m["st_S"] = np.ascontiguousarray(inputs["state_l1_gdn_S"][sl])
    m["st_sconv"] = np.ascontiguousarray(inputs["state_l1_sconv"][sl])
    m["st_ffn1"] = np.ascontiguousarray(inputs["state_l1_ffn_conv"][sl])
    for nm in ["l0_norm_mix", "l0_w_in", "l0_pool_w", "l0_pool_scale", "l0_sinks", "l0_w_out", "l0_norm_ffn",
               "l0_ffn_w_up", "l0_ffn_conv", "l0_ffn_w_down", "l1_norm_mix", "l1_w_in", "l1_gdn_conv",
               "l1_gdn_A_log", "l1_gdn_dt_bias", "l1_gdn_norm", "l1_sconv_w", "l1_w_out", "l1_norm_ffn",
               "l1_ffn_w_up", "l1_ffn_conv", "l1_ffn_w_down", "final_norm"]:
        m[nm] = np.ascontiguousarray(inputs[nm], dtype=np.float32)
    m["c_ident"] = np.eye(128, dtype=np.float32)
    oh = np.zeros((128, 8), np.float32)
    if c > 0:
        oh[:, c - 1] = 1.0
    m["c_onehot"] = oh
    kk = np.arange(128)[:, None]
    qq = np.arange(128)[None, :]
    m["c_masks"] = np.concatenate([(qq >= kk), (kk >= qq)], axis=1).astype(np.float32)
    m["c_flag"] = np.full((128, 1), 0.0 if c == 0 else 1.0, np.float32)
    corr = np.ones((128, 4, 16), np.float32)
    if c == 0:
        for g, w in enumerate((2, 4, 8, 16)):
            pos = np.arange(16)
            corr[:, g, :] = (w / np.minimum(pos + 1, w))[None, :]
    m["c_poolcorr"] = corr
    bd = np.zeros((128, 128), np.float32)
    bd[:64, :64] = 1.0
    bd[64:, 64:] = 1.0
    m["c_bdones"] = bd
    pp = np.arange(128)[:, None]
    ff = np.arange(128)[None, :]
    same = (pp // 64) == (ff // 64)
    m["c_gmasks"] = np.concatenate([(same & (pp <= ff)), (same & (pp > ff))], axis=1).astype(np.float32)
    sh_ = np.zeros((128, 8), np.float32)
    sh_[:, c] = 1.0
    m["c_selfhot"] = sh_
    return m


def kernel(**inputs):
    inputs = {k_: np.asarray(v) for k_, v in inputs.items()}
    nc = build()
    in_maps = []
    for c in range(NCORES):
        m = _core_inputs(inputs, c)
        in_maps.append({k_: v for k_, v in m.items() if k_ in nc._I})
    res = run_bass_kernel_spmd(nc, in_maps, core_ids=list(range(NCORES)))
    r = res.results
    outs = []
    for nm in OUT_ORDER:
        if nm in ("y_p",):
            outs.append(np.concatenate([r[c][nm] for c in range(NCORES)], axis=0)[None])
        elif nm.endswith("_p"):
            full = np.asarray(r[NCORES - 1][nm])
            shp = {"pool_p": (1, 15, 512), "k_p": (1, 128, 4, 64), "v_p": (1, 128, 4, 64),
                   "ffn0_p": (1, 2, 2 * DFF), "gconv_p": (1, 3, 4608), "S_p": (1, 12, 128, 128),
                   "sconv_p": (1, 2, 512), "ffn1_p": (1, 2, 2 * DFF)}[nm]
            outs.append(full.reshape(shp))
        else:
            cat = np.concatenate([np.asarray(r[c][nm]) for c in range(NCORES)], axis=0)
            shp = {"y_s": (128, 1, D), "pool_s": (128, 15, 512), "k_s": (128, 128, 4, 64),
                   "v_s": (128, 128, 4, 64), "ffn0_s": (128, 2, 2 * DFF), "gconv_s": (128, 3, 4608),
                   "S_s": (128, 12, 128, 128), "sconv_s": (128, 2, 512), "ffn1_s": (128, 2, 2 * DFF)}[nm]
            outs.append(cat.reshape(shp))
    return tuple(np.ascontiguousarray(o, dtype=np.float32) for o in outs)
```

```python
import contextlib
import numpy as np
import concourse.bass as bass
import concourse.mybir as mybir
from concourse.bass_utils import run_bass_kernel_spmd

F32 = mybir.dt.float32
BF16 = mybir.dt.bfloat16
AF = mybir.ActivationFunctionType
ALU = mybir.AluOpType
AX = mybir.AxisListType

NCORES = 8
D = 2048
SEQ = 8192
TP = SEQ // NCORES
NS = 16
NT = TP + NS
DFF = 5632
EPS = 1e-6
TILES = [(0, 512), (512, 512), (1024, 16)]

D_POOL = 512
D_ATTN = 1536
D_KV = 256
D_IN0 = 2560
D_IN1 = 7704
NDMASEM = 24


class Buf:
    __slots__ = ("name", "w", "r", "excl")

    def __init__(self, name="", excl=False):
        self.name = name
        self.w = None
        self.r = {}
        self.excl = excl


class Prog:
    ENGS = ("pe", "act", "dve", "pool", "sp")

    def __init__(self, nc, es):
        self.nc = nc
        self.q = {e: [] for e in self.ENGS}
        self.cnt = {}
        self.sem = {}
        for e in self.ENGS:
            self.sem[e] = es.enter_context(nc.semaphore("sem_" + e))
            self.cnt[e] = 0
        for i in range(NDMASEM):
            k = "dma%d" % i
            self.sem[k] = es.enter_context(nc.semaphore("sem_" + k))
            self.cnt[k] = 0
        self.waited = {e: {} for e in self.ENGS}
        self.dma_rr = 0
        self.dma_rr2 = 0
        self.ncc = 0
        self.es = es

    def _deps(self, reads, writes, extra):
        deps = list(extra)
        for b in reads:
            if b.w is not None:
                deps.append(b.w)
            if b.excl:
                deps.extend(b.r.items())
        for b in writes:
            if b.w is not None:
                deps.append(b.w)
            deps.extend(b.r.items())
        return deps

    def _waits(self, eng, deps):
        waits = []
        wd = self.waited[eng]
        for (k, c) in deps:
            if k == "pe" and eng == "pe":
                continue
            if wd.get(k, 0) >= c:
                continue
            wd[k] = c
            waits.append((k, c))
        return waits

    def _mark(self, tok, reads, writes):
        k, c = tok
        for b in reads:
            if b.r.get(k, 0) < c:
                b.r[k] = c
        for b in writes:
            b.w = tok
            b.r = {}

    def op(self, eng, fn, reads=(), writes=(), extra=()):
        waits = self._waits(eng, self._deps(reads, writes, extra))
        self.cnt[eng] += 1
        tok = (eng, self.cnt[eng])
        self.q[eng].append((waits, fn, (eng, 1)))
        self._mark(tok, reads, writes)
        return tok

    def group(self, eng, fns, reads=(), writes=(), extra=()):
        waits = self._waits(eng, self._deps(reads, writes, extra))
        self.cnt[eng] += 1
        tok = (eng, self.cnt[eng])
        n = len(fns)
        for i, fn in enumerate(fns):
            self.q[eng].append((waits if i == 0 else [], fn, (eng, 1) if i == n - 1 else None))
        self._mark(tok, reads, writes)
        return tok

    def dma(self, eng, fn, reads=(), writes=(), extra=()):
        if eng == "sp":
            k = "dma%d" % self.dma_rr
            self.dma_rr = (self.dma_rr + 1) % 16
        else:
            k = "dma%d" % (16 + self.dma_rr2)
            self.dma_rr2 = (self.dma_rr2 + 1) % (NDMASEM - 16)
        deps = self._deps(reads, writes, extra)
        if self.cnt[k] > 0:
            deps.append((k, self.cnt[k]))
        waits = self._waits(eng, deps)
        self.cnt[k] += 16
        tok = (k, self.cnt[k])
        self.q[eng].append((waits, fn, (k, 16)))
        self._mark(tok, reads, writes)
        return tok

    def collective(self, fn, reads=(), writes=()):
        k = "cc%d" % self.ncc
        self.ncc += 1
        self.sem[k] = self.es.enter_context(self.nc.semaphore("sem_" + k))
        self.cnt[k] = 0
        waits = self._waits("pool", self._deps(reads, writes, ()))
        self.cnt[k] += 1
        tok = (k, 1)
        self.q["pool"].append((waits, fn, (k, None)))
        self._mark(tok, reads, writes)
        return tok

    def barrier(self):
        toks = [(k, c) for k, c in self.cnt.items() if c > 0]
        for e in self.ENGS:
            waits = self._waits(e, toks)
            if waits:
                self.q[e].append((waits, None, None))

    def replay(self):
        nc = self.nc
        handles = {"pe": "tensor", "act": "scalar", "dve": "vector", "pool": "gpsimd", "sp": "sync"}
        self.barrier()
        with nc.Block() as block:
            for e in self.ENGS:
                q = self.q[e]
                sem = self.sem

                def body(eng, q=q):
                    for (waits, fn, sig) in q:
                        for (k, c) in waits:
                            eng.wait_ge(sem[k], c)
                        if fn is None:
                            continue
                        inst = fn(eng)
                        if sig is not None:
                            if sig[1] is None:
                                inst.then_inc(sem[sig[0]])
                            else:
                                inst.then_inc(sem[sig[0]], sig[1])

                getattr(block, handles[e])(body)


IN_SHAPES = {
    "xp": [TP, D], "xs": [NS, D], "st_pool": [NS, 15, 512], "ck": [NS, 128, 256], "cv": [NS, 128, 256],
    "st_ffn0": [NS, 2, 2 * DFF], "st_gconv": [NS, 3, 4608], "st_S": [NS, 12, 128, 128],
    "st_sconv": [NS, 2, 512], "st_ffn1": [NS, 2, 2 * DFF],
    "l0_norm_mix": [D], "l0_w_in": [D, D_IN0], "l0_pool_w": [4, 128, 128], "l0_pool_scale": [512],
    "l0_sinks": [24], "l0_w_out": [D, D], "l0_norm_ffn": [D], "l0_ffn_w_up": [D, 2 * DFF],
    "l0_ffn_conv": [3, 2 * DFF], "l0_ffn_w_down": [DFF, D],
    "l1_norm_mix": [D], "l1_w_in": [D, D_IN1], "l1_gdn_conv": [4, 4608], "l1_gdn_A_log": [12],
    "l1_gdn_dt_bias": [12], "l1_gdn_norm": [128], "l1_sconv_w": [3, 512], "l1_w_out": [D, D],
    "l1_norm_ffn": [D], "l1_ffn_w_up": [D, 2 * DFF], "l1_ffn_conv": [3, 2 * DFF],
    "l1_ffn_w_down": [DFF, D], "final_norm": [D],
    "c_ident": [128, 128], "c_onehot": [128, 8], "c_masks": [128, 256], "c_flag": [128, 1],
    "c_poolcorr": [128, 4, 16], "c_bdones": [128, 128], "c_gmasks": [128, 256], "c_selfhot": [128, 8],
}
OUT_SHAPES = {
    "y_p": [TP, D], "y_s": [NS, D], "pool_p": [15, 512], "pool_s": [NS, 15, 512],
    "k_p": [128, 256], "k_s": [NS, 128, 256], "v_p": [128, 256], "v_s": [NS, 128, 256],
    "ffn0_p": [2, 2 * DFF], "ffn0_s": [NS, 2, 2 * DFF], "gconv_p": [3, 4608], "gconv_s": [NS, 3, 4608],
    "S_p": [12, 128, 128], "S_s": [NS, 12, 128, 128], "sconv_p": [2, 512], "sconv_s": [NS, 2, 512],
    "ffn1_p": [2, 2 * DFF], "ffn1_s": [NS, 2, 2 * DFF],
}
OUT_ORDER = ["y_p", "y_s", "pool_p", "pool_s", "k_p", "k_s", "v_p", "v_s", "ffn0_p", "ffn0_s",
             "gconv_p", "gconv_s", "S_p", "S_s", "sconv_p", "sconv_s", "ffn1_p", "ffn1_s"]


class Lazy(dict):
    def __init__(self, mk):
        super().__init__()
        self.mk = mk

    def __missing__(self, key):
        v = self.mk(key)
        self[key] = v
        return v


def build(stage=99):
    try:
        return _build(stage)
    except _StopBuild as ex:
        return ex.nc


class _StopBuild(Exception):
    pass


def _build(stage=99):
    nc = bass.Bass("TRN2", target_bir_lowering=False)
    es = contextlib.ExitStack()
    P = Prog(nc, es)

    I = Lazy(lambda n: nc.dram_tensor(n, list(IN_SHAPES[n]), F32, kind="ExternalInput").ap())
    O = Lazy(lambda n: nc.dram_tensor(n, list(OUT_SHAPES[n]), F32, kind="ExternalOutput").ap())
    OB = Lazy(lambda n: Buf("o_" + n))
    nc._I, nc._O = I, O

    def sb(name, shape, dt=F32, st=None):
        return (st or es).enter_context(nc.sbuf_tensor(name, list(shape), dt))

    def ps(name, shape, dt=F32):
        return es.enter_context(nc.psum_tensor(name, list(shape), dt))

    def scratch(name, shape, dt=F32):
        return nc.dram_tensor(name, list(shape), dt).ap()

    import os
    rr = {"ev": 0, "acc": 0, "pst": 0, "wp": 0}
    KSTOP = os.environ.get("KSTOP", "")

    class Stop(Exception):
        pass

    def maybe_stop(tag, src_ap, bufs, shape, dt):
        if KSTOP != tag:
            return
        dbg = nc.dram_tensor("dbg_" + tag, list(shape), dt, kind="ExternalOutput").ap()
        P.dma("sp", lambda e: e.dma_start(out=dbg, in_=src_ap), reads=bufs)
        P.replay()
        es.close()
        ex = _StopBuild()
        ex.nc = nc
        raise ex
    AWORDS = 19 * 1024
    arena_t = sb("arena", [128, AWORDS], F32)
    ar = {"off": 0}

    def aalloc(shape, dt=F32):
        n = 1
        for d_ in shape[1:]:
            n *= d_
        words = n if dt == F32 else (n + 1) // 2
        off = ar["off"]
        assert off + words <= AWORDS, ("arena overflow", off, words)
        ar["off"] = off + words
        a = arena_t[0:shape[0], off:off + words]
        if dt != F32:
            a = a.bitcast(dt)[:, 0:n]
        if len(shape) == 3:
            a = a.rearrange("p (a b) -> p a b", a=shape[1])
        elif len(shape) == 4:
            a = a.rearrange("p (a b c) -> p a b c", a=shape[1], b=shape[2])
        return a

    def evac_eng():
        rr["ev"] += 1
        return "act" if rr["ev"] % 2 else "dve"

    def copy_fn(eng, out, in_, scale=None):
        if eng == "act":
            if scale is None:
                return lambda e: e.copy(out, in_)
            return lambda e: e.mul(out, in_, scale)
        if scale is None:
            return lambda e: e.tensor_copy(out, in_)
        return lambda e: e.tensor_scalar_mul(out, in_, scale)

    xT = sb("xT", [128, 16, NT], F32)
    xT_b = [Buf("xT%d" % c) for c in range(16)]
    xn = sb("xn", [128, 16, NT], BF16)
    xn_b = [Buf("xn%d" % c) for c in range(16)]
    ident = sb("ident", [128, 128], F32)
    ident_b = Buf("ident")
    ones_bf = sb("ones_bf", [128, 128], BF16)
    bdones = sb("bdones", [128, 128], BF16)
    onehot = sb("onehot", [128, 8], F32)
    flag = sb("flag", [128, 1], F32)
    const_b = Buf("consts")
    cst = sb("cst", [128, 256], F32)
    cst_b = Buf("cst")
    P.dma("sp", lambda e: e.dma_start(out=ident[:], in_=I["c_ident"][:, :]), writes=[ident_b])
    P.dma("sp", lambda e: e.dma_start(out=onehot[:], in_=I["c_onehot"][:, :]), writes=[const_b])
    P.dma("sp", lambda e: e.dma_start(out=flag[:], in_=I["c_flag"][:, :]), writes=[const_b])
    P.dma("sp", lambda e: e.dma_start(out=cst[:, 0:128], in_=I["c_bdones"][:, :]), writes=[cst_b])
    P.op("dve", lambda e: e.tensor_copy(bdones[:], cst[:, 0:128]), reads=[cst_b], writes=[const_b])
    P.op("dve", lambda e: e.memset(ones_bf[:], 1.0), writes=[const_b])
    identb = sb("identb", [128, 128], BF16)
    P.op("dve", lambda e: e.tensor_copy(identb[:], ident[:]), reads=[ident_b], writes=[const_b])
    epsc = sb("epsc", [128, 1], F32)
    for _ in range(int(os.environ.get("KSALT", "0"))):
        P.op("dve", lambda e: e.memset(ones_bf[:], 1.0), writes=[const_b])
    P.op("dve", lambda e: e.memset(epsc[:], EPS), writes=[const_b])

    NACC = 5
    pacc = [ps("pacc%d" % i, [128, 512], F32) for i in range(NACC)]
    pacc_b = [Buf("pacc%d" % i, True) for i in range(NACC)]
    pst = [ps("pst%d" % i, [128, 512], F32) for i in range(3)]
    pst_b = [Buf("pst%d" % i, True) for i in range(3)]

    def next_acc():
        i = rr["acc"] % NACC
        rr["acc"] += 1
        return pacc[i], pacc_b[i]

    def next_pst():
        i = rr["pst"] % 3
        rr["pst"] += 1
        return pst[i], pst_b[i]

    strow = sb("strow", [128, 128], F32)
    strow_b = Buf("strow")

    def load_cols(src, R, name):
        dst = sb(name, [128, R], F32)
        dstb = Buf(name)
        for r0 in range(0, R, 128):
            rn = min(128, R - r0)
            P.dma("sp", lambda e, r0=r0, rn=rn: e.dma_start(out=strow[0:rn, :], in_=src[r0:r0 + rn, :]),
                  writes=[strow_b])
            pt, ptb = next_pst()
            P.group("pe", [lambda e, pt=pt, rn=rn: e.transpose(pt[:, 0:rn], strow[0:rn, :], ident[0:rn, 0:rn])],
                    reads=[strow_b, ident_b], writes=[ptb])
            P.op("dve", lambda e, pt=pt, r0=r0, rn=rn: e.tensor_copy(dst[:, r0:r0 + rn], pt[:, 0:rn]),
                 reads=[ptb], writes=[dstb])
        return dst, dstb

    xin = aalloc([128, 2, D], F32)
    xin_b = [Buf("xin0"), Buf("xin1")]
    nblk = TP // 128
    for blk in range(nblk + 1):
        s = blk % 2
        if blk < nblk:
            rows = 128
            src = I["xp"][blk * 128:(blk + 1) * 128, :]
        else:
            rows = NS
            src = I["xs"][:, :]
        P.dma("sp", lambda e, s=s, rows=rows, src=src: e.dma_start(out=xin[0:rows, s, :], in_=src),
              writes=[xin_b[s]])
        for c4 in range(4):
            pt, ptb = next_pst()
            fns = []
            for j in range(4):
                c = c4 * 4 + j
                fns.append(lambda e, pt=pt, j=j, c=c, s=s, rows=rows: e.transpose(
                    pt[:, j * 128:j * 128 + rows], xin[0:rows, s, c * 128:(c + 1) * 128],
                    ident[0:rows, 0:rows]))
            P.group("pe", fns, reads=[xin_b[s], ident_b], writes=[ptb])
            t0 = blk * 128
            eng = evac_eng()
            P.op(eng, copy_fn(eng, xT[:, c4 * 4:c4 * 4 + 4, t0:t0 + rows],
                              pt[:].rearrange("p (j t) -> p j t", j=4)[:, :, 0:rows]),
                 reads=[ptb], writes=[xT_b[c4 * 4 + j] for j in range(4)])
    if stage == 0:
        dbg = nc.dram_tensor("dbg", [128, 16, NT], F32, kind="ExternalOutput").ap()
        P.dma("sp", lambda e: e.dma_start(out=dbg[:, :, :], in_=xT[:, :, :]), reads=xT_b)
        P.replay()
        es.close()
        return nc
    P.barrier()
    ar["off"] = 0

    sq = sb("sq", [128, 2, NT], BF16)
    sq_b = [Buf("sq0"), Buf("sq1")]
    rstd = sb("rstd", [128, NT], F32)
    rstd_b = Buf("rstd")

    def rmsnorm(wname):
        wcol, wcol_b = load_cols(I[wname].rearrange("(c p) -> c p", p=128), 16, "wc_" + wname)
        accs = [next_acc() for _ in TILES]
        for c in range(16):
            s = c % 2
            P.op("act", lambda e, s=s, c=c: e.activation(out=sq[:, s, :], in_=xT[:, c, :], func=AF.Square),
                 reads=[xT_b[c]], writes=[sq_b[s]])
            for ti, (t0, tn) in enumerate(TILES):
                a, ab = accs[ti]
                P.group("pe", [lambda e, a=a, s=s, t0=t0, tn=tn, c=c: e.matmul(
                    a[:, 0:tn], ones_bf[:], sq[:, s, t0:t0 + tn], start=(c == 0), stop=(c == 15))],
                    reads=[sq_b[s], const_b], writes=[ab])
        for ti, (t0, tn) in enumerate(TILES):
            a, ab = accs[ti]
            P.op("act", lambda e, a=a, t0=t0, tn=tn: e.activation(
                out=rstd[:, t0:t0 + tn], in_=a[:, 0:tn], func=AF.Sqrt, bias=epsc[:, 0:1], scale=1.0 / D),
                reads=[ab, const_b], writes=[rstd_b])
        P.op("dve", lambda e: e.reciprocal(rstd[:], rstd[:]), reads=[rstd_b], writes=[rstd_b])
        for c in range(16):
            P.op("dve", lambda e, c=c: e.scalar_tensor_tensor(
                xn[:, c, :], xT[:, c, :], wcol[:, c:c + 1], rstd[:], ALU.mult, ALU.mult),
                reads=[xT_b[c], wcol_b, rstd_b], writes=[xn_b[c]])
        return wcol, wcol_b

    WPE = 4096
    wp = [sb("wp%d" % i, [128, WPE], BF16) for i in range(2)]
    wp_b = [Buf("wp0"), Buf("wp1")]

    def load_panel(W, KC, c0, w):
        i = rr["wp"] % 2
        rr["wp"] += 1
        pv = wp[i][:, 0:KC * w].rearrange("p (k n) -> p k n", k=KC)
        src = W[:, c0:c0 + w].rearrange("(kc p) n -> p kc n", p=128)
        P.dma("pool", lambda e: e.dma_start(out=pv, in_=src), writes=[wp_b[i]])
        return pv, wp_b[i]

    def gemm_fm(pv, pvb, KC, parts, rhs, rhs_b, evac, tiles=TILES):
        for ti, (t0, tn) in enumerate(tiles):
            a, ab = next_acc()
            fns = []
            for (col, msz, op0) in parts:
                for kc in range(KC):
                    fns.append(lambda e, a=a, col=col, msz=msz, op0=op0, kc=kc, t0=t0, tn=tn: e.matmul(
                        a[op0:op0 + msz, 0:tn], pv[:, kc, col:col + msz], rhs(kc, t0, tn),
                        start=(kc == 0), stop=(kc == KC - 1)))
            P.group("pe", fns, reads=[pvb] + list(rhs_b), writes=[ab])
            evac(a, ab, ti, t0, tn)

    def gemm_tm(pv, pvb, KC, ncols, lhs, lhs_b, evac, blocks):
        for (bi, t0, tn) in blocks:
            a, ab = next_acc()
            fns = []
            for kc in range(KC):
                fns.append(lambda e, a=a, kc=kc, t0=t0, tn=tn: e.matmul(
                    a[0:tn, 0:ncols], lhs(kc, t0, tn), pv[:, kc, 0:ncols],
                    start=(kc == 0), stop=(kc == KC - 1)))
            P.group("pe", fns, reads=[pvb] + list(lhs_b), writes=[ab])
            evac(a, ab, bi, t0, tn)

    xn_rhs = lambda kc, t0, tn: xn[:, kc, t0:t0 + tn]

    TK = 128 + TP + NS
    qT = aalloc([128, 12, NT], BF16)
    qT_b = [Buf("qT%d" % c) for c in range(12)]
    kT2 = aalloc([128, 4, TK], BF16)
    kT2_b = Buf("kT2")
    vT2S = aalloc([128, 4, NS], F32)
    vT2S_b = Buf("vT2S")
    vtok = aalloc([128, 10, 256], BF16)
    vtok_b = [Buf("vtok%d" % i) for i in range(10)]
    mark_attn = ar["off"]
    uext = aalloc([128, 4, 15 + TP], F32)
    uext_b = [Buf("uext%d" % g) for g in range(4)]
    uS = aalloc([128, 4, NS], F32)
    uS_b = Buf("uS")
    tokst = aalloc([128, 2, 256], F32)
    tokst_b = [Buf("tokst0"), Buf("tokst1")]

    rmsnorm("l0_norm_mix")
    maybe_stop("norm", xn[:, :, :], xn_b, [128, 16, NT], BF16)

    W = I["l0_w_in"]
    def evac_u(g):
        def f(a, ab, ti, t0, tn):
            eng = evac_eng()
            if ti < 2:
                P.op(eng, copy_fn(eng, uext[:, g, 15 + t0:15 + t0 + tn], a[:, 0:tn]), reads=[ab],
                     writes=[uext_b[g]])
            else:
                P.op(eng, copy_fn(eng, uS[:, g, :], a[:, 0:tn]), reads=[ab], writes=[uS_b])
        return f

    for pi in range(2):
        pv, pvb = load_panel(W, 16, pi * 256, 256)
        if KSTOP == "panel":
            maybe_stop("panel", pv, [pvb], [128, 16, 256], BF16)
        for m in range(2):
            gemm_fm(pv, pvb, 16, [(m * 128, 128, 0)], xn_rhs, xn_b, evac_u(pi * 2 + m))
        if KSTOP == "gemm":
            maybe_stop("gemm", uext[:, 0:2, :], uext_b, [128, 2, 15 + TP], F32)

        def evac_utok(a, ab, bi, t0, tn, pi=pi):
            s = bi % 2
            eng = evac_eng()
            P.op(eng, copy_fn(eng, tokst[0:tn, s, 0:256], a[0:tn, 0:256]), reads=[ab], writes=[tokst_b[s]])
            if bi == 0:
                P.dma("sp", lambda e, s=s: e.dma_start(out=O["pool_p"][:, pi * 256:(pi + 1) * 256],
                                                       in_=tokst[113:128, s, 0:256]),
                      reads=[tokst_b[s]], writes=[OB["pool_p"]])
            else:
                P.dma("sp", lambda e, s=s: e.dma_start(out=O["pool_s"][:, 14, pi * 256:(pi + 1) * 256],
                                                       in_=tokst[0:NS, s, 0:256]),
                      reads=[tokst_b[s]], writes=[OB["pool_s"]])
        gemm_tm(pv, pvb, 16, 256, xn_rhs, xn_b, evac_utok, [(0, TP - 128, 128), (1, TP, NS)])
        if KSTOP == "tm":
            maybe_stop("tm", tokst[:, :, :], tokst_b, [128, 2, 256], F32)
    P.dma("sp", lambda e: e.dma_start(out=O["pool_s"][:, 0:14, :], in_=I["st_pool"][:, 1:15, :]),
          writes=[OB["pool_s"]])

    if KSTOP == "d2d":
        maybe_stop("d2d", tokst[:, :, :], tokst_b, [128, 2, 256], F32)
    for pi in range(6):
        pv, pvb = load_panel(W, 16, 512 + pi * 256, 256)
        for m in range(2):
            c = pi * 2 + m

            def evac_q(a, ab, ti, t0, tn, c=c):
                eng = evac_eng()
                P.op(eng, copy_fn(eng, qT[:, c, t0:t0 + tn], a[:, 0:tn], 0.125), reads=[ab], writes=[qT_b[c]])
            gemm_fm(pv, pvb, 16, [(m * 128, 128, 0)], xn_rhs, xn_b, evac_q)

    pv, pvb = load_panel(W, 16, 2048, 256)
    for g in range(4):
        def evac_k(a, ab, ti, t0, tn, g=g):
            eng = evac_eng()
            P.op(eng, copy_fn(eng, kT2[:, g, 128 + t0:128 + t0 + tn], a[:, 0:tn]), reads=[ab], writes=[kT2_b])
        gemm_fm(pv, pvb, 16, [(g * 64, 64, 0), (g * 64, 64, 64)], xn_rhs, xn_b, evac_k)

    if KSTOP == "k":
        maybe_stop("k", kT2[:, :, 128:128 + TP], [kT2_b], [128, 4, TP], BF16)

    def evac_ktok(a, ab, bi, t0, tn):
        s = bi % 2
        eng = evac_eng()
        P.op(eng, copy_fn(eng, tokst[0:tn, s, 0:256], a[0:tn, 0:256]), reads=[ab], writes=[tokst_b[s]])
        if bi == 0:
            P.dma("sp", lambda e, s=s: e.dma_start(out=O["k_p"][:, :], in_=tokst[:, s, 0:256]),
                  reads=[tokst_b[s]], writes=[OB["k_p"]])
        else:
            P.dma("sp", lambda e, s=s: e.dma_start(out=O["k_s"][:, 127, :], in_=tokst[0:NS, s, 0:256]),
                  reads=[tokst_b[s]], writes=[OB["k_s"]])
    gemm_tm(pv, pvb, 16, 256, xn_rhs, xn_b, evac_ktok, [(0, TP - 128, 128), (1, TP, NS)])

    if KSTOP == "ktok":
        maybe_stop("ktok", tokst[:, :, :], tokst_b, [128, 2, 256], F32)
    pv, pvb = load_panel(W, 16, 2304, 256)
    for g in range(4):
        def evac_vs(a, ab, ti, t0, tn, g=g):
            eng = evac_eng()
            P.op(eng, copy_fn(eng, vT2S[:, g, :], a[:, 0:tn]), reads=[ab], writes=[vT2S_b])
        gemm_fm(pv, pvb, 16, [(g * 64, 64, 0), (g * 64, 64, 64)], xn_rhs, xn_b, evac_vs, tiles=[TILES[2]])

    if KSTOP == "vs":
        maybe_stop("vs", vT2S[:, :, :], [vT2S_b], [128, 4, NS], F32)

    def evac_vtok(a, ab, bi, t0, tn):
        eng = evac_eng()
        P.op(eng, copy_fn(eng, vtok[0:tn, bi, :], a[0:tn, 0:256]), reads=[ab], writes=[vtok_b[bi]])
        if bi >= 8:
            s = bi % 2
            P.op(eng, copy_fn(eng, tokst[0:tn, s, 0:256], a[0:tn, 0:256]), reads=[ab], writes=[tokst_b[s]])
            if bi == 8:
                P.dma("sp", lambda e, s=s: e.dma_start(out=O["v_p"][:, :], in_=tokst[:, s, 0:256]),
                      reads=[tokst_b[s]], writes=[OB["v_p"]])
            else:
                P.dma("sp", lambda e, s=s: e.dma_start(out=O["v_s"][:, 127, :], in_=tokst[0:NS, s, 0:256]),
                      reads=[tokst_b[s]], writes=[OB["v_s"]])
    _vb = [(b + 1, b * 128, 128) for b in range(8)] + [(9, TP, NS)]
    if KSTOP == "vtokA":
        _vb = _vb[:7]
    if KSTOP == "vtokB":
        _vb = _vb[:8]
    gemm_tm(pv, pvb, 16, 256, xn_rhs, xn_b, evac_vtok, _vb)
    if KSTOP in ("vtokA", "vtokB"):
        maybe_stop(KSTOP, vtok[:, 1:8, :], vtok_b[1:8], [128, 7, 256], BF16)
    if KSTOP == "vtok":
        maybe_stop("vtok", vtok[:, :, :], vtok_b[1:], [128, 10, 256], BF16)
    P.dma("sp", lambda e: e.dma_start(out=O["k_s"][:, 0:127, :], in_=I["ck"][:, 1:128, :]), writes=[OB["k_s"]])
    P.dma("sp", lambda e: e.dma_start(out=O["v_s"][:, 0:127, :], in_=I["cv"][:, 1:128, :]), writes=[OB["v_s"]])

    if stage == 1:
        P.replay()
        es.close()
        return nc

    mark_halo = ar["off"]
    HW_ = 60 + 256 + 128
    hs = aalloc([128, HW_], F32)
    hs_b = Buf("hs")
    hg = aalloc([128, 2, HW_], F32)
    hg_b = [Buf("hg0"), Buf("hg1")]
    bin_t = nc.dram_tensor("halo_in", [128, HW_], F32)
    bout_t = nc.dram_tensor("halo_out", [NCORES * 128, HW_], F32)
    bin_b, bout_b = Buf("bin"), Buf("bout")

    def views(t):
        return (t[:, 0:60].rearrange("p (g t) -> p g t", g=4),
                t[:, 60:316].bitcast(BF16).rearrange("p (g t) -> p g t", g=4),
                t[:, 316:444].bitcast(BF16))
    hu, hk, hv = views(hs)
    P.op("dve", lambda e: e.tensor_copy(hu, uext[:, :, TP:TP + 15]), reads=uext_b, writes=[hs_b])
    P.op("dve", lambda e: e.tensor_copy(hk, kT2[:, :, TP:TP + 128]), reads=[kT2_b], writes=[hs_b])
    P.op("dve", lambda e: e.tensor_copy(hv, vtok[:, 8, :]), reads=[vtok_b[8]], writes=[hs_b])
    P.dma("sp", lambda e: e.dma_start(out=bin_t.ap(), in_=hs), reads=[hs_b], writes=[bin_b])
    P.collective(lambda e: e.collective_compute(
        "AllGather", ALU.bypass, replica_groups=[list(range(NCORES))],
        ins=[bin_t.ap().opt()], outs=[bout_t.ap().opt()]), reads=[bin_b], writes=[bout_b])
    for r in range(NCORES):
        s_ = r % 2
        P.dma("sp", lambda e, r=r, s_=s_: e.dma_start(out=hg[:, s_, :], in_=bout_t.ap()[r * 128:(r + 1) * 128, :]),
              reads=[bout_b], writes=[hg_b[s_]])
        gu, gk, gv = views(hg[:, s_, :])
        dsts = [(uext[:, :, 0:15], gu, uext_b), (kT2[:, :, 0:128], gk, [kT2_b]), (vtok[:, 0, :], gv, [vtok_b[0]])]
        for (dst, src_, db) in dsts:
            if r == 0:
                P.op("dve", lambda e, dst=dst, src_=src_, r=r: e.tensor_scalar_mul(dst, src_, onehot[:, r:r + 1]),
                     reads=[hg_b[s_], const_b], writes=db)
            else:
                P.op("dve", lambda e, dst=dst, src_=src_, r=r: e.scalar_tensor_tensor(
                    dst, src_, onehot[:, r:r + 1], dst, ALU.mult, ALU.add),
                    reads=[hg_b[s_], const_b], writes=db)

    if stage == 2:
        dbg3 = nc.dram_tensor("dbg_u", [128, 4, 15 + TP], F32, kind="ExternalOutput").ap()
        P.dma("sp", lambda e: e.dma_start(out=dbg3[:, :, :], in_=uext), reads=uext_b)
        P.replay()
        es.close()
        return nc

    P.barrier()
    ar["off"] = mark_halo
    L = 15 + TP
    tmpA = rstd[:, 0:L]
    tmpB = sq[:, :, :].rearrange("p a b -> p (a b)").bitcast(F32)[:, 0:L]
    tmp_b = [rstd_b, Buf("tmpB")]
    poolw = aalloc([128, 4, 128], BF16)
    poolw_b = Buf("poolw")
    P.dma("pool", lambda e: e.dma_start(out=poolw, in_=I["l0_pool_w"].rearrange("g c d -> c g d")),
          writes=[poolw_b])
    pscale, pscale_b = load_cols(I["l0_pool_scale"].rearrange("(c p) -> c p", p=128), 4, "pscale")
    corr = aalloc([128, 4, 16], F32)
    corr_b = Buf("corr")
    P.dma("sp", lambda e: e.dma_start(out=corr, in_=I["c_poolcorr"][:, :, :]), writes=[corr_b])
    diff = aalloc([128, 1, NT], BF16)
    diff_b = [Buf("diff0"), Buf("diff0b")]
    diff_b[1] = diff_b[0]
    stp = aalloc([NS, 15, 128], F32)
    stp_b = Buf("stp")
    Hs = aalloc([NS, 512], F32)
    Hs_b = Buf("Hs")
    HsT = aalloc([128, 4, NS], F32)
    HsT_b = Buf("HsT")
    for g in range(4):
        w = 2 ** (g + 1)
        P.dma("sp", lambda e, g=g: e.dma_start(out=stp, in_=I["st_pool"][:, :, g * 128:(g + 1) * 128]),
              writes=[stp_b])
        P.op("dve", lambda e, g=g, w=w: e.tensor_reduce(
            Hs[:, g * 128:(g + 1) * 128], stp[:, 15 - (w - 1):15, :].rearrange("p r c -> p c r"),
            AX.X, ALU.add), reads=[stp_b], writes=[Hs_b])
    for g in range(4):
        pt, ptb = next_pst()
        P.group("pe", [lambda e, pt=pt, g=g: e.transpose(pt[:, 0:NS], Hs[0:NS, g * 128:(g + 1) * 128],
                                                         ident[0:NS, 0:NS])],
                reads=[Hs_b, ident_b], writes=[ptb])
        P.op("dve", lambda e, pt=pt, g=g: e.tensor_copy(HsT[:, g, :], pt[:, 0:NS]), reads=[ptb], writes=[HsT_b])
    for g in range(4):
        w = 2 ** (g + 1)
        d_ = 0
        cur = uext[:, g, :]
        curb = uext_b[g]
        off = 0
        for step in range(g + 1):
            sh = 2 ** step
            new_ = tmpA if step % 2 == 0 else tmpB
            newb = tmp_b[step % 2]
            P.op("dve", lambda e, new_=new_, cur=cur, off=off, sh=sh: e.tensor_tensor(
                new_[:, off + sh:L], cur[:, off + sh:L], cur[:, off:L - sh], ALU.add),
                reads=[curb], writes=[newb])
            cur, curb = new_, newb
            off += sh
        P.op("dve", lambda e, cur=cur, g=g: e.tensor_tensor(cur[:, 15:31], cur[:, 15:31], corr[:, g, :], ALU.mult),
             reads=[curb, corr_b], writes=[curb])
        P.op("dve", lambda e, cur=cur, g=g, w=w, d_=d_: e.scalar_tensor_tensor(
            diff[:, d_, 0:TP], cur[:, 15:L], 1.0 / w, uext[:, g, 15:L], ALU.mult, ALU.subtract),
            reads=[curb, uext_b[g]], writes=[diff_b[d_]])
        P.op("dve", lambda e, g=g: e.tensor_tensor(HsT[:, g, :], HsT[:, g, :], uS[:, g, :], ALU.add),
             reads=[HsT_b, uS_b], writes=[HsT_b])
        P.op("dve", lambda e, g=g, w=w, d_=d_: e.scalar_tensor_tensor(
            diff[:, d_, TP:NT], HsT[:, g, :], 1.0 / w, uS[:, g, :], ALU.mult, ALU.subtract),
            reads=[HsT_b, uS_b], writes=[diff_b[d_]])
        for ti, (t0, tn) in enumerate(TILES):
            a_, ab = next_acc()
            P.group("pe", [lambda e, a_=a_, g=g, d_=d_, t0=t0, tn=tn: e.matmul(
                a_[:, 0:tn], poolw[:, g, :], diff[:, d_, t0:t0 + tn], start=True, stop=True)],
                reads=[poolw_b, diff_b[d_]], writes=[ab])
            P.op("act", lambda e, a_=a_, g=g, t0=t0, tn=tn: e.mul(xn[:, g, t0:t0 + tn], a_[:, 0:tn], pscale[:, g:g + 1]),
                 reads=[ab, pscale_b], writes=[xn_b[g]])

    P.barrier()
    ar["off"] = mark_attn
    mk = aalloc([128, 3, 128], BF16)
    mk_b = Buf("mk")
    P.dma("sp", lambda e: e.dma_start(out=cst[:, :], in_=I["c_masks"][:, :]), writes=[cst_b])
    P.op("dve", lambda e: e.tensor_copy(mk[:, 0:2, :], cst[:, :].rearrange("p (a b) -> p a b", a=2)),
         reads=[cst_b], writes=[mk_b])
    P.op("dve", lambda e: e.tensor_scalar_mul(mk[:, 2, :], cst[:, 128:256], flag[:, 0:1]),
         reads=[cst_b, const_b], writes=[mk_b])
    es24 = aalloc([128, 24], F32)
    esP = aalloc([128, 12], F32)
    esP_b = Buf("esP")
    P.dma("sp", lambda e: e.dma_start(out=es24, in_=I["l0_sinks"].partition_broadcast(128)), writes=[esP_b])
    P.op("act", lambda e: e.activation(out=es24, in_=es24, func=AF.Exp), reads=[esP_b], writes=[esP_b])
    es3 = es24.rearrange("p (c two) -> p c two", two=2)
    P.op("dve", lambda e: e.tensor_copy(esP[0:64, :], es3[0:64, :, 0]), reads=[esP_b], writes=[esP_b])
    P.op("dve", lambda e: e.tensor_copy(esP[64:128, :], es3[64:128, :, 1]), reads=[esP_b], writes=[esP_b])

    pT = aalloc([128, 8, 384], BF16)
    pT_b = [Buf("pT%d" % i) for i in range(8)]
    den = aalloc([128, 2, 384], F32)
    den_b = [Buf("den0"), Buf("den1")]
    it = 0
    for qb in range(8):
        for g in range(4):
            po, pob = next_pst()
            pd, pdb = next_pst()
            for half in range(2):
                p0 = 64 * half
                pts = []
                for kb in range(2):
                    kc0 = qb * 128 + kb * 128
                    a_, ab = next_acc()
                    P.group("pe", [lambda e, a_=a_, p0=p0, g=g, kc0=kc0, qb=qb: e.matmul(
                        a_[:, 0:384].rearrange("p (a b) -> p a b", a=3),
                        kT2[p0:p0 + 64, g, kc0:kc0 + 128],
                        qT[p0:p0 + 64, 3 * g:3 * g + 3, qb * 128:(qb + 1) * 128], start=True, stop=True)],
                        reads=[kT2_b] + qT_b[3 * g:3 * g + 3], writes=[ab])
                    pi_ = (it % 2) * 4 + half * 2 + kb
                    pv_ = pT[:, pi_, :]
                    P.op("act", lambda e, a_=a_, pv_=pv_: e.activation(out=pv_, in_=a_[:, 0:384], func=AF.Exp),
                         reads=[ab], writes=[pT_b[pi_]])
                    mi = 0 if kb == 1 else (2 if qb == 0 else 1)
                    pv3 = pv_.rearrange("p (a b) -> p a b", a=3)
                    P.op("dve", lambda e, pv3=pv3, mi=mi: e.tensor_tensor(
                        pv3, pv3, mk[:, mi, :].unsqueeze(1).broadcast_to([128, 3, 128]), ALU.mult),
                        reads=[mk_b], writes=[pT_b[pi_]])
                    pts.append((pv_, pT_b[pi_]))
                fns = []
                for kb in range(2):
                    fns.append(lambda e, po=po, p0=p0, kb=kb, qb=qb, g=g, pv_=pts[kb][0]: e.matmul(
                        po[p0:p0 + 64, 0:384], vtok[:, qb + kb, g * 64:(g + 1) * 64], pv_,
                        start=(kb == 0), stop=(kb == 1)))
                P.group("pe", fns, reads=[pts[0][1], pts[1][1], vtok_b[qb], vtok_b[qb + 1]], writes=[pob])
                fns = []
                for kb in range(2):
                    fns.append(lambda e, pd=pd, p0=p0, kb=kb, pv_=pts[kb][0]: e.matmul(
                        pd[p0:p0 + 64, 0:384], ones_bf[:, 0:64], pv_, start=(kb == 0), stop=(kb == 1)))
                P.group("pe", fns, reads=[pts[0][1], pts[1][1], const_b], writes=[pdb])
            d_ = it % 2
            dv = den[:, d_, :].rearrange("p (a b) -> p a b", a=3)
            P.op("dve", lambda e, dv=dv, pd=pd, g=g: e.tensor_tensor(
                dv, pd[:, 0:384].rearrange("p (a b) -> p a b", a=3),
                esP[:, 3 * g:3 * g + 3].unsqueeze(2).broadcast_to([128, 3, 128]), ALU.add),
                reads=[pdb, esP_b], writes=[den_b[d_]])
            P.op("dve", lambda e, dv=dv: e.reciprocal(dv, dv), reads=[den_b[d_]], writes=[den_b[d_]])
            P.op("dve", lambda e, dv=dv, po=po, g=g, qb=qb: e.tensor_tensor(
                xn[:, 4 + 3 * g:4 + 3 * g + 3, qb * 128:(qb + 1) * 128],
                po[:, 0:384].rearrange("p (a b) -> p a b", a=3), dv, ALU.mult),
                reads=[pob, den_b[d_]], writes=xn_b[4 + 3 * g:4 + 3 * g + 3])
            it += 1

    kdup = aalloc([128, 2, 4, 128], F32)
    kdup_b = [Buf("kdup0"), Buf("kdup1")]
    KcT = aalloc([128, 2, 4, 128], BF16)
    KcT_b = [Buf("KcT0"), Buf("KcT1")]
    vcS = aalloc([128, NS, 256], BF16)
    vcS_b = Buf("vcS")
    P.dma("pool", lambda e: e.dma_start(out=vcS, in_=I["cv"].rearrange("b k f -> k b f")), writes=[vcS_b])
    psS, psS_b = next_acc()
    for b_ in range(NS):
        s_ = b_ % 2
        kd4 = kdup[:, s_, :, :].rearrange("p g (two d) -> p g two d", two=2)
        for two in range(2):
            P.dma("sp", lambda e, b_=b_, two=two, kd4=kd4: e.dma_start(
                out=kd4[:, :, two, :], in_=I["ck"][b_, :, :].rearrange("k (g d) -> k g d", g=4)),
                writes=[kdup_b[s_]])
        pt, ptb = next_pst()
        P.group("pe", [lambda e, pt=pt, g=g, s_=s_: e.transpose(pt[:, g * 128:(g + 1) * 128], kdup[:, s_, g, :],
                                                                ident[:, :]) for g in range(4)],
                reads=[kdup_b[s_], ident_b], writes=[ptb])
        P.op("act", lambda e, pt=pt, s_=s_: e.copy(KcT[:, s_, :, :], pt[:, :].rearrange("p (g k) -> p g k", g=4)),
             reads=[ptb], writes=[KcT_b[s_]])
        fns = []
        for half in range(2):
            p0 = 64 * half
            for g in range(4):
                c0 = ((half * 4 + g) * NS + b_) * 3
                fns.append(lambda e, p0=p0, g=g, c0=c0, s_=s_, b_=b_: e.matmul(
                    psS[:, c0:c0 + 3], KcT[p0:p0 + 64, s_, g, :], qT[p0:p0 + 64, 3 * g:3 * g + 3, TP + b_],
                    start=True, stop=True))
        P.group("pe", fns, reads=[KcT_b[s_]] + qT_b, writes=[psS_b])
    pS = aalloc([128, 384], BF16)
    pS_b = Buf("pS")
    P.op("act", lambda e: e.activation(out=pS, in_=psS[:, 0:384], func=AF.Exp), reads=[psS_b], writes=[pS_b])
    po2, po2b = next_pst()
    pd2, pd2b = next_pst()
    fo, fd = [], []
    for b_ in range(NS):
        for half in range(2):
            p0 = 64 * half
            for g in range(4):
                c0 = ((half * 4 + g) * NS + b_) * 3
                o0 = (g * NS + b_) * 3
                fo.append(lambda e, p0=p0, g=g, c0=c0, o0=o0, b_=b_: e.matmul(
                    po2[p0:p0 + 64, o0:o0 + 3], vcS[:, b_, g * 64:(g + 1) * 64], pS[:, c0:c0 + 3],
                    start=True, stop=True))
                fd.append(lambda e, p0=p0, c0=c0, o0=o0: e.matmul(
                    pd2[p0:p0 + 64, o0:o0 + 3], ones_bf[:, 0:64], pS[:, c0:c0 + 3], start=True, stop=True))
    P.group("pe", fo, reads=[pS_b, vcS_b], writes=[po2b])
    P.group("pe", fd, reads=[pS_b, const_b], writes=[pd2b])
    prod = aalloc([128, 12, NS], BF16)
    prod_b = Buf("prod")
    for g in range(4):
        P.op("dve", lambda e, g=g: e.tensor_tensor(
            prod[:, 3 * g:3 * g + 3, :], qT[:, 3 * g:3 * g + 3, TP:NT],
            kT2[:, g, 128 + TP:TK].unsqueeze(1).broadcast_to([128, 3, NS]), ALU.mult),
            reads=qT_b[3 * g:3 * g + 3] + [kT2_b], writes=[prod_b])
    psN, psN_b = next_acc()
    P.group("pe", [lambda e: e.matmul(psN[:, 0:12 * NS], bdones[:, :], prod.rearrange("p a b -> p (a b)"),
                                      start=True, stop=True)], reads=[prod_b, const_b], writes=[psN_b])
    pnew = aalloc([128, 12, NS], F32)
    onew = aalloc([128, 12, NS], F32)
    pn_b = Buf("pnew")
    P.op("act", lambda e: e.activation(out=pnew.rearrange("p a b -> p (a b)"), in_=psN[:, 0:12 * NS], func=AF.Exp),
         reads=[psN_b], writes=[pn_b])
    for g in range(4):
        P.op("dve", lambda e, g=g: e.tensor_tensor(
            onew[:, 3 * g:3 * g + 3, :], pnew[:, 3 * g:3 * g + 3, :],
            vT2S[:, g, :].unsqueeze(1).broadcast_to([128, 3, NS]), ALU.mult),
            reads=[pn_b, vT2S_b], writes=[pn_b])
        pov = po2[:, g * 48:(g + 1) * 48].rearrange("p (b j) -> p j b", j=3)
        pdv = pd2[:, g * 48:(g + 1) * 48].rearrange("p (b j) -> p j b", j=3)
        P.op("dve", lambda e, g=g, pov=pov: e.tensor_tensor(
            onew[:, 3 * g:3 * g + 3, :], pov, onew[:, 3 * g:3 * g + 3, :], ALU.add),
            reads=[po2b, pn_b], writes=[pn_b])
        P.op("dve", lambda e, g=g, pdv=pdv: e.tensor_tensor(
            pnew[:, 3 * g:3 * g + 3, :], pdv, pnew[:, 3 * g:3 * g + 3, :], ALU.add),
            reads=[pd2b, pn_b], writes=[pn_b])
        P.op("dve", lambda e, g=g: e.tensor_tensor(
            pnew[:, 3 * g:3 * g + 3, :], pnew[:, 3 * g:3 * g + 3, :],
            esP[:, 3 * g:3 * g + 3].unsqueeze(2).broadcast_to([128, 3, NS]), ALU.add),
            reads=[esP_b, pn_b], writes=[pn_b])
        P.op("dve", lambda e, g=g: e.reciprocal(pnew[:, 3 * g:3 * g + 3, :], pnew[:, 3 * g:3 * g + 3, :]),
             reads=[pn_b], writes=[pn_b])
        P.op("dve", lambda e, g=g: e.tensor_tensor(
            xn[:, 4 + 3 * g:4 + 3 * g + 3, TP:NT], onew[:, 3 * g:3 * g + 3, :], pnew[:, 3 * g:3 * g + 3, :], ALU.mult),
            reads=[pn_b], writes=xn_b[4 + 3 * g:4 + 3 * g + 3])

    if stage == 3:
        dbg = nc.dram_tensor("dbg_mix", [128, 16, NT], BF16, kind="ExternalOutput").ap()
        P.dma("sp", lambda e: e.dma_start(out=dbg[:, :, :], in_=xn[:, :, :]), reads=xn_b)
        P.replay()
        es.close()
        return nc

    def load_panel_rows(Wd, row0, KC, c0, w):
        i = rr["wp"] % 2
        rr["wp"] += 1
        pv = wp[i][:, 0:KC * w].rearrange("p (k n) -> p k n", k=KC)
        src = Wd[row0:row0 + KC * 128, c0:c0 + w].rearrange("(kc p) n -> p kc n", p=128)
        P.dma("pool", lambda e: e.dma_start(out=pv, in_=src), writes=[wp_b[i]])
        return pv, wp_b[i]

    def proj_residual_rows(Wd, row0, KC, rhs, rhs_b):
        for pi in range(8):
            pv, pvb = load_panel_rows(Wd, row0, KC, pi * 256, 256)
            for m in range(2):
                c = pi * 2 + m

                def evac_res(a, ab, ti, t0, tn, c=c):
                    P.op("dve", lambda e, a=a, t0=t0, tn=tn, c=c: e.tensor_tensor(
                        xT[:, c, t0:t0 + tn], a[:, 0:tn], xT[:, c, t0:t0 + tn], ALU.add),
                        reads=[ab], writes=[xT_b[c]])
                gemm_fm(pv, pvb, KC, [(m * 128, 128, 0)], rhs, rhs_b, evac_res)

    def proj_residual(Wd, rhs=None, rhs_b=None):
        rhs = rhs or xn_rhs
        rhs_b = rhs_b or xn_b
        for pi in range(8):
            pv, pvb = load_panel(Wd, 16, pi * 256, 256)
            for m in range(2):
                c = pi * 2 + m

                def evac_res(a, ab, ti, t0, tn, c=c):
                    P.op("dve", lambda e, a=a, t0=t0, tn=tn, c=c: e.tensor_tensor(
                        xT[:, c, t0:t0 + tn], a[:, 0:tn], xT[:, c, t0:t0 + tn], ALU.add),
                        reads=[ab], writes=[xT_b[c]])
                gemm_fm(pv, pvb, 16, [(m * 128, 128, 0)], rhs, rhs_b, evac_res)
    proj_residual(I["l0_w_out"])
    P.barrier()
    ar["off"] = 0

    if stage == 4:
        dbg = nc.dram_tensor("dbg_x", [128, 16, NT], F32, kind="ExternalOutput").ap()
        P.dma("sp", lambda e: e.dma_start(out=dbg[:, :, :], in_=xT[:, :, :]), reads=xT_b)
        P.replay()
        es.close()
        return nc

    xh = sb("xh", [128, 16, 3], F32)
    xnh = sb("xnh", [128, 16, 3], BF16)
    xh_b, xnh_b = Buf("xh"), Buf("xnh")
    rsth = sb("rsth", [128, 3], F32)
    sqh = sb("sqh", [128, 16, 3], BF16)
    xex = {"n": 0}

    def exchange_x():
        n = xex["n"]
        xex["n"] += 1
        src_t = aalloc([128, 48], F32)
        gat = aalloc([128, 2, 48], F32)
        gat_b = [Buf("gat0"), Buf("gat1")]
        src_b = Buf("xsrc")
        bi = nc.dram_tensor("xh_in%d" % n, [128, 48], F32)
        bo = nc.dram_tensor("xh_out%d" % n, [NCORES * 128, 48], F32)
        bib, bob = Buf("bi"), Buf("bo")
        P.op("dve", lambda e: e.tensor_copy(src_t.rearrange("p (c t) -> p c t", c=16), xT[:, :, TP - 3:TP]),
             reads=xT_b, writes=[src_b])
        P.dma("sp", lambda e: e.dma_start(out=bi.ap(), in_=src_t), reads=[src_b], writes=[bib])
        P.collective(lambda e: e.collective_compute(
            "AllGather", ALU.bypass, replica_groups=[list(range(NCORES))],
            ins=[bi.ap().opt()], outs=[bo.ap().opt()]), reads=[bib], writes=[bob])
        dst = xh[:, :, :].rearrange("p c t -> p (c t)")
        for r in range(NCORES):
            s_ = r % 2
            P.dma("sp", lambda e, r=r, s_=s_: e.dma_start(out=gat[:, s_, :], in_=bo.ap()[r * 128:(r + 1) * 128, :]),
                  reads=[bob], writes=[gat_b[s_]])
            if r == 0:
                P.op("dve", lambda e, s_=s_, r=r: e.tensor_scalar_mul(dst, gat[:, s_, :], onehot[:, r:r + 1]),
                     reads=[gat_b[s_], const_b], writes=[xh_b])
            else:
                P.op("dve", lambda e, s_=s_, r=r: e.scalar_tensor_tensor(
                    dst, gat[:, s_, :], onehot[:, r:r + 1], dst, ALU.mult, ALU.add),
                    reads=[gat_b[s_], const_b], writes=[xh_b])

    def rmsnorm_halo(wname, wcol, wcol_b):
        P.op("act", lambda e: e.activation(out=sqh[:, :, :], in_=xh[:, :, :], func=AF.Square),
             reads=[xh_b], writes=[xnh_b])
        a_, ab = next_acc()
        P.group("pe", [lambda e, a_=a_, c=c: e.matmul(a_[:, 0:3], ones_bf[:], sqh[:, c, :], start=(c == 0),
                                                    stop=(c == 15)) for c in range(16)],
                reads=[xnh_b, const_b], writes=[ab])
        P.op("act", lambda e, a_=a_: e.activation(out=rsth[:, :], in_=a_[:, 0:3], func=AF.Sqrt, bias=epsc[:, 0:1],
                                                  scale=1.0 / D), reads=[ab, const_b], writes=[xnh_b])
        P.op("dve", lambda e: e.reciprocal(rsth[:, :], rsth[:, :]), reads=[xnh_b], writes=[xnh_b])
        for c in range(16):
            P.op("dve", lambda e, c=c: e.scalar_tensor_tensor(
                xnh[:, c, :], xh[:, c, :], wcol[:, c:c + 1], rsth[:, :], ALU.mult, ALU.mult),
                reads=[xh_b, wcol_b, xnh_b], writes=[xnh_b])

    def ffn(Lx):
        pre = "l%d_" % Lx
        Wup, Wdn = I[pre + "ffn_w_up"], I[pre + "ffn_w_down"]
        st_in = I["st_ffn%d" % Lx]
        o_p, o_s = "ffn%d_p" % Lx, "ffn%d_s" % Lx
        P.barrier()
        ar["off"] = 0
        exchange_x()
        rmsnorm(pre + "norm_ffn")
        wcol, wcol_b = load_cols(I[pre + "norm_ffn"].rearrange("(c p) -> c p", p=128), 16, "wch%d" % Lx)
        rmsnorm_halo(pre + "norm_ffn", wcol, wcol_b)
        cw, cw_b = load_cols(I[pre + "ffn_conv"].rearrange("j (c p) -> (j c) p", p=128), 264, "cw%d" % Lx)
        P.dma("sp", lambda e: e.dma_start(out=O[o_s][:, 0, :], in_=st_in[:, 1, :]), writes=[OB[o_s]])
        hid = aalloc([128, 12, NT], BF16)
        hid_b = [Buf("hid%d" % i) for i in range(12)]
        U = [aalloc([128, 2 + TP], F32) for _ in range(2)]
        U_b = [Buf("Ua"), Buf("Ub")]
        US = [aalloc([128, 18], F32) for _ in range(2)]
        US_b = [Buf("USa"), Buf("USb")]
        CV = [aalloc([128, NT], F32) for _ in range(2)]
        CV_b = [Buf("ca"), Buf("cb")]
        hst = aalloc([NS, 2, 2, 256], F32)
        hst_b = [Buf("hsta"), Buf("hstb")]
        hT = aalloc([128, 2, 4 * NS], F32)
        hT_b = [Buf("hTa"), Buf("hTb")]
        ost = aalloc([18, 2, 256], F32)
        ost_b = [Buf("osta"), Buf("ostb")]
        FT = TILES + [(-1, 2)]

        def rhs_f(kc, t0, tn):
            if t0 < 0:
                return xnh[:, kc, 1:3]
            return xn[:, kc, t0:t0 + tn]

        gs = [12, 12, 12, 8]
        j_base = 0
        for gsz in gs:
            for jp in range(0, gsz, 2):
                j0 = j_base + jp
                pvs = []
                for h_ in range(2):
                    colbase = h_ * DFF + j0 * 128
                    pvs.append(load_panel(Wup, 16, colbase, 256))
                    P.dma("sp", lambda e, colbase=colbase, h_=h_: e.dma_start(
                        out=hst[:, h_, :, :], in_=st_in[:, :, colbase:colbase + 256]), writes=[hst_b[h_]])
                    pt, ptb = next_pst()
                    P.group("pe", [lambda e, pt=pt, r=r, m=m, h_=h_: e.transpose(
                        pt[:, (r * 2 + m) * NS:(r * 2 + m + 1) * NS], hst[0:NS, h_, r, m * 128:(m + 1) * 128],
                        ident[0:NS, 0:NS]) for r in range(2) for m in range(2)],
                        reads=[hst_b[h_], ident_b], writes=[ptb])
                    P.op("act", lambda e, pt=pt, h_=h_: e.copy(hT[:, h_, :], pt[:, 0:4 * NS]),
                         reads=[ptb], writes=[hT_b[h_]])
                for m in range(2):
                    j = j0 + m
                    jj = jp + m
                    for h_ in range(2):
                        pv, pvb = pvs[h_]
                        ceng = "dve"
                        Uh, Ub, USh, USb, CVh, CVb = U[h_], U_b[h_], US[h_], US_b[h_], CV[h_], CV_b[h_]

                        def evac_up(a, ab, ti, t0, tn, Uh=Uh, Ub=Ub, USh=USh, USb=USb):
                            eng = evac_eng()
                            if ti < 2:
                                P.op(eng, copy_fn(eng, Uh[:, 2 + t0:2 + t0 + tn], a[:, 0:tn]), reads=[ab], writes=[Ub])
                            elif ti == 2:
                                P.op(eng, copy_fn(eng, USh[:, 0:NS], a[:, 0:NS]), reads=[ab], writes=[USb])
                            else:
                                P.op(eng, copy_fn(eng, Uh[:, 0:2], a[:, 0:2]), reads=[ab], writes=[Ub])
                        gemm_fm(pv, pvb, 16, [(m * 128, 128, 0)], rhs_f, xn_b + [xnh_b], evac_up, tiles=FT)
                        wi = [cw[:, t * 88 + h_ * 44 + j:t * 88 + h_ * 44 + j + 1] for t in range(3)]
                        P.op(ceng, lambda e, CVh=CVh, Uh=Uh, w=wi[0]: e.tensor_scalar_mul(CVh[:, 0:TP], Uh[:, 0:TP], w),
                             reads=[Ub, cw_b], writes=[CVb])
                        P.op(ceng, lambda e, CVh=CVh, Uh=Uh, w=wi[1]: e.scalar_tensor_tensor(
                            CVh[:, 0:TP], Uh[:, 1:TP + 1], w, CVh[:, 0:TP], ALU.mult, ALU.add),
                            reads=[Ub, cw_b], writes=[CVb])
                        P.op(ceng, lambda e, CVh=CVh, Uh=Uh, w=wi[2]: e.scalar_tensor_tensor(
                            CVh[:, 0:TP], Uh[:, 2:TP + 2], w, CVh[:, 0:TP], ALU.mult, ALU.add),
                            reads=[Ub, cw_b], writes=[CVb])
                        h0 = hT[:, h_, (0 * 2 + m) * NS:(0 * 2 + m + 1) * NS]
                        h1 = hT[:, h_, (1 * 2 + m) * NS:(1 * 2 + m + 1) * NS]
                        P.op(ceng, lambda e, CVh=CVh, h0=h0, w=wi[0]: e.tensor_scalar_mul(CVh[:, TP:NT], h0, w),
                             reads=[hT_b[h_], cw_b], writes=[CVb])
                        P.op(ceng, lambda e, CVh=CVh, h1=h1, w=wi[1]: e.scalar_tensor_tensor(
                            CVh[:, TP:NT], h1, w, CVh[:, TP:NT], ALU.mult, ALU.add),
                            reads=[hT_b[h_], cw_b], writes=[CVb])
                        P.op(ceng, lambda e, CVh=CVh, USh=USh, w=wi[2]: e.scalar_tensor_tensor(
                            CVh[:, TP:NT], USh[:, 0:NS], w, CVh[:, TP:NT], ALU.mult, ALU.add),
                            reads=[USb, cw_b], writes=[CVb])
                        P.op(ceng, lambda e, USh=USh, Uh=Uh: e.tensor_copy(USh[:, NS:18], Uh[:, TP:TP + 2]),
                             reads=[Ub], writes=[USb])
                        pt2, pt2b = next_pst()
                        P.group("pe", [lambda e, pt2=pt2, USh=USh: e.transpose(pt2[0:18, 0:128], USh[:, :], ident[:, :])],
                                reads=[USb, ident_b], writes=[pt2b])
                        P.op("act", lambda e, pt2=pt2, m=m, h_=h_: e.copy(ost[0:18, h_, m * 128:(m + 1) * 128],
                                                                        pt2[0:18, 0:128]),
                             reads=[pt2b], writes=[ost_b[h_]])
                        if m == 1:
                            colbase = h_ * DFF + j0 * 128
                            P.dma("sp", lambda e, colbase=colbase, h_=h_: e.dma_start(
                                out=O[o_s][:, 1, colbase:colbase + 256], in_=ost[0:NS, h_, :]),
                                reads=[ost_b[h_]], writes=[OB[o_s]])
                            P.dma("sp", lambda e, colbase=colbase, h_=h_: e.dma_start(
                                out=O[o_p][:, colbase:colbase + 256], in_=ost[NS:18, h_, :]),
                                reads=[ost_b[h_]], writes=[OB[o_p]])
                    P.op("act", lambda e: e.activation(out=CV[0][:, :], in_=CV[0][:, :], func=AF.Silu),
                         reads=[CV_b[0]], writes=[CV_b[0]])
                    P.op("dve", lambda e, jj=jj: e.tensor_tensor(hid[:, jj, :], CV[0][:, :], CV[1][:, :], ALU.mult),
                         reads=[CV_b[0], CV_b[1]], writes=[hid_b[jj]])
            for pi in range(8):
                pv, pvb = load_panel_rows(Wdn, j_base * 128, gsz, pi * 256, 256)
                for m in range(2):
                    c = pi * 2 + m

                    def evac_res(a, ab, ti, t0, tn, c=c):
                        P.op("dve", lambda e, a=a, t0=t0, tn=tn, c=c: e.tensor_tensor(
                            xT[:, c, t0:t0 + tn], a[:, 0:tn], xT[:, c, t0:t0 + tn], ALU.add),
                            reads=[ab], writes=[xT_b[c]])
                    gemm_fm(pv, pvb, gsz, [(m * 128, 128, 0)],
                            lambda kc, t0, tn: hid[:, kc, t0:t0 + tn], hid_b[0:gsz], evac_res)
            j_base += gsz

    ffn(0)
    if KSTOP == "x1":
        maybe_stop("x1", xT[:, :, :], xT_b, [128, 16, NT], F32)

    L1 = {}

    def layer1_inproj():
        P.barrier()
        ar["off"] = 0
        exchange_x()
        rmsnorm("l1_norm_mix")
        wcol, wcol_b = load_cols(I["l1_norm_mix"].rearrange("(c p) -> c p", p=128), 16, "wch_l1")
        rmsnorm_halo("l1_norm_mix", wcol, wcol_b)
        W1 = I["l1_w_in"]
        mark1 = ar["off"]
        gw, gw_b = load_cols(I["l1_gdn_conv"].rearrange("j (c p) -> (j c) p", p=128), 144, "gw")
        sw, sw_b = load_cols(I["l1_sconv_w"].rearrange("j (c p) -> (j c) p", p=128), 12, "sw")
        P.dma("sp", lambda e: e.dma_start(out=O["gconv_s"][:, 0:2, :], in_=I["st_gconv"][:, 1:3, :]),
              writes=[OB["gconv_s"]])
        P.dma("sp", lambda e: e.dma_start(out=O["sconv_s"][:, 0, :], in_=I["st_sconv"][:, 1, :]),
              writes=[OB["sconv_s"]])
        U = aalloc([128, 3 + TP], F32)
        U_b = Buf("U1")
        US = aalloc([128, 2, 19], F32)
        US_b = [Buf("US1a"), Buf("US1b")]
        ost = aalloc([19, 256], F32)
        ost_b = Buf("ost1")
        hst = aalloc([NS, 3, 256], F32)
        hst_b = Buf("hst1")
        hT = aalloc([128, 6 * NS], F32)
        hT_b = Buf("hT1")
        CVx = aalloc([128, NT], F32)
        CV_b = Buf("cv1")
        FT = TILES + [(-1, 3)]

        def rhs_f(kc, t0, tn):
            if t0 < 0:
                return xnh[:, kc, 0:3]
            return xn[:, kc, t0:t0 + tn]

        for pi in range(18):
            colbase = pi * 256
            pv, pvb = load_panel(W1, 16, colbase, 256)
            P.dma("sp", lambda e, colbase=colbase: e.dma_start(out=hst[:, :, :], in_=I["st_gconv"][:, :, colbase:colbase + 256]),
                  writes=[hst_b])
            pt, ptb = next_pst()
            P.group("pe", [lambda e, pt=pt, r=r, m=m: e.transpose(
                pt[:, (r * 2 + m) * NS:(r * 2 + m + 1) * NS], hst[0:NS, r, m * 128:(m + 1) * 128],
                ident[0:NS, 0:NS]) for r in range(3) for m in range(2)], reads=[hst_b, ident_b], writes=[ptb])
            P.op("act", lambda e, pt=pt: e.copy(hT[:, :], pt[:, 0:6 * NS]), reads=[ptb], writes=[hT_b])
            for m in range(2):
                ci = pi * 2 + m
                s_ = m

                def evac_up(a, ab, ti, t0, tn, s_=s_):
                    eng = evac_eng()
                    if ti < 2:
                        P.op(eng, copy_fn(eng, U[:, 3 + t0:3 + t0 + tn], a[:, 0:tn]), reads=[ab], writes=[U_b])
                    elif ti == 2:
                        P.op(eng, copy_fn(eng, US[:, s_, 0:NS], a[:, 0:NS]), reads=[ab], writes=[US_b[s_]])
                    else:
                        P.op(eng, copy_fn(eng, U[:, 0:3], a[:, 0:3]), reads=[ab], writes=[U_b])
                gemm_fm(pv, pvb, 16, [(m * 128, 128, 0)], rhs_f, xn_b + [xnh_b], evac_up, tiles=FT)
                P.op("pool", lambda e, s_=s_: e.tensor_copy(US[:, s_, NS:19], U[:, TP:TP + 3]),
                     reads=[U_b], writes=[US_b[s_]])
                pt2, pt2b = next_pst()
                P.group("pe", [lambda e, pt2=pt2, s_=s_: e.transpose(pt2[0:19, 0:128], US[:, s_, :], ident[:, :])],
                        reads=[US_b[s_], ident_b], writes=[pt2b])
                P.op("act", lambda e, pt2=pt2, m=m: e.copy(ost[0:19, m * 128:(m + 1) * 128], pt2[0:19, 0:128]),
                     reads=[pt2b], writes=[ost_b])
            P.dma("sp", lambda e, colbase=colbase: e.dma_start(
                out=O["gconv_s"][:, 2, colbase:colbase + 256], in_=ost[0:NS, :]), reads=[ost_b], writes=[OB["gconv_s"]])
            P.dma("sp", lambda e, colbase=colbase: e.dma_start(
                out=O["gconv_p"][:, colbase:colbase + 256], in_=ost[NS:19, :]), reads=[ost_b], writes=[OB["gconv_p"]])

        P.barrier()
        ar["off"] = mark1
        CVx = aalloc([128, NT], F32)
        CV_b = Buf("cv1b")
        ysc = aalloc([128, 4, NT], BF16)
        ysc_b = [Buf("ysc%d" % c_) for c_ in range(4)]
        L1["mark_ysc"] = ar["off"]
        o2 = 4608 + 1536 + 24
        sB = aalloc([128, 4, NT], BF16)
        sB_b = Buf("sB")
        scu = aalloc([128, 4, 2 + TP], F32)
        scu_b = Buf("scu")
        scS = aalloc([128, 4, NS], F32)
        scS_b = Buf("scS")
        for part in range(3):
            for half in range(2):
                colbase = o2 + part * 512 + half * 256
                pv, pvb = load_panel(W1, 16, colbase, 256)
                for m in range(2):
                    cc = half * 2 + m

                    def evac_sc(a, ab, ti, t0, tn, part=part, cc=cc):
                        if part == 0:
                            if ti < 3:
                                eng = evac_eng()
                                P.op(eng, copy_fn(eng, sB[:, cc, t0:t0 + tn], a[:, 0:tn]), reads=[ab], writes=[sB_b])
                            return
                        if ti < 2:
                            dst = scu[:, cc, 2 + t0:2 + t0 + tn]
                            src = a[:, 0:tn]
                            wb = scu_b
                        elif ti == 2:
                            dst = scS[:, cc, :]
                            src = a[:, 0:NS]
                            wb = scS_b
                        else:
                            dst = scu[:, cc, 0:2]
                            src = a[:, 1:3]
                            wb = scu_b
                        if part == 1:
                            eng = evac_eng()
                            P.op(eng, copy_fn(eng, dst, src), reads=[ab], writes=[wb])
                        else:
                            P.op("dve", lambda e, dst=dst, src=src: e.tensor_tensor(dst, src, dst, ALU.mult),
                                 reads=[ab], writes=[wb])
                    gemm_fm(pv, pvb, 16, [(m * 128, 128, 0)], rhs_f, xn_b + [xnh_b], evac_sc, tiles=FT)
        hs2 = aalloc([NS, 2, 512], F32)
        hs2_b = Buf("hs2")
        P.dma("sp", lambda e: e.dma_start(out=hs2[:, :, :], in_=I["st_sconv"][:, :, :]), writes=[hs2_b])
        hT2 = aalloc([128, 2, 4, NS], F32)
        hT2_b = Buf("hT2")
        US2 = aalloc([128, 4, 18], F32)
        US2_b = Buf("US2")
        ost2 = aalloc([18, 512], F32)
        ost2_b = Buf("ost2")
        pt, ptb = next_pst()
        P.group("pe", [lambda e, pt=pt, r=r, c=c: e.transpose(
            pt[:, (r * 4 + c) * NS:(r * 4 + c + 1) * NS], hs2[0:NS, r, c * 128:(c + 1) * 128], ident[0:NS, 0:NS])
            for r in range(2) for c in range(4)], reads=[hs2_b, ident_b], writes=[ptb])
        P.op("act", lambda e, pt=pt: e.copy(hT2.rearrange("p r c s -> p (r c s)"), pt[:, 0:8 * NS]),
             reads=[ptb], writes=[hT2_b])
        P.op("dve", lambda e: e.tensor_copy(US2[:, :, 0:NS], scS[:, :, :]), reads=[scS_b], writes=[US2_b])
        P.op("dve", lambda e: e.tensor_copy(US2[:, :, NS:18], scu[:, :, TP:TP + 2]), reads=[scu_b], writes=[US2_b])
        pt2, pt2b = next_pst()
        P.group("pe", [lambda e, pt2=pt2, c=c: e.transpose(pt2[0:18, c * 128:(c + 1) * 128], US2[:, c, :], ident[:, :])
                       for c in range(4)], reads=[US2_b, ident_b], writes=[pt2b])
        P.op("act", lambda e, pt2=pt2: e.copy(ost2[0:18, :], pt2[0:18, 0:512]), reads=[pt2b], writes=[ost2_b])
        P.dma("sp", lambda e: e.dma_start(out=O["sconv_s"][:, 1, :], in_=ost2[0:NS, :]), reads=[ost2_b],
              writes=[OB["sconv_s"]])
        P.dma("sp", lambda e: e.dma_start(out=O["sconv_p"][:, :], in_=ost2[NS:18, :]), reads=[ost2_b],
              writes=[OB["sconv_p"]])
        for c in range(4):
            wi = [sw[:, t * 4 + c:t * 4 + c + 1] for t in range(3)]
            P.op("dve", lambda e, c=c, w=wi[0]: e.tensor_scalar_mul(CVx[:, 0:TP], scu[:, c, 0:TP], w),
                 reads=[scu_b, sw_b], writes=[CV_b])
            P.op("dve", lambda e, c=c, w=wi[1]: e.scalar_tensor_tensor(
                CVx[:, 0:TP], scu[:, c, 1:TP + 1], w, CVx[:, 0:TP], ALU.mult, ALU.add),
                reads=[scu_b, sw_b], writes=[CV_b])
            P.op("dve", lambda e, c=c, w=wi[2]: e.scalar_tensor_tensor(
                CVx[:, 0:TP], scu[:, c, 2:TP + 2], w, CVx[:, 0:TP], ALU.mult, ALU.add),
                reads=[scu_b, sw_b], writes=[CV_b])
            P.op("dve", lambda e, c=c, w=wi[0]: e.tensor_scalar_mul(CVx[:, TP:NT], hT2[:, 0, c, :], w),
                 reads=[hT2_b, sw_b], writes=[CV_b])
            P.op("dve", lambda e, c=c, w=wi[1]: e.scalar_tensor_tensor(
                CVx[:, TP:NT], hT2[:, 1, c, :], w, CVx[:, TP:NT], ALU.mult, ALU.add),
                reads=[hT2_b, sw_b], writes=[CV_b])
            P.op("dve", lambda e, c=c, w=wi[2]: e.scalar_tensor_tensor(
                CVx[:, TP:NT], scS[:, c, :], w, CVx[:, TP:NT], ALU.mult, ALU.add),
                reads=[scS_b, sw_b], writes=[CV_b])
            P.op("dve", lambda e, c=c: e.tensor_tensor(ysc[:, c, :], CVx[:, :], sB[:, c, :], ALU.mult),
                 reads=[CV_b, sB_b], writes=[ysc_b[c]])
        L1["gw"], L1["gw_b"], L1["mark1"], L1["FT"] = gw, gw_b, mark1, FT
        L1["ysc"], L1["ysc_b"] = ysc, ysc_b
        L1["mark2"] = ar["off"]

    layer1_inproj()

    def layer1_gdn():
        gw, gw_b, FT = L1["gw"], L1["gw_b"], L1["FT"]
        ysc, ysc_b = L1["ysc"], L1["ysc_b"]
        W1 = I["l1_w_in"]
        P.barrier()
        ar["off"] = L1["mark_ysc"]
        yh = aalloc([128, NT], BF16)
        yh_b = Buf("yh")

        def rhs3(kc, t0, tn):
            if t0 < 0:
                return xnh[:, kc, 0:3]
            return xn[:, kc, t0:t0 + tn]

        onesf = aalloc([128, 128], F32)
        gm = aalloc([128, 2, 128], F32)
        selfhot = aalloc([128, 8], F32)
        gnw, gnw_b = load_cols(I["l1_gdn_norm"].rearrange("(c p) -> c p", p=128), 1, "gnw")
        cg_b = Buf("cg")
        P.op("dve", lambda e: e.memset(onesf, 1.0), writes=[cg_b])
        P.dma("sp", lambda e: e.dma_start(out=gm.rearrange("p a b -> p (a b)"), in_=I["c_gmasks"][:, :]), writes=[cg_b])
        P.dma("sp", lambda e: e.dma_start(out=selfhot, in_=I["c_selfhot"][:, :]), writes=[cg_b])
        hp = aalloc([1, 3, 12], F32)
        hp_b = Buf("hp")
        P.dma("sp", lambda e: e.dma_start(out=hp[:, 0, :], in_=I["l1_gdn_dt_bias"].rearrange("(o n) -> o n", o=1)),
              writes=[hp_b])
        P.dma("sp", lambda e: e.dma_start(out=hp[:, 1, :], in_=I["l1_gdn_A_log"].rearrange("(o n) -> o n", o=1)),
              writes=[hp_b])
        P.op("act", lambda e: e.activation(out=hp[:, 1, :], in_=hp[:, 1, :], func=AF.Exp), reads=[hp_b], writes=[hp_b])
        P.op("dve", lambda e: e.tensor_scalar_mul(hp[:, 1, :], hp[:, 1, :], -1.0), reads=[hp_b], writes=[hp_b])
        bap = aalloc([128, 16, 24], BF16)
        bap_b = Buf("bap")
        P.dma("pool", lambda e: e.dma_start(out=bap, in_=W1[:, 6144:6168].rearrange("(kc p) n -> p kc n", p=128)),
              writes=[bap_b])

        U = aalloc([128, 3 + TP], F32)
        U_b = Buf("gU")
        USn = aalloc([128, NS], F32)
        USn_b = Buf("gUSn")
        hst3 = aalloc([NS, 3, 128], F32)
        hst3_b = Buf("hst3")
        hT3 = aalloc([128, 3, NS], F32)
        hT3_b = Buf("hT3")
        CV = aalloc([128, NT], F32)
        CV_b = Buf("gCV")
        qkv = aalloc([128, 3, NT], BF16)
        qkv_b = [Buf("gq"), Buf("gk"), Buf("gv")]
        smp = aalloc([128, 3, NS], F32)
        smp_b = Buf("smp")
        zgs = aalloc([128, NT], BF16)
        zgs_b = Buf("zgs")
        rows = aalloc([1, 4, NT], F32)
        rmisc = aalloc([1, 2, NS], F32)
        rows_b = Buf("rows")
        colsT = aalloc([128, 8, 4], F32)
        colsT_b = Buf("colsT")
        Gb = aalloc([128, 16], F32)
        Gb_b = Buf("Gb")
        sscal = aalloc([128, 2, NS], F32)
        sscal_b = Buf("sscal")
        gkT = aalloc([128, 128], BF16)
        qdT = aalloc([128, 128], BF16)
        kd = aalloc([128, 128], BF16)
        bvx = aalloc([128, 256], F32)
        Rsb = aalloc([128, 128], F32)
        decT = aalloc([128, 128], F32)
        decL = aalloc([128, 128], F32)
        intraT = aalloc([128, 128], BF16)
        Lm = aalloc([128, 2, 128], F32)
        Nm = aalloc([128, 2, 128], F32)
        XT = aalloc([128, 128], F32)
        TT = aalloc([128, 128], BF16)
        prep_b = Buf("prep")
        br = aalloc([128, 256], BF16)
        br_b = Buf("br")
        vn = aalloc([128, 256], BF16)
        vn_b = Buf("vn")
        S = aalloc([128, 256], F32)
        Sb = aalloc([128, 256], BF16)
        S_b, Sb_b = Buf("S"), Buf("Sb")
        o0T = aalloc([128, NT], F32)
        o0T_b = Buf("o0T")
        OphT = aalloc([128, TP], BF16)
        OphT_b = Buf("OphT")
        ABt = aalloc([128, 256], F32)
        ABt_b = Buf("ABt")
        ABr = aalloc([128, 2, 256], F32)
        ABr_b = [Buf("ABr0"), Buf("ABr1")]
        Srun = aalloc([128, 128], F32)
        Sin = aalloc([128, 128], F32)
        Sinb = aalloc([128, 128], BF16)
        scan_b = Buf("scan")
        S0 = aalloc([128, 2, 128], F32)
        S0_b = [Buf("S0a"), Buf("S0b")]
        S1 = aalloc([128, 2, 128], F32)
        S1_b = [Buf("S1a"), Buf("S1b")]
        dg = aalloc([128, 128], F32)
        dl = aalloc([128, 2], F32)
        smisc_b = Buf("smisc")
        P.op("dve", lambda e: e.memset(bvx, 0.0), writes=[prep_b])
        sqv = sq[:, :, :].rearrange("p a b -> p (a b)")[:, 0:NT]
        sqv_b = sq_b[0]

        def row_bcast(dst_cols, src_row, ncols, pt):
            return lambda e: e.matmul(pt[:, dst_cols:dst_cols + ncols], onesf[0:1, 0:128], src_row, start=True, stop=True)

        for h in range(12):
            for wi_, ci in enumerate((h, 12 + h, 24 + h)):
                colbase = ci * 128
                pv, pvb = load_panel(W1, 16, colbase, 128)
                P.dma("sp", lambda e, colbase=colbase: e.dma_start(out=hst3, in_=I["st_gconv"][:, :, colbase:colbase + 128]),
                      writes=[hst3_b])
                pt, ptb = next_pst()
                P.group("pe", [lambda e, pt=pt, r=r: e.transpose(pt[:, r * NS:(r + 1) * NS], hst3[0:NS, r, :],
                                                                 ident[0:NS, 0:NS]) for r in range(3)],
                        reads=[hst3_b, ident_b], writes=[ptb])
                P.op("act", lambda e, pt=pt: e.copy(hT3.rearrange("p r s -> p (r s)"), pt[:, 0:3 * NS]),
                     reads=[ptb], writes=[hT3_b])

                def evac_up(a, ab, ti, t0, tn):
                    eng = evac_eng()
                    if ti < 2:
                        P.op(eng, copy_fn(eng, U[:, 3 + t0:3 + t0 + tn], a[:, 0:tn]), reads=[ab], writes=[U_b])
                    elif ti == 2:
                        P.op(eng, copy_fn(eng, USn[:, :], a[:, 0:NS]), reads=[ab], writes=[USn_b])
                    else:
                        P.op(eng, copy_fn(eng, U[:, 0:3], a[:, 0:3]), reads=[ab], writes=[U_b])
                gemm_fm(pv, pvb, 16, [(0, 128, 0)], rhs3, xn_b + [xnh_b], evac_up, tiles=FT)
                wt = [gw[:, t * 36 + ci:t * 36 + ci + 1] for t in range(4)]
                P.op("dve", lambda e, w=wt[0]: e.tensor_scalar_mul(CV[:, 0:TP], U[:, 0:TP], w),
                     reads=[U_b, gw_b], writes=[CV_b])
                for t in range(1, 4):
                    P.op("dve", lambda e, w=wt[t], t=t: e.scalar_tensor_tensor(
                        CV[:, 0:TP], U[:, t:t + TP], w, CV[:, 0:TP], ALU.mult, ALU.add),
                        reads=[U_b, gw_b], writes=[CV_b])
                P.op("dve", lambda e, w=wt[0]: e.tensor_scalar_mul(CV[:, TP:NT], hT3[:, 0, :], w),
                     reads=[hT3_b, gw_b], writes=[CV_b])
                for t in range(1, 3):
                    P.op("dve", lambda e, w=wt[t], t=t: e.scalar_tensor_tensor(
                        CV[:, TP:NT], hT3[:, t, :], w, CV[:, TP:NT], ALU.mult, ALU.add),
                        reads=[hT3_b, gw_b], writes=[CV_b])
                P.op("dve", lambda e, w=wt[3]: e.scalar_tensor_tensor(
                    CV[:, TP:NT], USn[:, :], w, CV[:, TP:NT], ALU.mult, ALU.add),
                    reads=[USn_b, gw_b], writes=[CV_b])
                P.op("act", lambda e: e.activation(out=CV[:, :], in_=CV[:, :], func=AF.Silu), reads=[CV_b], writes=[CV_b])
                if wi_ < 2:
                    P.op("act", lambda e: e.activation(out=sqv, in_=CV[:, :], func=AF.Square), reads=[CV_b], writes=[sqv_b])
                    accs = [next_acc() for _ in TILES]
                    for ti, (t0, tn) in enumerate(TILES):
                        a_, ab = accs[ti]
                        P.group("pe", [lambda e, a_=a_, t0=t0, tn=tn: e.matmul(a_[:, 0:tn], ones_bf[:], sqv[:, t0:t0 + tn],
                                                                              start=True, stop=True)],
                                reads=[sqv_b, const_b], writes=[ab])
                        P.op("act", lambda e, a_=a_, t0=t0, tn=tn: e.activation(
                            out=rstd[:, t0:t0 + tn], in_=a_[:, 0:tn], func=AF.Sqrt, bias=epsc[:, 0:1], scale=1.0),
                            reads=[ab, const_b], writes=[rstd_b])
                    P.op("dve", lambda e: e.reciprocal(rstd[:], rstd[:]), reads=[rstd_b], writes=[rstd_b])
                    sc_ = (128.0 ** -0.5) if wi_ == 0 else 1.0
                    P.op("dve", lambda e, sc_=sc_: e.scalar_tensor_tensor(
                        CV[:, :], CV[:, :], sc_, rstd[:], ALU.mult, ALU.mult), reads=[CV_b, rstd_b], writes=[CV_b])
                P.op("act", lambda e, wi_=wi_: e.copy(qkv[:, wi_, :], CV[:, :]), reads=[CV_b], writes=[qkv_b[wi_]])
                P.op("dve", lambda e, wi_=wi_: e.tensor_copy(smp[:, wi_, :], CV[:, TP:NT]), reads=[CV_b], writes=[smp_b])
            pv, pvb = load_panel(W1, 16, 4608 + h * 128, 128)

            def evac_zg(a, ab, ti, t0, tn):
                P.op("act", lambda e, a=a, t0=t0, tn=tn: e.activation(out=zgs[:, t0:t0 + tn], in_=a[:, 0:tn], func=AF.Silu),
                     reads=[ab], writes=[zgs_b])
            gemm_fm(pv, pvb, 16, [(0, 128, 0)], xn_rhs, xn_b, evac_zg)

            for qi, col in ((0, h), (1, 12 + h)):
                for ti, (t0, tn) in enumerate(TILES):
                    a_, ab = next_acc()
                    P.group("pe", [lambda e, a_=a_, kc=kc, col=col, t0=t0, tn=tn: e.matmul(
                        a_[0:1, 0:tn], bap[:, kc, col:col + 1], xn[:, kc, t0:t0 + tn], start=(kc == 0), stop=(kc == 15))
                        for kc in range(16)], reads=[bap_b] + xn_b, writes=[ab])
                    P.op("dve", lambda e, a_=a_, qi=qi, t0=t0, tn=tn: e.tensor_copy(rows[:, qi, t0:t0 + tn], a_[0:1, 0:tn]),
                         reads=[ab], writes=[rows_b])
            P.op("act", lambda e: e.activation(out=rows[:, 0, :], in_=rows[:, 0, :], func=AF.Sigmoid),
                 reads=[rows_b], writes=[rows_b])
            P.op("act", lambda e, h=h: e.activation(out=rows[:, 1, :], in_=rows[:, 1, :], func=AF.Exp, bias=hp[:, 0, h:h + 1]),
                 reads=[rows_b, hp_b], writes=[rows_b])
            P.op("act", lambda e: e.activation(out=rows[:, 1, :], in_=rows[:, 1, :], func=AF.Ln, bias=onesf[0:1, 0:1]),
                 reads=[rows_b, cg_b], writes=[rows_b])
            P.op("dve", lambda e, h=h: e.tensor_scalar_mul(rows[:, 1, :], rows[:, 1, :], hp[:, 1, h:h + 1]),
                 reads=[rows_b, hp_b], writes=[rows_b])
            P.op("act", lambda e: e.activation(out=rmisc[:, 0, :], in_=rows[:, 1, TP:NT], func=AF.Exp),
                 reads=[rows_b], writes=[rows_b])
            src_i = 1
            for st_i, sh in enumerate((1, 2, 4, 8, 16, 32)):
                dst_i = 2 if src_i == 1 else 1
                sv = rows[:, src_i, 0:TP].rearrange("o (c t) -> o c t", t=64)
                dv_ = rows[:, dst_i, 0:TP].rearrange("o (c t) -> o c t", t=64)
                P.op("dve", lambda e, sv=sv, dv_=dv_, sh=sh: e.tensor_copy(dv_[:, :, 0:sh], sv[:, :, 0:sh]),
                     reads=[rows_b], writes=[rows_b])
                P.op("dve", lambda e, sv=sv, dv_=dv_, sh=sh: e.tensor_tensor(dv_[:, :, sh:64], sv[:, :, sh:64],
                                                                           sv[:, :, 0:64 - sh], ALU.add),
                     reads=[rows_b], writes=[rows_b])
                src_i = dst_i
            P.op("dve", lambda e: e.tensor_copy(rows[:, 2, 0:TP], rows[:, 1, 0:TP]), reads=[rows_b], writes=[rows_b])
            gcv = rows[:, 2, 0:TP].rearrange("o (c t) -> o c t", t=64)
            P.op("dve", lambda e: e.tensor_tensor(rows[:, 3, 0:TP].rearrange("o (c t) -> o c t", t=64), gcv,
                                                  gcv[:, :, 63:64].broadcast_to([1, 16, 64]), ALU.subtract),
                 reads=[rows_b], writes=[rows_b])
            P.op("act", lambda e: e.activation(out=rmisc[:, 1, :], in_=gcv[:, :, 63], func=AF.Exp),
                 reads=[rows_b], writes=[rows_b])
            pt, ptb = next_pst()
            P.group("pe", [row_bcast(0, rmisc[:, 1, :], 16, pt), row_bcast(16, rmisc[:, 0, :], NS, pt),
                           row_bcast(32, rows[:, 0, TP:NT], NS, pt)], reads=[rows_b, cg_b], writes=[ptb])
            P.op("dve", lambda e, pt=pt: e.tensor_copy(Gb[:, :], pt[:, 0:16]), reads=[ptb], writes=[Gb_b])
            P.op("dve", lambda e, pt=pt: e.tensor_copy(sscal.rearrange("p a b -> p (a b)"), pt[:, 16:48]),
                 reads=[ptb], writes=[sscal_b])
            pt, ptb = next_pst()
            fns = []
            for blk in range(8):
                for qi, ri in enumerate((0, 3, 2)):
                    fns.append(lambda e, pt=pt, blk=blk, qi=qi, ri=ri: e.matmul(
                        pt[:, blk * 4 + qi:blk * 4 + qi + 1], rows[:, ri, blk * 128:(blk + 1) * 128], onesf[0:1, 0:1],
                        start=True, stop=True))
            P.group("pe", fns, reads=[rows_b, cg_b], writes=[ptb])
            P.op("dve", lambda e, pt=pt: e.tensor_copy(colsT[:, :, 0:3], pt[:, 0:32].rearrange("p (b q) -> p b q", q=4)[:, :, 0:3]),
                 reads=[ptb], writes=[colsT_b])
            P.op("dve", lambda e: e.tensor_scalar_mul(colsT[:, :, 3], colsT[:, :, 0], -1.0), reads=[colsT_b], writes=[colsT_b])
            P.op("act", lambda e: e.activation(out=colsT[:, :, 1], in_=colsT[:, :, 1], func=AF.Exp, scale=-1.0),
                 reads=[colsT_b], writes=[colsT_b])

            P.op("dve", lambda e: e.memset(S[:, 0:128], 0.0), writes=[S_b])
            P.op("dve", lambda e: e.tensor_copy(S[:, 128:256], ident[:, :]), reads=[ident_b], writes=[S_b])
            P.op("act", lambda e: e.copy(Sb[:, :], S[:, :]), reads=[S_b], writes=[Sb_b])

            for blk in range(8):
                tk = slice(blk * 128, (blk + 1) * 128)
                bcol, kcol, gcol, nbcol = (colsT[:, blk, i:i + 1] for i in range(4))
                pt, ptb = next_pst()
                P.group("pe", [row_bcast(128, rows[:, 2, tk], 128, pt)], reads=[rows_b, cg_b], writes=[ptb])
                P.op("act", lambda e, pt=pt: e.copy(Rsb[:, :], pt[:, 128:256]), reads=[ptb], writes=[prep_b])
                P.op("act", lambda e: e.activation(out=decT[:, :], in_=Rsb[:, :], func=AF.Exp), reads=[prep_b], writes=[prep_b])
                P.op("dve", lambda e, tk=tk: e.tensor_tensor(gkT[:, :], qkv[:, 1, tk], decT[:, :], ALU.mult),
                     reads=[prep_b, qkv_b[1]], writes=[prep_b])
                P.op("dve", lambda e, tk=tk: e.tensor_tensor(qdT[:, :], qkv[:, 0, tk], decT[:, :], ALU.mult),
                     reads=[prep_b, qkv_b[0]], writes=[prep_b])
                ptk, ptkb = next_pst()
                ptkv = ptk[:, :].bitcast(BF16)
                P.group("pe", [lambda e, ptkv=ptkv, tk=tk: e.transpose(ptkv[:, 0:128], qkv[:, 1, tk], identb[:, :]),
                               lambda e, ptkv=ptkv, tk=tk: e.transpose(ptkv[:, 128:256], qkv[:, 2, tk], identb[:, :])],
                        reads=[qkv_b[1], qkv_b[2], const_b], writes=[ptkb])
                P.op("dve", lambda e, ptkv=ptkv, kcol=kcol: e.tensor_scalar_mul(kd[:, :], ptkv[:, 0:128], kcol),
                     reads=[ptkb, colsT_b], writes=[prep_b])
                P.op("dve", lambda e, ptkv=ptkv, bcol=bcol: e.tensor_scalar_mul(bvx[:, 0:128], ptkv[:, 128:256], bcol),
                     reads=[ptkb, colsT_b], writes=[prep_b])
                P.op("dve", lambda e, gcol=gcol: e.tensor_scalar(decT[:, :], Rsb[:, :], gcol, 0.0, ALU.subtract, ALU.min),
                     reads=[prep_b, colsT_b], writes=[prep_b])
                P.op("act", lambda e: e.activation(out=decT[:, :], in_=decT[:, :], func=AF.Exp), reads=[prep_b], writes=[prep_b])
                P.op("dve", lambda e: e.tensor_tensor(decT[:, :], decT[:, :], gm[:, 0, :], ALU.mult),
                     reads=[prep_b, cg_b], writes=[prep_b])
                P.op("dve", lambda e, gcol=gcol: e.tensor_scalar(decL[:, :], Rsb[:, :], gcol, 0.0, ALU.subtract, ALU.max),
                     reads=[prep_b, colsT_b], writes=[prep_b])
                P.op("act", lambda e: e.activation(out=decL[:, :], in_=decL[:, :], func=AF.Exp, scale=-1.0),
                     reads=[prep_b], writes=[prep_b])
                P.op("dve", lambda e: e.tensor_tensor(decL[:, :], decL[:, :], gm[:, 1, :], ALU.mult),
                     reads=[prep_b, cg_b], writes=[prep_b])
                a_, ab = next_acc()
                P.group("pe", [lambda e, a_=a_, tk=tk: e.matmul(a_[:, 0:128], qkv[:, 1, tk], qkv[:, 0, tk], start=True, stop=True),
                               lambda e, a_=a_, tk=tk: e.matmul(a_[:, 128:256], qkv[:, 1, tk], qkv[:, 1, tk], start=True, stop=True)],
                        reads=[qkv_b[0], qkv_b[1]], writes=[ab])
                P.op("dve", lambda e, a_=a_: e.tensor_tensor(intraT[:, :], a_[:, 0:128], decT[:, :], ALU.mult),
                     reads=[ab, prep_b], writes=[prep_b])
                P.op("dve", lambda e, a_=a_: e.tensor_tensor(Lm[:, 0, :], a_[:, 128:256], decL[:, :], ALU.mult),
                     reads=[ab, prep_b], writes=[prep_b])
                P.op("dve", lambda e, bcol=bcol: e.tensor_scalar_mul(Lm[:, 0, :], Lm[:, 0, :], bcol),
                     reads=[prep_b, colsT_b], writes=[prep_b])
                pt, ptb = next_pst()
                P.group("pe", [lambda e, pt=pt: e.transpose(pt[:, 0:128], Lm[:, 0, :], ident[:, :])],
                        reads=[prep_b, ident_b], writes=[ptb])
                P.op("act", lambda e, pt=pt: e.copy(Nm[:, 0, :], pt[:, 0:128]), reads=[ptb], writes=[prep_b])
                P.op("dve", lambda e, pt=pt: e.tensor_tensor(XT[:, :], ident[:, :], pt[:, 0:128], ALU.subtract),
                     reads=[ptb, ident_b], writes=[prep_b])
                for lvl in range(5):
                    si, di = lvl % 2, (lvl + 1) % 2
                    a_, ab = next_acc()
                    P.group("pe", [lambda e, a_=a_, si=si: e.matmul(a_[:, 0:128], Nm[:, si, :], Lm[:, si, :], start=True, stop=True),
                                   lambda e, a_=a_, si=si: e.matmul(a_[:, 128:256], Lm[:, si, :], Nm[:, si, :], start=True, stop=True)],
                            reads=[prep_b], writes=[ab])
                    P.op("act", lambda e, a_=a_, di=di: e.copy(Lm[:, di, :], a_[:, 0:128]), reads=[ab], writes=[prep_b])
                    P.op("act", lambda e, a_=a_, di=di: e.copy(Nm[:, di, :], a_[:, 128:256]), reads=[ab], writes=[prep_b])
                    a2, a2b = next_acc()
                    P.group("pe", [lambda e, a2=a2, di=di: e.matmul(a2[:, 0:128], Lm[:, di, :], XT[:, :], start=True, stop=True)],
                            reads=[prep_b], writes=[a2b])
                    P.op("dve", lambda e, a2=a2: e.tensor_tensor(XT[:, :], XT[:, :], a2[:, 0:128], ALU.add),
                         reads=[a2b, prep_b], writes=[prep_b])
                P.op("act", lambda e: e.copy(TT[:, :], XT[:, :]), reads=[prep_b], writes=[prep_b])

                for cc in range(2):
                    p0 = 64 * cc
                    ch = blk * 2 + cc
                    ct = slice(blk * 128 + p0, blk * 128 + p0 + 64)
                    a1, a1b = next_acc()
                    P.group("pe", [lambda e, a1=a1, p0=p0: e.matmul(a1[p0:p0 + 64, 0:256], gkT[:, p0:p0 + 64], Sb[:, :],
                                                                    start=True, stop=True)],
                            reads=[prep_b, Sb_b], writes=[a1b])
                    P.op("dve", lambda e, a1=a1, p0=p0, blk=blk: e.scalar_tensor_tensor(
                        br[p0:p0 + 64, :], a1[p0:p0 + 64, 0:256], colsT[p0:p0 + 64, blk, 3:4], bvx[p0:p0 + 64, :],
                        ALU.mult, ALU.add), reads=[a1b, colsT_b, prep_b], writes=[br_b])
                    a2, a2b = next_acc()
                    P.group("pe", [lambda e, a2=a2, p0=p0: e.matmul(a2[p0:p0 + 64, 0:256], TT[p0:p0 + 64, p0:p0 + 64],
                                                                    br[p0:p0 + 64, :], start=True, stop=True)],
                            reads=[prep_b, br_b], writes=[a2b])
                    P.op("act", lambda e, a2=a2, p0=p0: e.copy(vn[p0:p0 + 64, :], a2[p0:p0 + 64, 0:256]),
                         reads=[a2b], writes=[vn_b])
                    a3, a3b = next_acc()
                    fns = []
                    for half in range(2):
                        cs = slice(half * 128, (half + 1) * 128)
                        fns.append(lambda e, a3=a3, cs=cs, half=half, p0=p0: e.matmul(
                            a3[:, half * 64:half * 64 + 64], Sb[:, cs], qdT[:, p0:p0 + 64], start=True, stop=False))
                        fns.append(lambda e, a3=a3, cs=cs, half=half, p0=p0: e.matmul(
                            a3[:, half * 64:half * 64 + 64], vn[p0:p0 + 64, cs], intraT[p0:p0 + 64, p0:p0 + 64],
                            start=False, stop=True))
                    P.group("pe", fns, reads=[Sb_b, prep_b, vn_b], writes=[a3b])
                    P.op("dve", lambda e, a3=a3, ct=ct: e.tensor_copy(o0T[:, ct], a3[:, 0:64]), reads=[a3b], writes=[o0T_b])
                    P.op("dve", lambda e, a3=a3, ct=ct: e.tensor_copy(OphT[:, ct], a3[:, 64:128]), reads=[a3b], writes=[OphT_b])
                    a4, a4b = next_acc()
                    P.group("pe", [lambda e, a4=a4, p0=p0: e.matmul(a4[:, 0:256], kd[p0:p0 + 64, :], vn[p0:p0 + 64, :],
                                                                    start=True, stop=True)],
                            reads=[prep_b, vn_b], writes=[a4b])
                    P.op("dve", lambda e, a4=a4, ch=ch: e.scalar_tensor_tensor(
                        S[:, :], S[:, :], Gb[:, ch:ch + 1], a4[:, 0:256], ALU.mult, ALU.add),
                        reads=[a4b, Gb_b, S_b], writes=[S_b])
                    P.op("act", lambda e: e.copy(Sb[:, :], S[:, :]), reads=[S_b], writes=[Sb_b])

            pt, ptb = next_pst()
            P.group("pe", [lambda e, pt=pt: e.transpose(pt[:, 0:128], S[:, 128:256], ident[:, :])],
                    reads=[S_b, ident_b], writes=[ptb])
            P.op("act", lambda e, pt=pt: e.copy(ABt[:, 0:128], pt[:, 0:128]), reads=[ptb], writes=[ABt_b])
            P.op("dve", lambda e: e.tensor_copy(ABt[:, 128:256], S[:, 0:128]), reads=[S_b], writes=[ABt_b])
            bi = nc.dram_tensor("gdn_in%d" % h, [128, 256], F32)
            bo = nc.dram_tensor("gdn_out%d" % h, [NCORES * 128, 256], F32)
            bib, bob = Buf("gbi"), Buf("gbo")
            P.dma("sp", lambda e, bi=bi: e.dma_start(out=bi.ap(), in_=ABt), reads=[ABt_b], writes=[bib])
            P.collective(lambda e, bi=bi, bo=bo: e.collective_compute(
                "AllGather", ALU.bypass, replica_groups=[list(range(NCORES))],
                ins=[bi.ap().opt()], outs=[bo.ap().opt()]), reads=[bib], writes=[bob])
            for b_ in range(NS):
                s_ = b_ % 2
                egc = sscal[:, 0, b_:b_ + 1]
                btc = sscal[:, 1, b_:b_ + 1]
                P.dma("sp", lambda e, b_=b_, s_=s_, h=h: e.dma_start(out=S0[:, s_, :], in_=I["st_S"][b_, h, :, :]),
                      writes=[S0_b[s_]])
                a_, ab = next_acc()
                P.group("pe", [lambda e, a_=a_, s_=s_, b_=b_: e.matmul(a_[:, 0:1], S0[:, s_, :], smp[:, 1, b_:b_ + 1],
                                                                      start=True, stop=True)],
                        reads=[S0_b[s_], smp_b], writes=[ab])
                P.op("dve", lambda e, a_=a_, egc=egc: e.tensor_scalar_mul(dl[:, 0:1], a_[:, 0:1], egc),
                     reads=[ab, sscal_b], writes=[smisc_b])
                P.op("dve", lambda e, b_=b_, btc=btc: e.scalar_tensor_tensor(
                    dl[:, 1:2], smp[:, 2, b_:b_ + 1], 1.0, dl[:, 0:1], ALU.mult, ALU.subtract),
                    reads=[smp_b, smisc_b], writes=[smisc_b])
                P.op("dve", lambda e, btc=btc: e.tensor_scalar_mul(dl[:, 1:2], dl[:, 1:2], btc),
                     reads=[smisc_b, sscal_b], writes=[smisc_b])
                P.op("dve", lambda e: e.tensor_scalar_mul(dg[:, :], ident[:, :], dl[:, 1:2]),
                     reads=[smisc_b, ident_b], writes=[smisc_b])
                a2, a2b = next_acc()
                P.group("pe", [lambda e, a2=a2: e.matmul(a2[:, 0:128], onesf[:, :], dg[:, :], start=True, stop=True)],
                        reads=[smisc_b, cg_b], writes=[a2b])
                P.op("act", lambda e, s_=s_, egc=egc: e.mul(S1[:, s_, :], S0[:, s_, :], egc),
                     reads=[S0_b[s_], sscal_b], writes=[S1_b[s_]])
                P.op("dve", lambda e, a2=a2, s_=s_, b_=b_: e.scalar_tensor_tensor(
                    S1[:, s_, :], a2[:, 0:128], smp[:, 1, b_:b_ + 1], S1[:, s_, :], ALU.mult, ALU.add),
                    reads=[a2b, smp_b, S1_b[s_]], writes=[S1_b[s_]])
                P.dma("sp", lambda e, b_=b_, s_=s_, h=h: e.dma_start(out=O["S_s"][b_, h, :, :], in_=S1[:, s_, :]),
                      reads=[S1_b[s_]], writes=[OB["S_s"]])
                a3, a3b = next_acc()
                P.group("pe", [lambda e, a3=a3, s_=s_, b_=b_: e.matmul(a3[:, 0:1], S1[:, s_, :], smp[:, 0, b_:b_ + 1],
                                                                      start=True, stop=True)],
                        reads=[S1_b[s_], smp_b], writes=[a3b])
                P.op("dve", lambda e, a3=a3, b_=b_: e.tensor_copy(o0T[:, TP + b_:TP + b_ + 1], a3[:, 0:1]),
                     reads=[a3b], writes=[o0T_b])

            P.op("dve", lambda e: e.memset(Srun, 0.0), writes=[scan_b])
            P.op("dve", lambda e: e.memset(Sin, 0.0), writes=[scan_b])
            for r in range(NCORES):
                s_ = r % 2
                P.dma("sp", lambda e, r=r, s_=s_, bo=bo: e.dma_start(out=ABr[:, s_, :], in_=bo.ap()[r * 128:(r + 1) * 128, :]),
                      reads=[bob], writes=[ABr_b[s_]])
                P.op("dve", lambda e, r=r: e.scalar_tensor_tensor(Sin, Srun, selfhot[:, r:r + 1], Sin, ALU.mult, ALU.add),
                     reads=[scan_b, cg_b], writes=[scan_b])
                a_, ab = next_acc()
                P.group("pe", [lambda e, a_=a_, s_=s_: e.matmul(a_[:, 0:128], ABr[:, s_, 0:128], Srun, start=True, stop=True)],
                        reads=[ABr_b[s_], scan_b], writes=[ab])
                P.op("dve", lambda e, a_=a_, s_=s_: e.tensor_tensor(Srun, a_[:, 0:128], ABr[:, s_, 128:256], ALU.add),
                     reads=[ab, ABr_b[s_]], writes=[scan_b])
            P.dma("sp", lambda e, h=h: e.dma_start(out=O["S_p"][h, :, :], in_=Srun), reads=[scan_b], writes=[OB["S_p"]])
            P.op("act", lambda e: e.copy(Sinb, Sin), reads=[scan_b], writes=[scan_b])
            for (t0, tn) in TILES[0:2]:
                a_, ab = next_acc()
                P.group("pe", [lambda e, a_=a_, t0=t0, tn=tn: e.matmul(a_[:, 0:tn], Sinb, OphT[:, t0:t0 + tn], start=True, stop=True)],
                        reads=[scan_b, OphT_b], writes=[ab])
                P.op("dve", lambda e, a_=a_, t0=t0, tn=tn: e.tensor_tensor(o0T[:, t0:t0 + tn], o0T[:, t0:t0 + tn], a_[:, 0:tn], ALU.add),
                     reads=[ab, o0T_b], writes=[o0T_b])

            P.op("act", lambda e: e.activation(out=sqv, in_=o0T[:, :], func=AF.Square), reads=[o0T_b], writes=[sqv_b])
            for ti, (t0, tn) in enumerate(TILES):
                a_, ab = next_acc()
                P.group("pe", [lambda e, a_=a_, t0=t0, tn=tn: e.matmul(a_[:, 0:tn], ones_bf[:], sqv[:, t0:t0 + tn],
                                                                      start=True, stop=True)],
                        reads=[sqv_b, const_b], writes=[ab])
                P.op("act", lambda e, a_=a_, t0=t0, tn=tn: e.activation(
                    out=rstd[:, t0:t0 + tn], in_=a_[:, 0:tn], func=AF.Sqrt, bias=epsc[:, 0:1], scale=1.0 / 128),
                    reads=[ab, const_b], writes=[rstd_b])
            P.op("dve", lambda e: e.reciprocal(rstd[:], rstd[:]), reads=[rstd_b], writes=[rstd_b])
            P.op("dve", lambda e: e.scalar_tensor_tensor(o0T[:, :], o0T[:, :], gnw[:, 0:1], rstd[:], ALU.mult, ALU.mult),
                 reads=[o0T_b, gnw_b, rstd_b], writes=[o0T_b])
            P.op("dve", lambda e: e.tensor_tensor(yh[:, :], o0T[:, :], zgs[:, :], ALU.mult),
                 reads=[o0T_b, zgs_b], writes=[yh_b])
            proj_residual_rows(I["l1_w_out"], h * 128, 1, lambda kc, t0, tn: yh[:, t0:t0 + tn], [yh_b])

        proj_residual_rows(I["l1_w_out"], 1536, 4, lambda kc, t0, tn: ysc[:, kc, t0:t0 + tn], ysc_b)

    layer1_gdn()
    if KSTOP == "x2":
        maybe_stop("x2", xT[:, :, :], xT_b, [128, 16, NT], F32)

    ffn(1)

    def final_out():
        P.barrier()
        ar["off"] = 0
        wcol, wcol_b = rmsnorm("final_norm")
        ytmp = aalloc([128, 2, 4, 128], F32)
        ytmp_b = [Buf("ytmp0"), Buf("ytmp1")]
        yst = aalloc([128, 2, D], F32)
        yst_b = [Buf("yst0"), Buf("yst1")]
        k_ = 0
        for blk in range(9):
            t0 = blk * 128
            tn = 128 if blk < 8 else NS
            so = blk % 2
            for c4 in range(4):
                s_ = k_ % 2
                k_ += 1
                for j in range(4):
                    c = c4 * 4 + j
                    P.op("dve", lambda e, c=c, j=j, s_=s_, t0=t0, tn=tn: e.scalar_tensor_tensor(
                        ytmp[:, s_, j, 0:tn], xT[:, c, t0:t0 + tn], wcol[:, c:c + 1], rstd[:, t0:t0 + tn],
                        ALU.mult, ALU.mult), reads=[xT_b[c], wcol_b, rstd_b], writes=[ytmp_b[s_]])
                pt, ptb = next_pst()
                P.group("pe", [lambda e, pt=pt, j=j, s_=s_, tn=tn: e.transpose(
                    pt[0:tn, j * 128:(j + 1) * 128], ytmp[:, s_, j, 0:tn], ident[:, :]) for j in range(4)],
                    reads=[ytmp_b[s_], ident_b], writes=[ptb])
                P.op("act", lambda e, pt=pt, so=so, c4=c4, tn=tn: e.copy(yst[0:tn, so, c4 * 512:(c4 + 1) * 512], pt[0:tn, 0:512]),
                     reads=[ptb], writes=[yst_b[so]])
            if blk < 8:
                P.dma("sp", lambda e, so=so, t0=t0: e.dma_start(out=O["y_p"][t0:t0 + 128, :], in_=yst[:, so, :]),
                      reads=[yst_b[so]], writes=[OB["y_p"]])
            else:
                P.dma("sp", lambda e, so=so: e.dma_start(out=O["y_s"][:, :], in_=yst[0:NS, so, :]),
                      reads=[yst_b[so]], writes=[OB["y_s"]])

    final_out()
    for nm in OUT_ORDER:
        O[nm]

    P.replay()
    es.close()
    return nc


def _core_inputs(inputs, c):
    m = {}
    m["xp"] = np.ascontiguousarray(inputs["x_prompt"][0, c * TP:(c + 1) * TP, :])
    sl = slice(c * NS, (c + 1) * NS)
    m["xs"] = np.ascontiguousarray(inputs["x_sample"][sl, 0, :])
    m["st_pool"] = np.ascontiguousarray(inputs["state_l0_pool"][sl])
    m["ck"] = np.ascontiguousarray(inputs["cache_l0_k"][sl]).reshape(NS, 128, 256)
    m["cv"] = np.ascontiguousarray(inputs["cache_l0_v"][sl]).reshape(NS, 128, 256)
    m["st_ffn0"] = np.ascontiguousarray(inputs["state_l0_ffn_conv"][sl])
    m["st_gconv"] = np.ascontiguousarray(inputs["state_l1_gdn_conv"][sl])
    m["st_S"] = np.ascontiguousarray(inputs["state_l1_gdn_S"][sl])
    m["st_sconv"] = np.ascontiguousarray(inputs["state_l1_sconv"][sl])
    m["st_ffn1"] = np.ascontiguousarray(inputs["state_l1_ffn_conv"][sl])
    for nm in ["l0_norm_mix", "l0_w_in", "l0_pool_w", "l0_pool_scale", "l0_sinks", "l0_w_out", "l0_norm_ffn",
               "l0_ffn_w_up", "l0_ffn_conv", "l0_ffn_w_down", "l1_norm_mix", "l1_w_in", "l1_gdn_conv",
               "l1_gdn_A_log", "l1_gdn_dt_bias", "l1_gdn_norm", "l1_sconv_w", "l1_w_out", "l1_norm_ffn",
               "l1_ffn_w_up", "l1_ffn_conv", "l1_ffn_w_down", "final_norm"]:
        m[nm] = np.ascontiguousarray(inputs[nm], dtype=np.float32)
    m["c_ident"] = np.eye(128, dtype=np.float32)
    oh = np.zeros((128, 8), np.float32)
    if c > 0:
        oh[:, c - 1] = 1.0
    m["c_onehot"] = oh
    kk = np.arange(128)[:, None]
    qq = np.arange(128)[None, :]
    m["c_masks"] = np.concatenate([(qq >= kk), (kk >= qq)], axis=1).astype(np.float32)
    m["c_flag"] = np.full((128, 1), 0.0 if c == 0 else 1.0, np.float32)
    corr = np.ones((128, 4, 16), np.float32)
    if c == 0:
        for g, w in enumerate((2, 4, 8, 16)):
            pos = np.arange(16)
            corr[:, g, :] = (w / np.minimum(pos + 1, w))[None, :]
    m["c_poolcorr"] = corr
    bd = np.zeros((128, 128), np.float32)
    bd[:64, :64] = 1.0
    bd[64:, 64:] = 1.0
    m["c_bdones"] = bd
    pp = np.arange(128)[:, None]
    ff = np.arange(128)[None, :]
    same = (pp // 64) == (ff // 64)
    m["c_gmasks"] = np.concatenate([(same & (pp <= ff)), (same & (pp > ff))], axis=1).astype(np.float32)
    sh_ = np.zeros((128, 8), np.float32)
    sh_[:, c] = 1.0
    m["c_selfhot"] = sh_
    return m


def kernel(**inputs):
    inputs = {k_: np.asarray(v) for k_, v in inputs.items()}
    nc = build()
    in_maps = []
    for c in range(NCORES):
        m = _core_inputs(inputs, c)
        in_maps.append({k_: v for k_, v in m.items() if k_ in nc._I})
    res = run_bass_kernel_spmd(nc, in_maps, core_ids=list(range(NCORES)))
    r = res.results
    outs = []
    for nm in OUT_ORDER:
        if nm in ("y_p",):
            outs.append(np.concatenate([r[c][nm] for c in range(NCORES)], axis=0)[None])
        elif nm.endswith("_p"):
            full = np.asarray(r[NCORES - 1][nm])
            shp = {"pool_p": (1, 15, 512), "k_p": (1, 128, 4, 64), "v_p": (1, 128, 4, 64),
                   "ffn0_p": (1, 2, 2 * DFF), "gconv_p": (1, 3, 4608), "S_p": (1, 12, 128, 128),
                   "sconv_p": (1, 2, 512), "ffn1_p": (1, 2, 2 * DFF)}[nm]
            outs.append(full.reshape(shp))
        else:
            cat = np.concatenate([np.asarray(r[c][nm]) for c in range(NCORES)], axis=0)
            shp = {"y_s": (128, 1, D), "pool_s": (128, 15, 512), "k_s": (128, 128, 4, 64),
                   "v_s": (128, 128, 4, 64), "ffn0_s": (128, 2, 2 * DFF), "gconv_s": (128, 3, 4608),
                   "S_s": (128, 12, 128, 128), "sconv_s": (128, 2, 512), "ffn1_s": (128, 2, 2 * DFF)}[nm]
            outs.append(cat.reshape(shp))
    return tuple(np.ascontiguousarray(o, dtype=np.float32) for o in outs)
```
